# Optimizing a Trainium2 kernel written in Bass

```python
import math
import jax, jax.numpy as jnp
from jax import lax
import numpy as np

D_MODEL = 1024
BATCH = 4
SEQ = 8192
DEPTH = 2

CHUNK = 64
MEM_LEN = 256
D_FF = 2816
N_EVEN = (DEPTH + 1) // 2
N_ODD = DEPTH // 2

A_HEADS = 4
A_QK_DIM = 64
A_V_DIM = 2 * A_QK_DIM
A_QK_WIDTH = A_HEADS * 2 * A_QK_DIM
A_V_WIDTH = A_HEADS * A_V_DIM
QBLOCK = 128

B_GROUPS = 4
B_GROUP_DIM = 128
B_WIDTH = B_GROUPS * B_GROUP_DIM
B_CHUNK = 128

AB_IN = 2 * A_QK_WIDTH + A_V_WIDTH + 2 * B_WIDTH
AB_OUT = A_V_WIDTH + B_WIDTH

C_HEADS = 16
C_HEAD_DIM = 64
C_WIDTH = C_HEADS * C_HEAD_DIM
C_LEFT_CHUNKS = 8
C_MAX_REL = 256

X_HEADS = 4
X_HEAD_DIM = D_MODEL // X_HEADS

RMS_EPS = 1e-6

kernel_name = "hybrid_diffattn_gmlp_bandattn_macaron_encoder"


def rms_norm(x, g):
    xf = x.astype(jnp.float32)
    y = xf * lax.rsqrt(jnp.mean(xf * xf, axis=-1, keepdims=True) + RMS_EPS)
    return (y * g.astype(jnp.float32)).astype(x.dtype)


def swiglu(h, wg, wu, wd):
    return (jax.nn.silu(h @ wg) * (h @ wu)) @ wd


def diff_attention(q, k, v, lam):
    b, s, h, _, dk = q.shape
    nb = s // QBLOCK
    scale = dk ** -0.5
    qb = q.reshape(b, nb, QBLOCK, h, 2, dk).swapaxes(0, 1)
    k_chunk = jnp.arange(s) // CHUNK

    def block(args):
        qi, bi = args
        q_chunk = (bi * QBLOCK + jnp.arange(QBLOCK)) // CHUNK
        mask = k_chunk[None, :] <= q_chunk[:, None]
        sc = jnp.einsum('bqhmd,bkhmd->bhmqk', qi, k).astype(jnp.float32) * scale
        p = jax.nn.softmax(jnp.where(mask, sc, -jnp.inf), axis=-1)
        a = p[:, :, 0] - lam * p[:, :, 1]
        return jnp.einsum('bhqk,bkhd->bqhd', a.astype(v.dtype), v)

    o = lax.map(block, (qb, jnp.arange(nb)))
    return o.swapaxes(0, 1).reshape(b, s, h, v.shape[-1])


def spatial_gating(u, vg, g_norm, w_s, b_s):
    b, s, _ = u.shape
    vg = rms_norm(vg, g_norm).reshape(b, s // B_CHUNK, B_CHUNK, B_GROUPS, B_GROUP_DIM)
    pos_chunk = jnp.arange(B_CHUNK) // CHUNK
    mask = pos_chunk[None, :] <= pos_chunk[:, None]
    w = jnp.where(mask[None], w_s, jnp.zeros_like(w_s)).astype(vg.dtype)
    sg = jnp.einsum('gpq,bnqgc->bnpgc', w, vg) + b_s.T[:, :, None].astype(vg.dtype)
    return u * sg.reshape(b, s, B_WIDTH)


def band_attention(q, k, v, rel_bias):
    b, s, h, d = q.shape
    nc = s // CHUNK
    left = C_LEFT_CHUNKS * CHUNK
    width = left + CHUNK
    scale = d ** -0.5
    pad = ((0, 0), (left, 0), (0, 0), (0, 0))
    kp = jnp.pad(k, pad)
    vp = jnp.pad(v, pad)
    k_off = jnp.arange(width) - left
    dist = jnp.arange(CHUNK)[:, None] - k_off[None, :]
    idx = jnp.clip(dist, -C_MAX_REL, C_MAX_REL) + C_MAX_REL
    bias = rel_bias[:, idx].astype(jnp.float32)
    qc = q.reshape(b, nc, CHUNK, h, d).swapaxes(0, 1)

    def chunk(args):
        qi, c = args
        kb = lax.dynamic_slice_in_dim(kp, c * CHUNK, width, axis=1)
        vb = lax.dynamic_slice_in_dim(vp, c * CHUNK, width, axis=1)
        valid = (c * CHUNK + k_off) >= 0
        sc = jnp.einsum('bqhd,bkhd->bhqk', qi, kb).astype(jnp.float32) * scale + bias
        p = jax.nn.softmax(jnp.where(valid, sc, -jnp.inf), axis=-1)
        return jnp.einsum('bhqk,bkhd->bqhd', p.astype(vb.dtype), vb)

    o = lax.map(chunk, (qc, jnp.arange(nc)))
    return o.swapaxes(0, 1).reshape(b, s, h * d)


def cross_attention(h, m, wq, wkv, wo):
    b, s, _ = h.shape
    q = (h @ wq).reshape(b, s, X_HEADS, X_HEAD_DIM)
    k, v = jnp.split(m @ wkv, 2, axis=-1)
    k = k.reshape(b, m.shape[1], X_HEADS, X_HEAD_DIM)
    v = v.reshape(b, m.shape[1], X_HEADS, X_HEAD_DIM)
    sc = jnp.einsum('bshd,bmhd->bhsm', q, k).astype(jnp.float32) * (X_HEAD_DIM ** -0.5)
    p = jax.nn.softmax(sc, axis=-1)
    o = jnp.einsum('bhsm,bmhd->bshd', p.astype(v.dtype), v).reshape(b, s, D_MODEL)
    return o @ wo


def mixer_ab(h, w_in, lam_p, subln, sg_norm, sg_w, sg_b, w_out, lam_init):
    b, s, _ = h.shape
    z = h @ w_in
    c1 = A_QK_WIDTH
    c2 = 2 * A_QK_WIDTH
    c3 = c2 + A_V_WIDTH
    c4 = c3 + B_WIDTH
    qa, ka, va, u, vg = jnp.split(z, [c1, c2, c3, c4], axis=-1)
    qa = qa.reshape(b, s, A_HEADS, 2, A_QK_DIM)
    ka = ka.reshape(b, s, A_HEADS, 2, A_QK_DIM)
    va = va.reshape(b, s, A_HEADS, A_V_DIM)
    lp = lam_p.astype(jnp.float32)
    lam = jnp.exp(jnp.sum(lp[0] * lp[1])) - jnp.exp(jnp.sum(lp[2] * lp[3])) + lam_init
    oa = diff_attention(qa, ka, va, lam)
    oa = rms_norm(oa, subln) * (1.0 - lam_init)
    ob = spatial_gating(jax.nn.gelu(u), jax.nn.gelu(vg), sg_norm, sg_w, sg_b)
    return jnp.concatenate([oa.reshape(b, s, A_V_WIDTH), ob], axis=-1) @ w_out


def mixer_c(h, w_in, rel_bias, w_out):
    b, s, _ = h.shape
    q, k, v = jnp.split(h @ w_in, 3, axis=-1)
    shp = (b, s, C_HEADS, C_HEAD_DIM)
    o = band_attention(q.reshape(shp), k.reshape(shp), v.reshape(shp), rel_bias)
    return o @ w_out


def setup_inputs(seed: int = 0) -> dict:
    key = jax.random.key(seed)
    ks = iter(jax.random.split(key, 40))

    def nrm(shape, scale):
        return jax.random.normal(next(ks), shape, jnp.float32) * scale

    def gain(shape):
        return 1.0 + 0.02 * jax.random.normal(next(ks), shape, jnp.float32)

    L, LE, LO, D, F = DEPTH, N_EVEN, N_ODD, D_MODEL, D_FF
    return {
        "x": nrm((BATCH, SEQ, D), 1.0),
        "mem": nrm((BATCH, MEM_LEN, D), 1.0),
        "ffn1_norm": gain((L, D)),
        "ffn1_wg": nrm((L, D, F), D ** -0.5),
        "ffn1_wu": nrm((L, D, F), D ** -0.5),
        "ffn1_wd": nrm((L, F, D), F ** -0.5),
        "mix_norm": gain((L, D)),
        "ab_w_in": nrm((LE, D, AB_IN), D ** -0.5),
        "ab_lam": nrm((LE, 4, A_QK_DIM), 0.1),
        "ab_subln": gain((LE, A_V_DIM)),
        "ab_sg_norm": gain((LE, B_WIDTH)),
        "ab_sg_w": nrm((LE, B_GROUPS, B_CHUNK, B_CHUNK), B_CHUNK ** -0.5),
        "ab_sg_b": gain((LE, B_GROUPS, B_CHUNK)),
        "ab_w_out": nrm((LE, AB_OUT, D), AB_OUT ** -0.5),
        "c_w_in": nrm((LO, D, 3 * C_WIDTH), D ** -0.5),
        "c_rel_bias": nrm((LO, C_HEADS, 2 * C_MAX_REL + 1), 0.2),
        "c_w_out": nrm((LO, C_WIDTH, D), C_WIDTH ** -0.5),
        "xa_norm": gain((L, D)),
        "xa_mem_norm": gain((L, D)),
        "xa_wq": nrm((L, D, D), D ** -0.5),
        "xa_wkv": nrm((L, D, 2 * D), D ** -0.5),
        "xa_wo": nrm((L, D, D), D ** -0.5),
        "ffn2_norm": gain((L, D)),
        "ffn2_wg": nrm((L, D, F), D ** -0.5),
        "ffn2_wu": nrm((L, D, F), D ** -0.5),
        "ffn2_wd": nrm((L, F, D), F ** -0.5),
        "final_norm": gain((D,)),
    }


def reference(x, mem, ffn1_norm, ffn1_wg, ffn1_wu, ffn1_wd, mix_norm,
              ab_w_in, ab_lam, ab_subln, ab_sg_norm, ab_sg_w, ab_sg_b, ab_w_out,
              c_w_in, c_rel_bias, c_w_out,
              xa_norm, xa_mem_norm, xa_wq, xa_wkv, xa_wo,
              ffn2_norm, ffn2_wg, ffn2_wu, ffn2_wd, final_norm):
    for l in range(DEPTH):
        x = x + 0.5 * swiglu(rms_norm(x, ffn1_norm[l]), ffn1_wg[l], ffn1_wu[l], ffn1_wd[l])
        h = rms_norm(x, mix_norm[l])
        if l % 2 == 0:
            e = l // 2
            lam_init = 0.8 - 0.6 * math.exp(-0.3 * l)
            x = x + mixer_ab(h, ab_w_in[e], ab_lam[e], ab_subln[e], ab_sg_norm[e],
                             ab_sg_w[e], ab_sg_b[e], ab_w_out[e], lam_init)
        else:
            o = l // 2
            x = x + mixer_c(h, c_w_in[o], c_rel_bias[o], c_w_out[o])
        x = x + cross_attention(rms_norm(x, xa_norm[l]), rms_norm(mem, xa_mem_norm[l]),
                                xa_wq[l], xa_wkv[l], xa_wo[l])
        x = x + 0.5 * swiglu(rms_norm(x, ffn2_norm[l]), ffn2_wg[l], ffn2_wu[l], ffn2_wd[l])
    return rms_norm(x, final_norm)
```

```python
import numpy as np
from contextlib import ExitStack
import concourse.bass as bass
import concourse.mybir as mybir
from concourse.bass_utils import run_bass_kernel_spmd

F32 = mybir.dt.float32
BF16 = mybir.dt.bfloat16
AF = mybir.ActivationFunctionType
ALU = mybir.AluOpType
AX = mybir.AxisListType

D = 1024
DFF = 2816
NFF = DFF // 128
S_E = 8192
EPS = 1e-6
SEM_CAP = 50000
NEG = -80.0


class Buf:
    __slots__ = ("name", "w", "r")

    def __init__(self, name=""):
        self.name = name
        self.w = None
        self.r = []


class Chan:
    def __init__(self, sem):
        self.sem = sem
        self.issued = 0


class Op:
    __slots__ = ("eng", "fn", "deps", "needed", "count", "chan")

    def __init__(self, eng, fn, deps, chan=None):
        self.eng = eng
        self.fn = fn
        self.deps = deps
        self.needed = False
        self.count = 0
        self.chan = chan


ENGS = ("pe", "act", "dve", "pool", "sp")


class Prog:
    def __init__(self, nc, es):
        self.nc = nc
        self.es = es
        self.ops = {e: [] for e in ENGS}
        self.extra = {e: [] for e in ENGS}
        self.chans = []
        self.sems = {e: [] for e in ENGS}

    def chan(self, name):
        c = Chan(self.es.enter_context(self.nc.semaphore(name)))
        self.chans.append(c)
        return c

    def _deps(self, eng, reads, writes, strict=False):
        deps = []
        for b in reads:
            if b.w is not None:
                deps.append(b.w)
        for b in writes:
            if b.w is not None:
                deps.append(b.w)
            deps.extend(b.r)
        out = []
        for d in deps:
            if d[0] == "op":
                if d[1].eng == eng and not strict:
                    continue
                d[1].needed = True
                out.append(d)
            else:
                out.append(("dma", d[1], d[1].issued))
        out.extend(self.extra[eng])
        self.extra[eng] = []
        return out

    def op(self, eng, fn, reads=(), writes=(), strict=False):
        o = Op(eng, fn, self._deps(eng, reads, writes, strict))
        self.ops[eng].append(o)
        ref = ("op", o)
        for b in reads:
            b.r.append(ref)
        for b in writes:
            b.w = ref
            b.r = []
        return o

    def dma(self, eng, chan, out, in_, reads=(), writes=()):
        o = Op(eng, lambda e: e.dma_start(out=out, in_=in_), self._deps(eng, reads, writes), chan=chan)
        chan.issued += 16
        self.ops[eng].append(o)
        ref = ("dma", chan)
        for b in reads:
            b.r.append(ref)
        for b in writes:
            b.w = ref
            b.r = []
        return o

    def barrier(self):
        for e in ENGS:
            for e2 in ENGS:
                if e2 != e and self.ops[e2]:
                    last = self.ops[e2][-1]
                    if last.chan is None:
                        last.needed = True
                        self.extra[e].append(("op", last))
                    else:
                        for o in reversed(self.ops[e2]):
                            if o.chan is None:
                                o.needed = True
                                self.extra[e].append(("op", o))
                                break
            for c in self.chans:
                if c.issued:
                    self.extra[e].append(("dma", c, c.issued))

    def final_wait(self, eng="sp"):
        deps = [("dma", c, c.issued) for c in self.chans if c.issued]
        o = Op(eng, None, deps)
        self.ops[eng].append(o)

    def emit(self):
        nc = self.nc
        for e in ENGS:
            c = 0
            for o in self.ops[e]:
                if o.chan is None and o.needed:
                    c += 1
                    o.count = c
            nsem = (c + SEM_CAP - 1) // SEM_CAP
            for i in range(max(nsem, 1)):
                self.sems[e].append(self.es.enter_context(nc.semaphore("s_%s%d" % (e, i))))
        block = self.es.enter_context(nc.Block())

        def replay(e, engobj):
            known_op = {}
            known_dma = {}
            for o in self.ops[e]:
                for d in o.deps:
                    if d[0] == "op":
                        t = d[1]
                        key = ((t.count - 1) // SEM_CAP, (t.count - 1) % SEM_CAP + 1)
                        if known_op.get(t.eng, (-1, 0)) >= key:
                            continue
                        known_op[t.eng] = key
                        engobj.wait_ge(self.sems[t.eng][key[0]], key[1])
                    else:
                        ch, val = d[1], d[2]
                        if known_dma.get(id(ch), 0) >= val:
                            continue
                        known_dma[id(ch)] = val
                        engobj.wait_ge(ch.sem, val)
                if o.fn is None:
                    continue
                ins = o.fn(engobj)
                if o.chan is not None:
                    ins.then_inc(o.chan.sem, 16)
                elif o.needed:
                    ins.then_inc(self.sems[e][(o.count - 1) // SEM_CAP], 1)

        @block.tensor
        def _(x):
            replay("pe", x)

        @block.scalar
        def _(x):
            replay("act", x)

        @block.vector
        def _(x):
            replay("dve", x)

        @block.gpsimd
        def _(x):
            replay("pool", x)

        @block.sync
        def _(x):
            replay("sp", x)


class Alloc:
    def __init__(self, arena, ncol):
        self.arena = arena
        self.n = ncol
        self.off = 0

    def reset(self):
        self.off = 0

    def f32(self, n):
        ap = self.arena[:, self.off:self.off + n]
        self.off += n
        assert self.off <= self.n, ("sbuf overflow", self.off, self.n)
        return ap

    def b16(self, n):
        assert n % 2 == 0
        ap = self.arena[:, self.off:self.off + n // 2].bitcast(BF16)
        self.off += n // 2
        assert self.off <= self.n, ("sbuf overflow", self.off, self.n)
        return ap


ARENA_COLS = 52800

CST = {}
_off = 0
for _n, _w in [("g_ffn1_0", 1024), ("g_mix_0", 1024), ("g_xa_0", 1024), ("g_mem_0", 1024), ("g_ffn2_0", 1024),
               ("g_ffn1_1", 1024), ("g_mix_1", 1024), ("g_xa_1", 1024), ("g_mem_1", 1024), ("g_ffn2_1", 1024),
               ("g_final", 1024), ("lam", 256), ("subln", 16), ("sgnorm", 512), ("sgb", 512), ("maskb", 16)]:
    CST[_n] = (_off, _w)
    _off += _w
CST_COLS = _off


def build_program(phases=tuple(range(1, 11)), debug=False):
    nc = bass.Bass("TRN2", target_bir_lowering=False)
    es = ExitStack()
    with es:
        pg = Prog(nc, es)
        kin = "ExternalInput"
        kscr = "ExternalOutput" if debug else "Internal"

        def din(name, shape, dt=F32):
            return nc.dram_tensor(name, list(shape), dt, kind=kin).ap()

        xe = din("xe", [S_E, D])
        mem = din("mem", [256, D])
        cst = din("cst", [128, CST_COLS])
        W = {}
        for l in range(2):
            for nm, shp in [("ffn1_wg", [D, DFF]), ("ffn1_wu", [D, DFF]), ("ffn1_wd", [DFF, D]),
                            ("xa_wq", [D, D]), ("xa_wkv", [D, 2 * D]), ("xa_wo", [D, D]),
                            ("ffn2_wg", [D, DFF]), ("ffn2_wu", [D, DFF]), ("ffn2_wd", [DFF, D])]:
                W["%s%d" % (nm, l)] = din("%s%d" % (nm, l), shp)
        W["ab_w_in"] = din("ab_w_in", [D, 2560])
        W["ab_w_out"] = din("ab_w_out", [D, D])
        W["sgwT"] = din("sgwT", [128, 512])
        W["c_w_in"] = din("c_w_in", [D, 3072])
        W["c_w_out"] = din("c_w_out", [D, D])
        W["relb"] = din("relb", [128, 16 * 640])

        out = nc.dram_tensor("out", [4096, D], F32, kind="ExternalOutput").ap()
        xs = nc.dram_tensor("xs", [S_E, D], F32, kind=kscr).ap()
        qT0 = nc.dram_tensor("qT0", [4, 128, 4608], BF16, kind=kscr).ap()
        kT0 = nc.dram_tensor("kT0", [4, 128, S_E], BF16, kind=kscr).ap()
        v0 = nc.dram_tensor("v0", [S_E, 512], BF16, kind=kscr).ap()
        obT = nc.dram_tensor("obT", [4, 128, 4608], BF16, kind=kscr).ap()
        qT1 = nc.dram_tensor("qT1", [8, 128, 4608], BF16, kind=kscr).ap()
        kT1 = nc.dram_tensor("kT1", [8, 128, 4608], BF16, kind=kscr).ap()
        v1 = nc.dram_tensor("v1", [4608, 2 * D], BF16, kind=kscr).ap()

        DBG = {}
        if debug:
            DBG["hT"] = nc.dram_tensor("dbg_hT", [128, 4096], BF16, kind="ExternalOutput").ap()
            DBG["hid"] = nc.dram_tensor("dbg_hid", [128, NFF * 512], BF16, kind="ExternalOutput").ap()
            DBG["wg"] = nc.dram_tensor("dbg_wg", [128, 8 * DFF], BF16, kind="ExternalOutput").ap()
            DBG["xsb"] = nc.dram_tensor("dbg_xsb", [128, 1024], BF16, kind="ExternalOutput").ap()
            DBG["rs"] = nc.dram_tensor("dbg_rs", [128, 64], F32, kind="ExternalOutput").ap()
            DBG["cat"] = nc.dram_tensor("dbg_cat", [2, 128, 2048], BF16, kind="ExternalOutput").ap()
            DBG["oa"] = nc.dram_tensor("dbg_oa", [2, 128, 512], F32, kind="ExternalOutput").ap()
            DBG["P"] = nc.dram_tensor("dbg_P", [4, 128, 1024], BF16, kind="ExternalOutput").ap()
            DBG["rec"] = nc.dram_tensor("dbg_rec", [128, 1024], F32, kind="ExternalOutput").ap()
            DBG["t1"] = nc.dram_tensor("dbg_t1", [128, 512], F32, kind="ExternalOutput").ap()
            DBG["nl"] = nc.dram_tensor("dbg_nl", [128, 16], F32, kind="ExternalOutput").ap()
            DBG["ssum"] = nc.dram_tensor("dbg_ssum", [128, 32], F32, kind="ExternalOutput").ap()
            DBG["ee"] = nc.dram_tensor("dbg_ee", [128, 32], F32, kind="ExternalOutput").ap()
            DBG["lamt"] = nc.dram_tensor("dbg_lamt", [128, 256], F32, kind="ExternalOutput").ap()
        arena = es.enter_context(nc.sbuf_tensor("arena", [128, ARENA_COLS], F32))
        psum = es.enter_context(nc.psum_tensor("psum", [128, 4096], F32))
        al = Alloc(arena, ARENA_COLS)
        PB = [Buf("ps%d" % i) for i in range(8)]

        def bank(i, n=512, off=0):
            return psum[:, i * 512 + off:i * 512 + off + n]

        def bank16(i):
            return psum[:, i * 512:(i + 1) * 512].bitcast(BF16)

        XS = [Buf("xs_blk%d" % i) for i in range(16)]

        ch_x = [pg.chan("ch_x%d" % i) for i in range(2)]
        ch_r = [pg.chan("ch_r%d" % i) for i in range(2)]
        ch_st = [pg.chan("ch_st%d" % i) for i in range(2)]
        ch_w = [pg.chan("ch_w%d" % i) for i in range(4)]
        ch_c = pg.chan("ch_c")
        ch_m = [pg.chan("ch_m%d" % i) for i in range(6)]
        ch_dbg = pg.chan("ch_dbg")
        ch_sc = [pg.chan("ch_sc%d" % i) for i in range(2)]
        ch_wb = [pg.chan("ch_wb%d" % i) for i in range(12)]

        class NormT:
            def __init__(self, gname, src_fn, src_buf_fn, width=512, nslots=2):
                self.width = width
                self.nslots = nslots
                self.src_fn = src_fn
                self.src_buf_fn = src_buf_fn
                self.gam = al.f32(1024)
                self.b_gam = Buf("gam")
                o, w = CST[gname]
                pg.dma("sp", ch_c, self.gam, cst[:, o:o + w], writes=[self.b_gam])
                self.xin = [al.f32(1024) for _ in range(nslots)]
                self.b_xin = [Buf("xin%d" % i) for i in range(nslots)]
                self.xsb = [al.b16(1024) for _ in range(4)]
                self.b_xsb = [Buf("xsb%d" % i) for i in range(4)]
                self.junk = al.f32(1024)
                self.b_junk = Buf("junk")
                self.ss = al.f32(64)
                self.rs = al.f32(64)
                self.b_ss = [Buf("ss%d" % i) for i in range(4)]
                self.b_rs = [Buf("rs%d" % i) for i in range(4)]
                self.hT = al.b16(8 * width)
                self.b_hT = Buf("hT")
                self.ident = al.b16(128)
                self.b_ident = Buf("ident")
                ident = self.ident
                pg.op("pool", lambda e: e.memset(ident, 0.0), writes=[self.b_ident])
                pg.op("pool", lambda e: e.affine_select(out=ident, in_=ident, pattern=[[-1, 128]],
                                                          compare_op=ALU.not_equal, fill=1.0, base=0,
                                                          channel_multiplier=1), writes=[self.b_ident])
                self.cnt = 0

            def norm(self, blk, nsub=4):
                if self.nslots >= 4:
                    self.load(blk, nsub)
                    self.compute(blk, nsub)
                    return
                for t in range(nsub):
                    self.norm1(blk, t)

            def load(self, blk, nsub=4):
                sb = self.src_buf_fn(blk)
                for t in range(nsub):
                    pg.dma("sp", ch_x[t % 2], self.xin[t], self.src_fn(blk, t), reads=[sb] if sb else [], writes=[self.b_xin[t]])

            def compute(self, blk, nsub=4, defer_scale=False):
                ssall = self.ss[:, 0:64]
                rsall = self.rs[:, 0:64]
                junk = self.junk
                pg.op("dve", lambda e: e.memset(ssall, 0.0), writes=self.b_ss)
                for t in range(nsub):
                    xin, bx = self.xin[t], self.b_xin[t]
                    ss = self.ss[:, 16 * t:16 * t + 1]
                    pg.op("act", lambda e, xin=xin, ss=ss: e.activation(out=junk, in_=xin, func=AF.Square, accum_out=ss),
                          reads=[bx], writes=[self.b_junk, self.b_ss[t]])
                pg.op("act", lambda e: e.activation(out=rsall, in_=ssall, func=AF.Sqrt, scale=1.0 / 1024, bias=EPS),
                      reads=self.b_ss, writes=self.b_rs, strict=True)
                pg.op("dve", lambda e: e.reciprocal(out=rsall, in_=rsall), reads=self.b_rs, writes=self.b_rs)
                gam = self.gam
                scales = []
                for t in range(nsub):
                    def sc(t=t):
                        xin, bx = self.xin[t], self.b_xin[t]
                        rs = self.rs[:, 16 * t:16 * t + 1]
                        xsb = self.xsb[t]
                        pg.op("dve", lambda e, xsb=xsb, xin=xin, rs=rs: e.scalar_tensor_tensor(
                            out=xsb, in0=xin, scalar=rs, in1=gam, op0=ALU.mult, op1=ALU.mult),
                            reads=[bx, self.b_rs[t], self.b_gam], writes=[self.b_xsb[t]], strict=True)
                    scales.append(sc)
                if defer_scale:
                    return scales
                for sc in scales:
                    sc()

            def norm1(self, blk, t):
                if True:
                    k = self.cnt % 2
                    self.cnt += 1
                    xin, bx = self.xin[k], self.b_xin[k]
                    sb = self.src_buf_fn(blk)
                    pg.dma("sp", ch_x[k], xin, self.src_fn(blk, t), reads=[sb] if sb else [], writes=[bx])
                    ss = self.ss[:, 16 * t:16 * t + 1]
                    rs = self.rs[:, 16 * t:16 * t + 1]
                    junk = self.junk
                    pg.op("dve", lambda e, ss=ss: e.memset(ss, 0.0), writes=[self.b_ss[t]])
                    pg.op("act", lambda e, xin=xin, ss=ss, junk=junk: e.activation(
                        out=junk, in_=xin, func=AF.Square, accum_out=ss),
                        reads=[bx], writes=[self.b_junk, self.b_ss[t]])
                    pg.op("act", lambda e, ss=ss, rs=rs: e.activation(
                        out=rs, in_=ss, func=AF.Sqrt, scale=1.0 / 1024, bias=EPS),
                        reads=[self.b_ss[t]], writes=[self.b_rs[t]], strict=True)
                    pg.op("dve", lambda e, rs=rs: e.reciprocal(out=rs, in_=rs),
                          reads=[self.b_rs[t]], writes=[self.b_rs[t]])
                    xsb = self.xsb[t]
                    gam = self.gam
                    pg.op("dve", lambda e, xsb=xsb, xin=xin, rs=rs, gam=gam: e.scalar_tensor_tensor(
                        out=xsb, in0=xin, scalar=rs, in1=gam, op0=ALU.mult, op1=ALU.mult),
                        reads=[bx, self.b_rs[t], self.b_gam], writes=[self.b_xsb[t]], strict=True)

            def tr(self, blk, pbanks, nsub=4):
                hT = self.hT
                for t in range(nsub):
                    pb = pbanks[t % 2]
                    p16 = bank16(pb)
                    xsb = self.xsb[t]
                    ident = self.ident

                    def f(e, p16=p16, xsb=xsb, ident=ident):
                        for c in range(8):
                            ins = e.transpose(p16[:, c * 128:(c + 1) * 128], xsb[:, c * 128:(c + 1) * 128], ident)
                        return ins

                    pg.op("pe", f, reads=[self.b_xsb[t], self.b_ident], writes=[PB[pb]])
                    dst = hT.rearrange("p (c n) -> p c n", c=8)[:, :, t * 128:(t + 1) * 128]
                    src = p16.rearrange("p (c n) -> p c n", c=8)
                    if t % 2 == 0:
                        pg.op("act", lambda e, dst=dst, src=src: e.activation(out=dst, in_=src, func=AF.Copy),
                              reads=[PB[pb]], writes=[self.b_hT])
                    else:
                        pg.op("dve", lambda e, dst=dst, src=src: e.tensor_copy(dst, src),
                              reads=[PB[pb]], writes=[self.b_hT])

        def load_weight_cast(dst16, src, kchunks, ncols, bufs_cols, chans):
            dv = dst16.rearrange("p (k n) -> p k n", k=kchunks)
            sv = src.rearrange("(k p) n -> p k n", p=128)
            for i, (b, c0, c1) in enumerate(bufs_cols):
                pg.dma("pool", chans[i % len(chans)], dv[:, :, c0:c1], sv[:, :, c0:c1], writes=[b])

        def ffn_phase(layer, which, blocks, src, dst_is_out=False, final=False):
            al.reset()
            wg = W["ffn%d_wg%d" % (which, layer)]
            wu = W["ffn%d_wu%d" % (which, layer)]
            wd = W["ffn%d_wd%d" % (which, layer)]
            wg16 = al.b16(8 * DFF)
            wu16 = al.b16(8 * DFF)
            wd16 = al.b16(NFF * D)
            bounds = [0, 128, 704, 1408, 2112, DFF]
            NCB = len(bounds) - 1
            b_wg = [Buf("wg%d" % i) for i in range(NCB)]
            b_wu = [Buf("wu%d" % i) for i in range(NCB)]
            b_wd = [Buf("wd%d" % i) for i in range(2)]
            for i in range(NCB):
                load_weight_cast(wg16, wg, 8, DFF, [(b_wg[i], bounds[i], bounds[i + 1])], [ch_wb[i]])
                load_weight_cast(wu16, wu, 8, DFF, [(b_wu[i], bounds[i], bounds[i + 1])], [ch_wb[5 + i]])

            def wblocks(m):
                return [i for i in range(NCB) if bounds[i] < (m + 1) * 128 and bounds[i + 1] > m * 128]
            for i in range(2):
                load_weight_cast(wd16, wd, NFF, D, [(b_wd[i], i * 512, (i + 1) * 512)], [ch_wb[10 + i]])

            def src_fn(blk, t):
                r0 = blk * 512 + t * 128
                return src[r0:r0 + 128, :]

            def src_buf(blk):
                return XS[blk] if src is xs else None

            nt = NormT("g_ffn%d_%d" % (which, layer), src_fn, src_buf)
            hid = al.b16(NFF * 512)
            b_hid = Buf("hid")
            sg = [al.f32(512) for _ in range(2)]
            b_sg = [Buf("sg%d" % i) for i in range(2)]
            xres = [al.f32(1024) for _ in range(2)]
            b_xres = [Buf("xres%d" % i) for i in range(2)]
            if final:
                gfin = al.f32(1024)
                b_gfin = Buf("gfin")
                o, w = CST["g_final"]
                pg.dma("sp", ch_c, gfin, cst[:, o:o + w], writes=[b_gfin])
                fss = al.f32(32)
                frs = al.f32(32)
                b_fss = [Buf("fss%d" % i) for i in range(2)]
                b_frs = [Buf("frs%d" % i) for i in range(2)]
                fjunk = nt.junk
                b_fjunk = nt.b_junk
            rescnt = [0]

            def gate_up(blk, nxt=None):
                hT = nt.hT
                for m in range(NFF):
                    if nxt is not None and m in (2, 7, 12, 17):
                        nt.norm1(nxt, (2, 7, 12, 17).index(m))
                    s = m % 2
                    pg_, pu_ = 2 + s, 4 + s
                    wbs = wblocks(m)

                    def fg(e, m=m, pb=pg_, w16=wg16):
                        for k in range(8):
                            ins = e.matmul(bank(pb), lhsT=w16[:, k * DFF + m * 128:k * DFF + (m + 1) * 128],
                                           rhs=hT[:, k * 512:(k + 1) * 512], start=(k == 0), stop=(k == 7))
                        return ins

                    pg.op("pe", fg, reads=[nt.b_hT] + [b_wg[i] for i in wbs], writes=[PB[pg_]])

                    def fu(e, m=m, pb=pu_, w16=wu16):
                        for k in range(8):
                            ins = e.matmul(bank(pb), lhsT=w16[:, k * DFF + m * 128:k * DFF + (m + 1) * 128],
                                           rhs=hT[:, k * 512:(k + 1) * 512], start=(k == 0), stop=(k == 7))
                        return ins

                    pg.op("pe", fu, reads=[nt.b_hT] + [b_wu[i] for i in wbs], writes=[PB[pu_]])
                    sgt = sg[s]
                    pg.op("act", lambda e, sgt=sgt, pb=pg_: e.activation(out=sgt, in_=bank(pb), func=AF.Silu),
                          reads=[PB[pg_]], writes=[b_sg[s]])
                    hslice = hid[:, m * 512:(m + 1) * 512]
                    pg.op("dve", lambda e, hslice=hslice, sgt=sgt, pb=pu_: e.tensor_tensor(
                        out=hslice, in0=sgt, in1=bank(pb), op=ALU.mult),
                        reads=[b_sg[s], PB[pu_]], writes=[b_hid])

            def down(blk):
                for t in range(4):
                    k = rescnt[0] % 2
                    rescnt[0] += 1
                    xr, bxr = xres[k], b_xres[k]
                    sb = src_buf(blk)
                    pg.dma("sp", ch_r[k], xr, src_fn(blk, t), reads=[sb] if sb else [], writes=[bxr])
                    for nh in range(2):
                        pb = 6 + nh

                        def fd(e, t=t, nh=nh, pb=pb):
                            for kf in range(NFF):
                                ins = e.matmul(bank(pb), lhsT=hid[:, kf * 512 + t * 128:kf * 512 + (t + 1) * 128],
                                               rhs=wd16[:, kf * D + nh * 512:kf * D + (nh + 1) * 512],
                                               start=(kf == 0), stop=(kf == NFF - 1))
                            return ins

                        pg.op("pe", fd, reads=[b_hid, b_wd[nh]], writes=[PB[pb]])
                        xsl = xr[:, nh * 512:(nh + 1) * 512]
                        pg.op("dve", lambda e, xsl=xsl, pb=pb: e.scalar_tensor_tensor(
                            out=xsl, in0=bank(pb), scalar=0.5, in1=xsl, op0=ALU.mult, op1=ALU.add),
                            reads=[PB[pb], bxr], writes=[bxr])
                    r0 = blk * 512 + t * 128
                    if final:
                        ss = fss[:, 16 * k:16 * k + 1]
                        rs = frs[:, 16 * k:16 * k + 1]
                        pg.op("dve", lambda e, ss=ss: e.memset(ss, 0.0), writes=[b_fss[k]])
                        pg.op("act", lambda e, xr=xr, ss=ss: e.activation(out=fjunk, in_=xr, func=AF.Square,
                                                                            accum_out=ss),
                              reads=[bxr], writes=[b_fjunk, b_fss[k]])
                        pg.op("act", lambda e, ss=ss, rs=rs: e.activation(out=rs, in_=ss, func=AF.Sqrt,
                                                                            scale=1.0 / 1024, bias=EPS),
                              reads=[b_fss[k]], writes=[b_frs[k]], strict=True)
                        pg.op("dve", lambda e, rs=rs: e.reciprocal(out=rs, in_=rs), reads=[b_frs[k]],
                              writes=[b_frs[k]])
                        pg.op("dve", lambda e, xr=xr, rs=rs: e.scalar_tensor_tensor(
                            out=xr, in0=xr, scalar=rs, in1=gfin, op0=ALU.mult, op1=ALU.mult),
                            reads=[bxr, b_frs[k], b_gfin], writes=[bxr], strict=True)
                        ro = (blk - 8) * 512 + t * 128
                        pg.dma("pool", ch_st[k], out[ro:ro + 128, :], xr, reads=[bxr])
                    else:
                        pg.dma("pool", ch_st[k], xs[r0:r0 + 128, :], xr, reads=[bxr], writes=[XS[blk]])

            nt.norm(blocks[0])
            nt.tr(blocks[0], (0, 1))
            if debug and which == 1 and layer == 0:
                pg.dma("sp", ch_dbg, DBG["hT"], nt.hT, reads=[nt.b_hT])
                pg.dma("sp", ch_dbg, DBG["xsb"], nt.xsb[0], reads=[nt.b_xsb[0]])
                pg.dma("sp", ch_dbg, DBG["rs"], nt.rs, reads=nt.b_rs)
                pg.dma("sp", ch_dbg, DBG["wg"], wg16, reads=b_wg)
            for i, blk in enumerate(blocks):
                nxt = blocks[i + 1] if i + 1 < len(blocks) else None
                gate_up(blk, nxt)
                if debug and which == 1 and layer == 0 and i == 0:
                    pg.dma("sp", ch_dbg, DBG["hid"], hid, reads=[b_hid])
                if nxt is not None:
                    nt.tr(nxt, (0, 1))
                down(blk)
            pg.barrier()

        def mm_group(pb, n, lhs_fn, rhs_fn, nk, reads, off=0):
            def f(e):
                for k in range(nk):
                    ins = e.matmul(bank(pb, n, off), lhsT=lhs_fn(k), rhs=rhs_fn(k), start=(k == 0), stop=(k == nk - 1))
                return ins
            return pg.op("pe", f, reads=reads, writes=[PB[pb]])

        cp_cnt = [0]

        def evac_copy(dst, pb, n, dbuf, off=0, eng=None):
            if eng is None:
                eng = "act" if cp_cnt[0] % 2 == 0 else "dve"
                cp_cnt[0] += 1
            src = bank(pb, n, off)
            if eng == "act":
                pg.op("act", lambda e: e.activation(out=dst, in_=src, func=AF.Copy), reads=[PB[pb]], writes=[dbuf])
            else:
                pg.op("dve", lambda e: e.tensor_copy(dst, src), reads=[PB[pb]], writes=[dbuf])

        def xs_src(blk, t):
            r0 = blk * 512 + t * 128
            return xs[r0:r0 + 128, :]

        def cst_load(name, buf, ncol=None):
            o, w = CST[name]
            tl = al.f32(w if ncol is None else ncol)
            pg.dma("sp", ch_c, tl[:, 0:w], cst[:, o:o + w], writes=[buf])
            return tl

        def proj_ab_phase():
            al.reset()
            NW = 2560
            w16 = al.b16(8 * NW)
            b_w = [Buf("wab%d" % i) for i in range(5)]
            for ci in (1, 2, 0, 3, 4):
                load_weight_cast(w16, W["ab_w_in"], 8, NW, [(b_w[ci], ci * 512, (ci + 1) * 512)], [ch_w[ci % 4]])
            nt = NormT("g_mix_0", xs_src, lambda blk: XS[blk], nslots=4)
            b_c = Buf("cst2")
            sgn = cst_load("sgnorm", b_c)
            sgb = cst_load("sgb", b_c)
            sgw16 = al.b16(512)
            b_sgw = Buf("sgw")
            pg.dma("pool", ch_w[0], sgw16, W["sgwT"], writes=[b_sgw])
            for g in range(4):
                sl = sgw16[64:128, g * 128:g * 128 + 64]
                pg.op("pool", lambda e, sl=sl: e.memset(sl, 0.0), reads=[b_sgw], writes=[b_sgw])
            kst = al.b16(4 * 512); b_kst = Buf("kst")
            qst = al.b16(4 * 512); b_qst = Buf("qst")
            gu = al.b16(4 * 512); b_gu = Buf("gu")
            vst = al.b16(4 * 512); b_vst = Buf("vst")
            obst = al.b16(4 * 512); b_obst = Buf("obst")
            gv = [al.f32(512) for _ in range(4)]; b_gv = [Buf("gv%d" % i) for i in range(4)]
            vgn = [al.b16(512) for _ in range(4)]; b_vgn = [Buf("vgn%d" % i) for i in range(4)]
            tmpg = [al.f32(512) for _ in range(2)]; b_tmpg = [Buf("tmpg%d" % i) for i in range(2)]
            gss = al.f32(64); grs = al.f32(64)
            b_gss = [Buf("gss%d" % i) for i in range(4)]; b_grs = [Buf("grs%d" % i) for i in range(4)]
            gjunk = nt.junk; b_gjunk = nt.b_junk
            hT = nt.hT
            fm_rot = [0]

            def fm_chunk(colbase, f, wb):
                pb = 2 + (fm_rot[0] % 2)
                fm_rot[0] += 1
                mm_group(pb, 512, lambda k: w16[:, k * NW + colbase + f * 128:k * NW + colbase + (f + 1) * 128],
                         lambda k: hT[:, k * 512:(k + 1) * 512], 8, [nt.b_hT, wb])
                return pb

            tm_rot = [0]

            def tm_tile(colbase, t, wb):
                pb = 4 + (tm_rot[0] % 2)
                tm_rot[0] += 1
                mm_group(pb, 512, lambda k: hT[:, k * 512 + t * 128:k * 512 + (t + 1) * 128],
                         lambda k: w16[:, k * NW + colbase:k * NW + colbase + 512], 8, [nt.b_hT, wb])
                return pb

            blocks = list(range(16))
            nt.norm(blocks[0])
            nt.tr(blocks[0], (0, 1))
            for i, blk in enumerate(blocks):
                nxt = blocks[i + 1] if i + 1 < len(blocks) else None
                full = blk >= 7
                if nxt is not None:
                    nt.load(nxt)
                for f in range(4):
                    pb = fm_chunk(512, f, b_w[1])
                    evac_copy(kst[:, f * 512:(f + 1) * 512], pb, 512, b_kst)
                pg.dma("pool", ch_sc[0], kT0[:, :, blk * 512:(blk + 1) * 512].rearrange("h p n -> p h n"),
                       kst.rearrange("p (h n) -> p h n", h=4), reads=[b_kst])
                for t in range(4):
                    pb = tm_tile(1024, t, b_w[2])
                    evac_copy(vst[:, t * 512:(t + 1) * 512], pb, 512, b_vst)
                pg.dma("pool", ch_sc[1], v0[blk * 512:(blk + 1) * 512, :].rearrange("(t p) c -> p t c", p=128),
                       vst.rearrange("p (t c) -> p t c", t=4), reads=[b_vst])
                if full:
                    lb = blk - 7
                    pg.op("dve", lambda e: e.memset(gss, 0.0), writes=b_gss)
                    for t in range(4):
                        pb = tm_tile(2048, t, b_w[4])
                        gvt = gv[t]
                        pg.op("act", lambda e, gvt=gvt, pb=pb: e.activation(out=gvt, in_=bank(pb), func=AF.Gelu_apprx_tanh),
                              reads=[PB[pb]], writes=[b_gv[t]])
                        ss = gss[:, 16 * t:16 * t + 1]
                        pg.op("act", lambda e, gvt=gvt, ss=ss: e.activation(out=gjunk[:, 0:512], in_=gvt, func=AF.Square,
                                                                             accum_out=ss),
                              reads=[b_gv[t]], writes=[b_gjunk, b_gss[t]])
                if nxt is not None:
                    nt.compute(nxt)
                if full:
                    for f in range(4):
                        pb = fm_chunk(0, f, b_w[0])
                        evac_copy(qst[:, f * 512:(f + 1) * 512], pb, 512, b_qst)
                    pg.dma("pool", ch_sc[0], qT0[:, :, lb * 512:(lb + 1) * 512].rearrange("h p n -> p h n"),
                           qst.rearrange("p (h n) -> p h n", h=4), reads=[b_qst])
                    for f in range(4):
                        pb = fm_chunk(1536, f, b_w[3])
                        dst = gu[:, f * 512:(f + 1) * 512]
                        pg.op("act", lambda e, dst=dst, pb=pb: e.activation(out=dst, in_=bank(pb), func=AF.Gelu_apprx_tanh),
                              reads=[PB[pb]], writes=[b_gu])
                    pg.op("act", lambda e: e.activation(out=grs, in_=gss, func=AF.Sqrt, scale=1.0 / 512, bias=EPS),
                          reads=b_gss, writes=b_grs, strict=True)
                    pg.op("dve", lambda e: e.reciprocal(out=grs, in_=grs), reads=b_grs, writes=b_grs)
                    for t in range(4):
                        gvt = gv[t]
                        rs = grs[:, 16 * t:16 * t + 1]
                        vg = vgn[t]
                        pg.op("dve", lambda e, vg=vg, gvt=gvt, rs=rs: e.scalar_tensor_tensor(
                            out=vg, in0=gvt, scalar=rs, in1=sgn, op0=ALU.mult, op1=ALU.mult),
                            reads=[b_gv[t], b_grs[t], b_c], writes=[b_vgn[t]], strict=True)
                    for t in range(4):
                        s2 = t % 2
                        vg = vgn[t]
                        gb = (6, 7)[s2]

                        def fgate(e, vg=vg, gb=gb):
                            for g in range(4):
                                ins = e.matmul(bank(gb, 128, g * 128), lhsT=vg[:, g * 128:(g + 1) * 128],
                                               rhs=sgw16[:, g * 128:(g + 1) * 128], start=True, stop=True)
                            return ins

                        pg.op("pe", fgate, reads=[b_vgn[t], b_sgw], writes=[PB[gb]])
                        tg = tmpg[s2]
                        pg.op("dve", lambda e, tg=tg, gb=gb: e.tensor_tensor(out=tg, in0=bank(gb), in1=sgb, op=ALU.add),
                              reads=[PB[gb], b_c], writes=[b_tmpg[s2]])
                        o3 = obst.rearrange("p (g n) -> p g n", g=4)[:, :, t * 128:(t + 1) * 128]
                        g3 = gu.rearrange("p (g n) -> p g n", g=4)[:, :, t * 128:(t + 1) * 128]
                        t3 = tg.rearrange("p (g n) -> p g n", g=4)
                        pg.op("dve", lambda e, o3=o3, g3=g3, t3=t3: e.tensor_tensor(out=o3, in0=t3, in1=g3, op=ALU.mult),
                              reads=[b_tmpg[s2], b_gu], writes=[b_obst])
                    pg.dma("pool", ch_sc[1], obT[:, :, lb * 512:(lb + 1) * 512].rearrange("h p n -> p h n"),
                           obst.rearrange("p (h n) -> p h n", h=4), reads=[b_obst])
                if nxt is not None:
                    nt.tr(nxt, (0, 1))
            pg.barrier()

        def attn_ab_phase():
            al.reset()
            kT = al.b16(4 * S_E)
            V = al.b16(64 * 512)
            wo16 = al.b16(8 * D)
            b_kv = [Buf("kv%d" % g) for g in range(4)]
            b_wo = Buf("wo")
            kT3 = kT.rearrange("p (h n) -> p h n", h=4)
            V3 = V.rearrange("p (k c) -> p k c", k=64)
            for g in range(4):
                pg.dma("sp", ch_m[g % 4], kT3[:, :, g * 2048:(g + 1) * 2048],
                       kT0[:, :, g * 2048:(g + 1) * 2048].rearrange("h p n -> p h n"), writes=[b_kv[g]])
                pg.dma("sp", ch_m[g % 4], V3[:, g * 16:(g + 1) * 16, :],
                       v0[g * 2048:(g + 1) * 2048, :].rearrange("(k p) c -> p k c", p=128), writes=[b_kv[g]])
            load_weight_cast(wo16, W["ab_w_out"], 8, D, [(b_wo, 0, D)], [ch_w[0]])
            b_c = Buf("cst3")
            lamt = cst_load("lam", b_c)
            subln = cst_load("subln", b_c, 16)
            maskb = cst_load("maskb", b_c, 16)
            ones16 = al.b16(128); b_ones = Buf("ones")
            pg.op("pool", lambda e: e.memset(ones16, 1.0), writes=[b_ones])
            prod = al.f32(128); ssum = al.f32(32); ee = al.f32(32); nl = al.f32(16); sublnS = al.f32(16)
            b_l = Buf("lamc")
            pg.op("dve", lambda e: e.memset(ssum, 0.0), writes=[b_l])
            pg.op("dve", lambda e: e.scalar_tensor_tensor(out=prod[:, 0:64], in0=lamt[:, 0:64], scalar=1.0, in1=lamt[:, 64:128],
                                                            op0=ALU.mult, op1=ALU.mult, accum_out=ssum[:, 0:1]),
                  reads=[b_c], writes=[b_l], strict=True)
            pg.op("dve", lambda e: e.scalar_tensor_tensor(out=prod[:, 64:128], in0=lamt[:, 128:192], scalar=1.0, in1=lamt[:, 192:256],
                                                            op0=ALU.mult, op1=ALU.mult, accum_out=ssum[:, 16:17]),
                  reads=[b_c], writes=[b_l])
            b_l2 = Buf("lamc2")
            pg.op("act", lambda e: e.activation(out=ee[:, 0:1], in_=ssum[:, 0:1], func=AF.Exp), reads=[b_l], writes=[b_l2])
            pg.op("act", lambda e: e.activation(out=ee[:, 16:17], in_=ssum[:, 16:17], func=AF.Exp), reads=[b_l], writes=[b_l2])
            b_l3 = Buf("lamc3")
            pg.op("dve", lambda e: e.tensor_tensor(out=nl[:, 0:1], in0=ee[:, 16:17], in1=ee[:, 0:1], op=ALU.subtract),
                  reads=[b_l2], writes=[b_l3])
            pg.op("dve", lambda e: e.tensor_scalar(out=nl[:, 0:1], in0=nl[:, 0:1], scalar1=-0.2, scalar2=None, op0=ALU.add),
                  reads=[b_l3], writes=[b_l3], strict=True)
            pg.op("dve", lambda e: e.tensor_scalar(out=sublnS[:, 0:1], in0=subln[:, 0:1], scalar1=0.8, scalar2=None, op0=ALU.mult),
                  reads=[b_c], writes=[b_l3])
            qT = [al.b16(4 * 512) for _ in range(2)]; b_qT = [Buf("qT%d" % i) for i in range(2)]
            obt1 = al.b16(4 * 512); b_obt1 = Buf("obt")
            obt = [obt1, obt1]; b_obt = [b_obt1, b_obt1]
            NP = 5
            Pt = [al.b16(1024) for _ in range(NP)]; b_P = [Buf("P%d" % i) for i in range(NP)]
            cat = al.b16(4 * 512); b_cat = Buf("cat")
            rec = al.f32(1024); b_rec = Buf("rec")
            t1 = al.f32(512); t2 = al.f32(512); rstd = t1
            oa4 = [al.f32(512) for _ in range(4)]
            b_t = Buf("t12"); b_oa = [Buf("oa%d" % i) for i in range(4)]; b_rstd = b_t
            sq4 = [al.b16(512) for _ in range(4)]; b_sq = [Buf("sq%d" % i) for i in range(4)]
            lnS = rec; b_lnS = b_rec
            xres2 = [al.f32(1024) for _ in range(2)]; b_xres2 = [Buf("xres%d" % i) for i in range(2)]
            xres = xres2 + xres2; b_xres = b_xres2 + b_xres2
            acc = al.f32(1024); b_acc = Buf("acc")
            ones32 = al.f32(128); b_ones32 = Buf("ones32")
            pg.op("pool", lambda e: e.memset(ones32, 1.0), writes=[b_ones32])
            ucnt = [0]
            rcnt = [0]

            def sslot(k):
                return (4 + 2 * (k % 2), 5 + 2 * (k % 2))

            blocks = list(range(7, 16))
            DEPTH = 2

            def make_wout(blk):
                groups = []
                for t in range(4):
                    for nh in range(2):
                        def g(t=t, nh=nh, blk=blk):
                            kx = t % 2
                            xr, bxr = xres2[kx], b_xres2[kx]
                            if nh == 0:
                                pg.dma("sp", ch_r[kx], xr, xs_src(blk, t), reads=[XS[blk]], writes=[bxr])
                            pb = 2 + nh

                            def fo(e, t=t, nh=nh, pb=pb):
                                for c in range(8):
                                    src = cat if c < 4 else obt1
                                    cc = c % 4
                                    ins = e.matmul(bank(pb), lhsT=src[:, cc * 512 + t * 128:cc * 512 + (t + 1) * 128],
                                                   rhs=wo16[:, c * D + nh * 512:c * D + (nh + 1) * 512],
                                                   start=(c == 0), stop=(c == 7))
                                return ins

                            pg.op("pe", fo, reads=[b_cat, b_obt1, b_wo], writes=[PB[pb]])
                            xsl = xr[:, nh * 512:(nh + 1) * 512]
                            pg.op("dve", lambda e, xsl=xsl, pb=pb: e.tensor_tensor(out=xsl, in0=bank(pb), in1=xsl, op=ALU.add),
                                  reads=[PB[pb], bxr], writes=[bxr])
                            if nh == 1:
                                r0 = blk * 512 + t * 128
                                pg.dma("pool", ch_st[kx], xs[r0:r0 + 128, :], xr, reads=[bxr], writes=[XS[blk]])
                        groups.append(g)
                return groups

            pending = None
            for bi, blk in enumerate(blocks):
                lb = blk - 7
                qs = bi % 2
                q = qT[qs]
                pg.dma("sp", ch_x[qs], q.rearrange("p (h n) -> p h n", h=4),
                       qT0[:, :, lb * 512:(lb + 1) * 512].rearrange("h p n -> p h n"), writes=[b_qT[qs]])
                nkt = 4 * (blk + 1)
                units = [(h, kt) for h in range(4) for kt in range(nkt)]
                base = ucnt[0]
                wgroups = make_wout(pending) if pending is not None else []

                def stage_a(u):
                    h, kt = units[u]
                    k = base + u
                    b0, b1 = sslot(k)
                    kvb = b_kv[kt // 16]

                    def f(e, h=h, kt=kt, b0=b0, b1=b1, q=q):
                        e.matmul(bank(b0), lhsT=kT[0:64, h * S_E + kt * 128:h * S_E + (kt + 1) * 128],
                                 rhs=q[0:64, h * 512:(h + 1) * 512], start=True, stop=True)
                        return e.matmul(bank(b1), lhsT=kT[64:128, h * S_E + kt * 128:h * S_E + (kt + 1) * 128],
                                        rhs=q[64:128, h * 512:(h + 1) * 512], start=True, stop=True)

                    pg.op("pe", f, reads=[kvb, b_qT[qs]], writes=[PB[b0], PB[b1]])
                    ps2 = psum[:, b0 * 512:b0 * 512 + 1024]
                    P = Pt[k % NP]; bP = b_P[k % NP]
                    j = kt - 4 * blk
                    bias = maskb[:, 0:1] if kt < 32 else 0.0
                    if j <= 0:
                        pg.op("act", lambda e, P=P, ps2=ps2, bias=bias: e.activation(out=P, in_=ps2, func=AF.Exp,
                                                                                     scale=0.125, bias=bias),
                              reads=[PB[b0], PB[b1], b_c], writes=[bP])
                    else:
                        P3 = P.rearrange("p (m n) -> p m n", m=2)
                        s3 = ps2.rearrange("p (m n) -> p m n", m=2)
                        pg.op("act", lambda e, P3=P3, s3=s3, j=j, bias=bias: e.activation(
                            out=P3[:, :, 128 * j:512], in_=s3[:, :, 128 * j:512], func=AF.Exp, scale=0.125, bias=bias),
                            reads=[PB[b0], PB[b1], b_c], writes=[bP])
                        pg.op("pool", lambda e, P3=P3, j=j: e.memset(P3[:, :, 0:128 * j], 0.0), writes=[bP])
                    if j >= 0:
                        P3 = P.rearrange("p (m n) -> p m n", m=2)
                        pg.op("pool", lambda e, P3=P3, j=j: e.memset(P3[64:128, :, 128 * j:128 * j + 64], 0.0),
                              reads=[bP], writes=[bP])
                    if debug and bi == 0 and h == 3 and kt in (0, 1, 28, 31):
                        pg.dma("sp", ch_dbg, DBG["P"][(0, 1, 28, 31).index(kt)], P, reads=[bP])

                def stage_b_pv(u):
                    h, kt = units[u]
                    k = base + u
                    P = Pt[k % NP]; bP = b_P[k % NP]
                    kvb = b_kv[kt // 16]
                    first = (kt == 0)
                    last = (kt == nkt - 1)

                    def fpv(e, h=h, kt=kt, P=P, first=first, last=last):
                        vv = V[:, kt * 512 + h * 128:kt * 512 + (h + 1) * 128]
                        e.matmul(bank(0), lhsT=vv, rhs=P[:, 0:512], start=first, stop=last)
                        return e.matmul(bank(1), lhsT=vv, rhs=P[:, 512:1024], start=first, stop=last)

                    pg.op("pe", fpv, reads=[bP, kvb], writes=[PB[0], PB[1]])

                def stage_b_add(u):
                    h, kt = units[u]
                    k = base + u
                    P = Pt[k % NP]; bP = b_P[k % NP]
                    first = (kt == 0)
                    last = (kt == nkt - 1)
                    if first:
                        pg.op("dve", lambda e, P=P: e.tensor_copy(acc, P), reads=[bP], writes=[b_acc])
                    else:
                        pg.op("dve", lambda e, P=P: e.tensor_tensor(out=acc, in0=acc, in1=P, op=ALU.add),
                              reads=[bP, b_acc], writes=[b_acc])
                    if last:
                        def fsum(e):
                            e.matmul(bank(2), lhsT=ones32, rhs=acc[:, 0:512], start=True, stop=True)
                            return e.matmul(bank(3), lhsT=ones32, rhs=acc[:, 512:1024], start=True, stop=True)

                        pg.op("pe", fsum, reads=[b_acc, b_ones32], writes=[PB[2], PB[3]])
                        oa = oa4[h]
                        sq16 = sq4[h]
                        pg.op("act", lambda e: e.activation(out=lnS, in_=psum[:, 2 * 512:4 * 512], func=AF.Ln),
                              reads=[PB[2], PB[3]], writes=[b_lnS])
                        pg.op("act", lambda e: e.activation(out=rec, in_=lnS, func=AF.Exp, scale=-1.0),
                              reads=[b_lnS], writes=[b_rec])

                        def fin(h=h, oa=oa, sq16=sq16):
                            pg.op("dve", lambda e: e.tensor_tensor(out=t1, in0=bank(0), in1=rec[:, 0:512], op=ALU.mult),
                                  reads=[PB[0], b_rec], writes=[b_t])
                            pg.op("dve", lambda e: e.tensor_tensor(out=t2, in0=bank(1), in1=rec[:, 512:1024], op=ALU.mult),
                                  reads=[PB[1], b_rec], writes=[b_t])
                            pg.op("dve", lambda e, oa=oa: e.scalar_tensor_tensor(out=oa, in0=t2, scalar=nl[:, 0:1], in1=t1,
                                                                                  op0=ALU.mult, op1=ALU.add),
                                  reads=[b_t, b_l3], writes=[b_oa[h]])
                            pg.op("dve", lambda e, oa=oa, sq16=sq16: e.tensor_tensor(out=sq16, in0=oa, in1=oa, op=ALU.mult),
                                  reads=[b_oa[h]], writes=[b_sq[h]])
                        deferred.append([2, fin])

                nu = len(units)
                deferred = []
                pvq = []
                for step in range(nu + DEPTH):
                    if step < nu:
                        stage_a(step)
                    for d in deferred:
                        d[0] -= 1
                    while deferred and deferred[0][0] <= 0:
                        deferred.pop(0)[1]()
                    if step >= DEPTH:
                        pvq.append(step - DEPTH)
                        if not deferred:
                            while pvq:
                                stage_b_pv(pvq.pop(0))
                        stage_b_add(step - DEPTH)
                    if wgroups and step >= 6 and (step - 6) % 6 == 0:
                        wgroups.pop(0)()
                while deferred:
                    deferred.pop(0)[1]()
                while pvq:
                    stage_b_pv(pvq.pop(0))
                while wgroups:
                    wgroups.pop(0)()
                ucnt[0] += nu
                pg.dma("sp", ch_x[qs], obt1.rearrange("p (h n) -> p h n", h=4),
                       obT[:, :, lb * 512:(lb + 1) * 512].rearrange("h p n -> p h n"), writes=[b_obt1])
                for h in range(4):
                    ucnt[0] += 1
                    mb = sslot(ucnt[0])[0]
                    oa = oa4[h]
                    sq16 = sq4[h]
                    pg.op("pe", lambda e, mb=mb, sq16=sq16: e.matmul(bank(mb), lhsT=ones16, rhs=sq16, start=True, stop=True),
                          reads=[b_sq[h], b_ones], writes=[PB[mb]])
                    pg.op("act", lambda e, mb=mb: e.activation(out=rstd, in_=bank(mb), func=AF.Ln, scale=1.0 / 128, bias=EPS),
                          reads=[PB[mb]], writes=[b_rstd])
                    pg.op("act", lambda e: e.activation(out=rstd, in_=rstd, func=AF.Exp, scale=-0.5),
                          reads=[b_rstd], writes=[b_rstd])
                    ch = cat[:, h * 512:(h + 1) * 512]
                    pg.op("dve", lambda e, ch=ch, oa=oa: e.scalar_tensor_tensor(out=ch, in0=oa, scalar=sublnS[:, 0:1], in1=rstd,
                                                                                 op0=ALU.mult, op1=ALU.mult),
                          reads=[b_oa[h], b_rstd, b_l3], writes=[b_cat])
                pending = blk
            for g in make_wout(pending):
                g()
            pg.barrier()

        def xattn_phase(layer, blocks):
            al.reset()
            wq16 = al.b16(8 * D); wkv16 = al.b16(8 * 2 * D); wo16 = al.b16(8 * D)
            b_wq = Buf("wq"); b_wkv = [Buf("wkvk"), Buf("wkvv")]; b_wo = Buf("wo")
            load_weight_cast(wkv16, W["xa_wkv%d" % layer], 8, 2 * D, [(b_wkv[0], 0, D), (b_wkv[1], D, 2 * D)], [ch_w[0], ch_w[1]])
            load_weight_cast(wq16, W["xa_wq%d" % layer], 8, D, [(b_wq, 0, D)], [ch_w[2]])
            load_weight_cast(wo16, W["xa_wo%d" % layer], 8, D, [(b_wo, 0, D)], [ch_w[3]])
            ntm = NormT("g_mem_%d" % layer, lambda blk, t: mem[t * 128:(t + 1) * 128, :], lambda blk: None, width=256)
            nt = NormT("g_xa_%d" % layer, xs_src, lambda blk: XS[blk], nslots=4)
            ones16 = al.b16(128); b_ones = Buf("ones")
            pg.op("pool", lambda e: e.memset(ones16, 1.0), writes=[b_ones])
            kxT = al.b16(8 * 256); b_kx = Buf("kxT")
            Vx = al.b16(2 * D); b_vx = Buf("Vx")
            qT = al.b16(8 * 512); b_qT = Buf("qTx")
            oT = al.b16(8 * 512); b_oT = Buf("oTx")
            Pt = [al.b16(1024) for _ in range(2)]; b_P = [Buf("Px%d" % i) for i in range(2)]
            rec = al.f32(512); b_rec = Buf("recx")
            xres = [al.f32(1024) for _ in range(2)]; b_xres = [Buf("xres%d" % i) for i in range(2)]
            ntm.norm(0, nsub=2)
            ntm.tr(0, (0, 1), nsub=2)
            memT = ntm.hT
            prot = [0]

            def pbank():
                pb = (7, 6)[prot[0] % 2]
                prot[0] += 1
                return pb

            for f in range(8):
                pb = pbank()
                mm_group(pb, 256, lambda k, f=f: wkv16[:, k * 2 * D + f * 128:k * 2 * D + (f + 1) * 128],
                         lambda k: memT[:, k * 256:(k + 1) * 256], 8, [ntm.b_hT, b_wkv[0]])
                evac_copy(kxT[:, f * 256:(f + 1) * 256], pb, 256, b_kx)
            for mt in range(2):
                for nh in range(2):
                    pb = pbank()
                    mm_group(pb, 512, lambda k, mt=mt: memT[:, k * 256 + mt * 128:k * 256 + (mt + 1) * 128],
                             lambda k, nh=nh: wkv16[:, k * 2 * D + D + nh * 512:k * 2 * D + D + (nh + 1) * 512], 8,
                             [ntm.b_hT, b_wkv[1]])
                    evac_copy(Vx[:, mt * D + nh * 512:mt * D + (nh + 1) * 512], pb, 512, b_vx)
            hT = nt.hT
            scnt = [0]
            lnS = al.f32(512); b_lnS = Buf("lnS")
            xres4 = xres + [al.f32(1024) for _ in range(2)]
            b_xres4 = b_xres + [Buf("xres%d" % i) for i in range(2, 4)]
            nt.norm(blocks[0])
            nt.tr(blocks[0], (0, 1))
            for i, blk in enumerate(blocks):
                nxt = blocks[i + 1] if i + 1 < len(blocks) else None
                for t in range(4):
                    pg.dma("sp", ch_r[t % 2], xres4[t], xs_src(blk, t), reads=[XS[blk]], writes=[b_xres4[t]])
                if nxt is not None:
                    nt.load(nxt)
                for f in range(8):
                    pb = pbank()
                    mm_group(pb, 512, lambda k, f=f: wq16[:, k * D + f * 128:k * D + (f + 1) * 128],
                             lambda k: hT[:, k * 512:(k + 1) * 512], 8, [nt.b_hT, b_wq])
                    evac_copy(qT[:, f * 512:(f + 1) * 512], pb, 512, b_qT)

                def emit_qk(h):
                    sl = (scnt[0] + h) % 2
                    sb0 = (2, 0)[sl]

                    def fs(e, h=h, sb0=sb0):
                        for mt in range(2):
                            for c in range(2):
                                ins = e.matmul(bank(sb0 + mt), lhsT=kxT[:, (2 * h + c) * 256 + mt * 128:(2 * h + c) * 256 + (mt + 1) * 128],
                                               rhs=qT[:, (2 * h + c) * 512:(2 * h + c + 1) * 512], start=(c == 0), stop=(c == 1))
                        return ins

                    pg.op("pe", fs, reads=[b_kx, b_qT], writes=[PB[sb0], PB[sb0 + 1]])
                    ps2 = psum[:, sb0 * 512:sb0 * 512 + 1024]
                    P = Pt[sl]; bP = b_P[sl]
                    pg.op("act", lambda e, P=P, ps2=ps2: e.activation(out=P, in_=ps2, func=AF.Exp, scale=1.0 / 16),
                          reads=[PB[sb0], PB[sb0 + 1]], writes=[bP])

                emit_qk(0)
                for h in range(4):
                    sl = (scnt[0] + h) % 2
                    if h + 1 < 4:
                        emit_qk(h + 1)
                    P = Pt[sl]; bP = b_P[sl]

                    def fo(e, h=h, P=P):
                        for ei in range(2):
                            for mt in range(2):
                                e.matmul(bank(4 + ei), lhsT=Vx[:, mt * D + h * 256 + ei * 128:mt * D + h * 256 + (ei + 1) * 128],
                                         rhs=P[:, mt * 512:(mt + 1) * 512], start=(mt == 0), stop=(mt == 1))
                        for mt in range(2):
                            ins = e.matmul(bank(6), lhsT=ones16, rhs=P[:, mt * 512:(mt + 1) * 512], start=(mt == 0), stop=(mt == 1))
                        return ins

                    pg.op("pe", fo, reads=[bP, b_vx, b_ones], writes=[PB[4], PB[5], PB[6]])
                    pg.op("act", lambda e: e.activation(out=lnS, in_=bank(6), func=AF.Ln), reads=[PB[6]], writes=[b_lnS])
                    pg.op("act", lambda e: e.activation(out=rec, in_=lnS, func=AF.Exp, scale=-1.0), reads=[b_lnS], writes=[b_rec])
                    for ei in range(2):
                        dst = oT[:, (2 * h + ei) * 512:(2 * h + ei + 1) * 512]
                        pg.op("dve", lambda e, dst=dst, ei=ei: e.tensor_tensor(out=dst, in0=bank(4 + ei), in1=rec, op=ALU.mult),
                              reads=[PB[4 + ei], b_rec], writes=[b_oT])
                scnt[0] += 4
                scales = nt.compute(nxt, defer_scale=True) if nxt is not None else []
                for t in range(4):
                    xr, bxr = xres4[t], b_xres4[t]
                    for nh in range(2):
                        pb = pbank()
                        mm_group(pb, 512, lambda c, t=t: oT[:, c * 512 + t * 128:c * 512 + (t + 1) * 128],
                                 lambda c, nh=nh: wo16[:, c * D + nh * 512:c * D + (nh + 1) * 512], 8, [b_oT, b_wo])
                        xsl = xr[:, nh * 512:(nh + 1) * 512]
                        pg.op("dve", lambda e, xsl=xsl, pb=pb: e.tensor_tensor(out=xsl, in0=bank(pb), in1=xsl, op=ALU.add),
                              reads=[PB[pb], bxr], writes=[bxr])
                    r0 = blk * 512 + t * 128
                    pg.dma("pool", ch_st[t % 2], xs[r0:r0 + 128, :], xr, reads=[bxr], writes=[XS[blk]])
                    if scales:
                        scales[t]()
                if nxt is not None:
                    nt.tr(nxt, (0, 1))
            pg.barrier()

        def proj_c_phase():
            al.reset()
            NW = 3072
            w16 = al.b16(8 * NW)
            b_w = [Buf("wc%d" % i) for i in range(3)]
            for ci in (1, 2, 0):
                load_weight_cast(w16, W["c_w_in"], 8, NW, [(b_w[ci], ci * D, ci * D + 512)], [ch_w[ci]])
                load_weight_cast(w16, W["c_w_in"], 8, NW, [(b_w[ci], ci * D + 512, (ci + 1) * D)], [ch_w[ci]])
            nt = NormT("g_mix_1", xs_src, lambda blk: XS[blk], nslots=4)
            hT = nt.hT
            kst = al.b16(8 * 512); b_kst = Buf("kst")
            qst = al.b16(8 * 512); b_qst = Buf("qst")
            vst = al.b16(4 * 2 * D); b_vst = Buf("vst")
            pg.op("pool", lambda e: e.memset(vst, 1.0), writes=[b_vst])
            rot = [0]

            def pbk():
                pb = 2 + rot[0] % 4
                rot[0] += 1
                return pb

            blocks = list(range(7, 16))
            nt.norm(blocks[0])
            nt.tr(blocks[0], (0, 1))
            for i, blk in enumerate(blocks):
                lb = blk - 7
                nxt = blocks[i + 1] if i + 1 < len(blocks) else None
                if nxt is not None:
                    nt.load(nxt)
                for f in range(8):
                    if nxt is not None and f == 4:
                        nt.compute(nxt)
                    pb = pbk()
                    mm_group(pb, 512, lambda k, f=f: w16[:, k * NW + D + f * 128:k * NW + D + (f + 1) * 128],
                             lambda k: hT[:, k * 512:(k + 1) * 512], 8, [nt.b_hT, b_w[1]])
                    evac_copy(kst[:, f * 512:(f + 1) * 512], pb, 512, b_kst)
                pg.dma("pool", ch_sc[0], kT1[:, :, lb * 512:(lb + 1) * 512].rearrange("h p n -> p h n"),
                       kst.rearrange("p (h n) -> p h n", h=8), reads=[b_kst])
                for t in range(4):
                    for nh in range(2):
                        pb = pbk()
                        mm_group(pb, 512, lambda k, t=t: hT[:, k * 512 + t * 128:k * 512 + (t + 1) * 128],
                                 lambda k, nh=nh: w16[:, k * NW + 2 * D + nh * 512:k * NW + 2 * D + (nh + 1) * 512], 8,
                                 [nt.b_hT, b_w[2]])
                        vdst = vst[:, t * 2 * D + nh * D:t * 2 * D + (nh + 1) * D].rearrange("p (h c) -> p h c", h=8)[:, :, 0:64]
                        vsrc = bank(pb).rearrange("p (h c) -> p h c", h=8)
                        if (t + nh) % 2 == 0:
                            pg.op("act", lambda e, vdst=vdst, vsrc=vsrc: e.activation(out=vdst, in_=vsrc, func=AF.Copy),
                                  reads=[PB[pb]], writes=[b_vst])
                        else:
                            pg.op("dve", lambda e, vdst=vdst, vsrc=vsrc: e.tensor_copy(vdst, vsrc), reads=[PB[pb]], writes=[b_vst])
                pg.dma("pool", ch_sc[1], v1[lb * 512:(lb + 1) * 512, :].rearrange("(t p) c -> p t c", p=128),
                       vst.rearrange("p (t c) -> p t c", t=4), reads=[b_vst])
                if blk >= 8:
                    for f in range(8):
                        pb = pbk()
                        mm_group(pb, 512, lambda k, f=f: w16[:, k * NW + f * 128:k * NW + (f + 1) * 128],
                                 lambda k: hT[:, k * 512:(k + 1) * 512], 8, [nt.b_hT, b_w[0]])
                        evac_copy(qst[:, f * 512:(f + 1) * 512], pb, 512, b_qst)
                    pg.dma("pool", ch_sc[0], qT1[:, :, lb * 512:(lb + 1) * 512].rearrange("h p n -> p h n"),
                           qst.rearrange("p (h n) -> p h n", h=8), reads=[b_qst])
                if nxt is not None:
                    nt.tr(nxt, (0, 1))
            pg.barrier()

        def band_phase():
            al.reset()
            wo = al.b16(8 * D)
            b_wo = Buf("woc")
            load_weight_cast(wo, W["c_w_out"], 8, D, [(b_wo, 0, D)], [ch_w[0]])
            relb = al.f32(16 * 640); b_relb = Buf("relb")
            for g in range(4):
                pg.dma("sp", ch_c, relb[:, g * 2560:(g + 1) * 2560], W["relb"][:, g * 2560:(g + 1) * 2560], writes=[b_relb])
            relb3 = relb.rearrange("p (h m) -> p h m", h=16)
            pg.op("pool", lambda e: e.memset(relb3[0:64, :, 576:640], -30000.0), reads=[b_relb], writes=[b_relb])
            pg.op("pool", lambda e: e.memset(relb3[64:128, :, 0:64], -30000.0), reads=[b_relb], writes=[b_relb])
            b_c = Buf("cst8")
            maskb = cst_load("maskb", b_c, 16)
            qTb = al.b16(8 * 512); b_q = Buf("qb")
            kTb = al.b16(8 * 1024); b_k = Buf("kb")
            Vb = al.b16(8 * 2 * D); b_v = Buf("vb")
            oT = al.b16(8 * 512); b_oT = Buf("oTc")
            NS = 6
            DEPTH = 4
            sbt = [al.f32(1024) for _ in range(NS)]; b_sb = [Buf("sb%d" % i) for i in range(NS)]
            Pt = [al.b16(1024) for _ in range(NS)]; b_P = [Buf("Pc%d" % i) for i in range(NS)]
            rec = [al.f32(1024) for _ in range(2)]; b_rec = [Buf("recc%d" % i) for i in range(2)]
            xres2 = [al.f32(1024) for _ in range(2)]; b_xres2 = [Buf("xres%d" % i) for i in range(2)]
            xres = xres2 + xres2; b_xres = b_xres2 + b_xres2
            ucnt = [0]
            KT_ORDER = (3, 4, 2, 5, 1, 6, 0, 7)
            blocks = list(range(8, 16))

            def load_blk(bi):
                blk = blocks[bi]
                li = blk - 8
                pg.dma("sp", ch_m[0], qTb.rearrange("p (f n) -> p f n", f=8),
                       qT1[:, :, (li + 1) * 512:(li + 2) * 512].rearrange("f p n -> p f n"), writes=[b_q])
                pg.dma("sp", ch_m[1], kTb.rearrange("p (f n) -> p f n", f=8),
                       kT1[:, :, li * 512:li * 512 + 1024].rearrange("f p n -> p f n"), writes=[b_k])
                pg.dma("sp", ch_m[2], Vb.rearrange("p (k c) -> p k c", k=8),
                       v1[li * 512:li * 512 + 1024, :].rearrange("(k p) c -> p k c", p=128), writes=[b_v])

            def geom(kt):
                cq_lo = max(0, 2 * kt - 8)
                cq_hi = min(7, 2 * kt + 1)
                n0 = 64 * cq_lo
                n1 = 64 * (cq_hi + 1)
                return n0, n1, n1 - n0, n0 + 512 - 128 * kt

            load_blk(0)
            for bi, blk in enumerate(blocks):
                for t in range(2):
                    pg.dma("sp", ch_r[t % 2], xres[t], xs_src(blk, t), reads=[XS[blk]], writes=[b_xres[t]])
                units = [(f, oi, kt) for f in range(8) for oi, kt in enumerate(KT_ORDER)]
                base = ucnt[0]

                def stage_a(u):
                    f, oi, kt = units[u]
                    g = base + u
                    n0, n1, N, m0 = geom(kt)
                    sb0 = (0, 2)[g % 2]
                    sl = g % NS

                    def fqk(e, kt=kt, n0=n0, n1=n1, N=N, sb0=sb0, f=f):
                        e.matmul(bank(sb0, N), lhsT=kTb[0:64, f * 1024 + kt * 128:f * 1024 + (kt + 1) * 128],
                                 rhs=qTb[0:64, f * 512 + n0:f * 512 + n1], start=True, stop=True)
                        return e.matmul(bank(sb0 + 1, N), lhsT=kTb[64:128, f * 1024 + kt * 128:f * 1024 + (kt + 1) * 128],
                                        rhs=qTb[64:128, f * 512 + n0:f * 512 + n1], start=True, stop=True)

                    pg.op("pe", fqk, reads=[b_k, b_q], writes=[PB[sb0], PB[sb0 + 1]])
                    sb3 = sbt[sl].rearrange("p (m n) -> p m n", m=2)[:, :, 0:N]
                    ps3 = psum[:, sb0 * 512:sb0 * 512 + 1024].rearrange("p (m n) -> p m n", m=2)[:, :, 0:N]
                    r3 = relb[:, 2 * f * 640:(2 * f + 2) * 640].rearrange("p (m n) -> p m n", m=2)[:, :, m0:m0 + N]
                    pg.op("dve", lambda e, sb3=sb3, ps3=ps3, r3=r3: e.scalar_tensor_tensor(
                        out=sb3, in0=ps3, scalar=0.125, in1=r3, op0=ALU.mult, op1=ALU.add),
                        reads=[PB[sb0], PB[sb0 + 1], b_relb], writes=[b_sb[sl]])
                    P3 = Pt[sl].rearrange("p (m n) -> p m n", m=2)[:, :, 0:N]
                    bias = maskb[:, 0:1] if (blk == 8 and kt < 4) else 0.0
                    pg.op("act", lambda e, P3=P3, sb3=sb3, bias=bias: e.activation(out=P3, in_=sb3, func=AF.Exp, bias=bias),
                          reads=[b_sb[sl], b_c], writes=[b_P[sl]])

                def stage_b(u):
                    f, oi, kt = units[u]
                    g = base + u
                    n0, n1, N, m0 = geom(kt)
                    sl = g % NS
                    P = Pt[sl]
                    fs = f % 2
                    pb0 = 4 + 2 * fs
                    first = (oi == 0)
                    last = (oi == 7)

                    def fpv(e, kt=kt, f=f, P=P, N=N, n0=n0, first=first, last=last, pb0=pb0):
                        for par in range(2):
                            h = 2 * f + par
                            ins = e.matmul(bank(pb0 + par, N, n0), lhsT=Vb[:, kt * 2 * D + h * 128:kt * 2 * D + (h + 1) * 128],
                                           rhs=P[:, par * 512:par * 512 + N], start=first, stop=last)
                        return ins

                    pg.op("pe", fpv, reads=[b_P[sl], b_v], writes=[PB[pb0], PB[pb0 + 1]])
                    if last:
                        rc = rec[fs]
                        pg.op("act", lambda e, rc=rc, pb0=pb0: e.activation(out=rc[0:64, :], in_=psum[64:128, pb0 * 512:pb0 * 512 + 1024],
                                                                              func=AF.Ln),
                              reads=[PB[pb0], PB[pb0 + 1]], writes=[b_rec[fs]])
                        pg.op("act", lambda e, rc=rc: e.activation(out=rc[0:64, :], in_=rc[0:64, :], func=AF.Exp, scale=-1.0),
                              reads=[b_rec[fs]], writes=[b_rec[fs]])
                        def fin(f=f, rc=rc, pb0=pb0, fs=fs):
                            for par in range(2):
                                dst = oT[64 * par:64 * par + 64, f * 512:(f + 1) * 512]
                                pg.op("dve", lambda e, dst=dst, rc=rc, pb0=pb0, par=par: e.tensor_tensor(
                                    out=dst, in0=psum[0:64, (pb0 + par) * 512:(pb0 + par + 1) * 512],
                                    in1=rc[0:64, par * 512:(par + 1) * 512], op=ALU.mult),
                                    reads=[PB[pb0 + par], b_rec[fs]], writes=[b_oT])
                        deferred.append([3, fin])

                nu = len(units)
                deferred = []
                for step in range(nu + DEPTH):
                    if step < nu:
                        stage_a(step)
                    for d in deferred:
                        d[0] -= 1
                    while deferred and deferred[0][0] <= 0:
                        deferred.pop(0)[1]()
                    if step >= DEPTH:
                        stage_b(step - DEPTH)
                while deferred:
                    deferred.pop(0)[1]()
                ucnt[0] += nu
                if bi + 1 < len(blocks):
                    load_blk(bi + 1)
                for t in range(4):
                    xr, bxr = xres[t], b_xres[t]
                    if t >= 2:
                        pg.dma("sp", ch_r[t % 2], xr, xs_src(blk, t), reads=[XS[blk]], writes=[bxr])
                    for nh in range(2):
                        pb = nh

                        def fo(e, t=t, nh=nh, pb=pb):
                            for ff in range(8):
                                ins = e.matmul(bank(pb), lhsT=oT[:, ff * 512 + t * 128:ff * 512 + (t + 1) * 128],
                                               rhs=wo[:, ff * D + nh * 512:ff * D + (nh + 1) * 512],
                                               start=(ff == 0), stop=(ff == 7))
                            return ins

                        pg.op("pe", fo, reads=[b_oT, b_wo], writes=[PB[pb]])
                        xsl = xr[:, nh * 512:(nh + 1) * 512]
                        pg.op("dve", lambda e, xsl=xsl, pb=pb: e.tensor_tensor(out=xsl, in0=bank(pb), in1=xsl, op=ALU.add),
                              reads=[PB[pb], bxr], writes=[bxr])
                    r0 = blk * 512 + t * 128
                    pg.dma("pool", ch_st[t % 2], xs[r0:r0 + 128, :], xr, reads=[bxr], writes=[XS[blk]])
            pg.barrier()

        if 1 in phases:
            ffn_phase(0, 1, list(range(16)), xe)
        if 2 in phases:
            proj_ab_phase()
        if 3 in phases:
            attn_ab_phase()
        if 4 in phases:
            xattn_phase(0, list(range(7, 16)))
        if 5 in phases:
            ffn_phase(0, 2, list(range(7, 16)), xs)
        if 6 in phases:
            ffn_phase(1, 1, list(range(7, 16)), xs)
        if 7 in phases:
            proj_c_phase()
        if 8 in phases:
            band_phase()
        if 9 in phases:
            xattn_phase(1, list(range(8, 16)))
        if 10 in phases:
            ffn_phase(1, 2, list(range(8, 16)), xs, final=True)

        pg.final_wait("sp")
        pg.emit()
    return nc


def _rep(v, n=128):
    return np.ascontiguousarray(np.broadcast_to(np.asarray(v, np.float32).reshape(1, -1), (n, v.size)))


def make_in_maps(inputs):
    x = np.asarray(inputs["x"], np.float32)
    mem = np.asarray(inputs["mem"], np.float32)
    g = lambda k: np.asarray(inputs[k], np.float32)
    shared = {}
    for l in range(2):
        for nm in ["ffn1_wg", "ffn1_wu", "ffn1_wd", "xa_wq", "xa_wkv", "xa_wo", "ffn2_wg", "ffn2_wu", "ffn2_wd"]:
            shared["%s%d" % (nm, l)] = np.ascontiguousarray(g(nm)[l])
    shared["ab_w_in"] = np.ascontiguousarray(g("ab_w_in")[0])
    shared["ab_w_out"] = np.ascontiguousarray(g("ab_w_out")[0])
    sgw = g("ab_sg_w")[0]
    shared["sgwT"] = np.ascontiguousarray(sgw.transpose(2, 0, 1).reshape(128, 512))
    shared["c_w_in"] = np.ascontiguousarray(g("c_w_in")[0])
    shared["c_w_out"] = np.ascontiguousarray(g("c_w_out")[0])
    rb = g("c_rel_bias")[0]
    j = np.arange(128)[:, None]
    m = np.arange(640)[None, :]
    idx = np.clip(m - j, -256, 256) + 256
    shared["relb"] = np.ascontiguousarray(rb[:, idx].transpose(1, 0, 2).reshape(128, 16 * 640))

    def cst_for(maskval):
        c = np.zeros((128, CST_COLS), np.float32)

        def put(name, arr):
            o, w = CST[name]
            c[:, o:o + w] = arr

        for l in range(2):
            put("g_ffn1_%d" % l, _rep(g("ffn1_norm")[l]))
            put("g_mix_%d" % l, _rep(g("mix_norm")[l]))
            put("g_xa_%d" % l, _rep(g("xa_norm")[l]))
            put("g_mem_%d" % l, _rep(g("xa_mem_norm")[l]))
            put("g_ffn2_%d" % l, _rep(g("ffn2_norm")[l]))
        put("g_final", _rep(g("final_norm")))
        put("lam", _rep(g("ab_lam")[0].reshape(-1)))
        put("subln", np.repeat(g("ab_subln")[0].reshape(128, 1), 16, axis=1))
        put("sgnorm", _rep(g("ab_sg_norm")[0]))
        put("sgb", _rep(g("ab_sg_b")[0].reshape(-1)))
        put("maskb", np.full((128, 16), maskval, np.float32))
        return c

    cstA = cst_for(NEG)
    cstB = cst_for(0.0)
    in_maps = []
    for b in range(4):
        for h in range(2):
            if h == 0:
                xe = np.concatenate([np.zeros((4096, D), np.float32), x[b, :4096]], axis=0)
            else:
                xe = np.ascontiguousarray(x[b])
            d = dict(shared)
            d["xe"] = xe
            d["mem"] = np.ascontiguousarray(mem[b])
            d["cst"] = cstA if h == 0 else cstB
            in_maps.append(d)
    return in_maps


_NC_CACHE = {}


def kernel(**inputs):
    in_maps = make_in_maps(inputs)
    if "nc" not in _NC_CACHE:
        _NC_CACHE["nc"] = build_program()
    nc = _NC_CACHE["nc"]
    res = run_bass_kernel_spmd(nc, in_maps, core_ids=list(range(8)))
    outp = np.empty((4, 8192, D), np.float32)
    for b in range(4):
        for h in range(2):
            outp[b, h * 4096:(h + 1) * 4096] = res.results[b * 2 + h]["out"]
    return outp
```

```python
import numpy as np
from contextlib import ExitStack
import concourse.bass as bass
import concourse.mybir as mybir
from concourse.bass_utils import run_bass_kernel_spmd

F32 = mybir.dt.float32
BF16 = mybir.dt.bfloat16
AF = mybir.ActivationFunctionType
ALU = mybir.AluOpType
AX = mybir.AxisListType

D = 1024
DFF = 2816
NFF = DFF // 128
S_E = 8192
EPS = 1e-6
SEM_CAP = 50000
NEG = -80.0


class Buf:
    __slots__ = ("name", "w", "r")

    def __init__(self, name=""):
        self.name = name
        self.w = None
        self.r = []


class Chan:
    def __init__(self, sem):
        self.sem = sem
        self.issued = 0


class Op:
    __slots__ = ("eng", "fn", "deps", "needed", "count", "chan")

    def __init__(self, eng, fn, deps, chan=None):
        self.eng = eng
        self.fn = fn
        self.deps = deps
        self.needed = False
        self.count = 0
        self.chan = chan


ENGS = ("pe", "act", "dve", "pool", "sp")


class Prog:
    def __init__(self, nc, es):
        self.nc = nc
        self.es = es
        self.ops = {e: [] for e in ENGS}
        self.extra = {e: [] for e in ENGS}
        self.chans = []
        self.sems = {e: [] for e in ENGS}

    def chan(self, name):
        c = Chan(self.es.enter_context(self.nc.semaphore(name)))
        self.chans.append(c)
        return c

    def _deps(self, eng, reads, writes, strict=False):
        deps = []
        for b in reads:
            if b.w is not None:
                deps.append(b.w)
        for b in writes:
            if b.w is not None:
                deps.append(b.w)
            deps.extend(b.r)
        out = []
        for d in deps:
            if d[0] == "op":
                if d[1].eng == eng and not strict:
                    continue
                d[1].needed = True
                out.append(d)
            else:
                out.append(("dma", d[1], d[1].issued))
        out.extend(self.extra[eng])
        self.extra[eng] = []
        return out

    def op(self, eng, fn, reads=(), writes=(), strict=False):
        o = Op(eng, fn, self._deps(eng, reads, writes, strict))
        self.ops[eng].append(o)
        ref = ("op", o)
        for b in reads:
            b.r.append(ref)
        for b in writes:
            b.w = ref
            b.r = []
        return o

    def dma(self, eng, chan, out, in_, reads=(), writes=()):
        o = Op(eng, lambda e: e.dma_start(out=out, in_=in_), self._deps(eng, reads, writes), chan=chan)
        chan.issued += 16
        self.ops[eng].append(o)
        ref = ("dma", chan)
        for b in reads:
            b.r.append(ref)
        for b in writes:
            b.w = ref
            b.r = []
        return o

    def barrier(self):
        for e in ENGS:
            for e2 in ENGS:
                if e2 != e and self.ops[e2]:
                    last = self.ops[e2][-1]
                    if last.chan is None:
                        last.needed = True
                        self.extra[e].append(("op", last))
                    else:
                        for o in reversed(self.ops[e2]):
                            if o.chan is None:
                                o.needed = True
                                self.extra[e].append(("op", o))
                                break
            for c in self.chans:
                if c.issued:
                    self.extra[e].append(("dma", c, c.issued))

    def final_wait(self, eng="sp"):
        deps = [("dma", c, c.issued) for c in self.chans if c.issued]
        o = Op(eng, None, deps)
        self.ops[eng].append(o)

    def emit(self):
        nc = self.nc
        for e in ENGS:
            c = 0
            for o in self.ops[e]:
                if o.chan is None and o.needed:
                    c += 1
                    o.count = c
            nsem = (c + SEM_CAP - 1) // SEM_CAP
            for i in range(max(nsem, 1)):
                self.sems[e].append(self.es.enter_context(nc.semaphore("s_%s%d" % (e, i))))
        block = self.es.enter_context(nc.Block())

        def replay(e, engobj):
            known_op = {}
            known_dma = {}
            for o in self.ops[e]:
                for d in o.deps:
                    if d[0] == "op":
                        t = d[1]
                        key = ((t.count - 1) // SEM_CAP, (t.count - 1) % SEM_CAP + 1)
                        if known_op.get(t.eng, (-1, 0)) >= key:
                            continue
                        known_op[t.eng] = key
                        engobj.wait_ge(self.sems[t.eng][key[0]], key[1])
                    else:
                        ch, val = d[1], d[2]
                        if known_dma.get(id(ch), 0) >= val:
                            continue
                        known_dma[id(ch)] = val
                        engobj.wait_ge(ch.sem, val)
                if o.fn is None:
                    continue
                ins = o.fn(engobj)
                if o.chan is not None:
                    ins.then_inc(o.chan.sem, 16)
                elif o.needed:
                    ins.then_inc(self.sems[e][(o.count - 1) // SEM_CAP], 1)

        @block.tensor
        def _(x):
            replay("pe", x)

        @block.scalar
        def _(x):
            replay("act", x)

        @block.vector
        def _(x):
            replay("dve", x)

        @block.gpsimd
        def _(x):
            replay("pool", x)

        @block.sync
        def _(x):
            replay("sp", x)


class Alloc:
    def __init__(self, arena, ncol):
        self.arena = arena
        self.n = ncol
        self.off = 0

    def reset(self):
        self.off = 0

    def f32(self, n):
        ap = self.arena[:, self.off:self.off + n]
        self.off += n
        assert self.off <= self.n, ("sbuf overflow", self.off, self.n)
        return ap

    def b16(self, n):
        assert n % 2 == 0
        ap = self.arena[:, self.off:self.off + n // 2].bitcast(BF16)
        self.off += n // 2
        assert self.off <= self.n, ("sbuf overflow", self.off, self.n)
        return ap


ARENA_COLS = 52800

CST = {}
_off = 0
for _n, _w in [("g_ffn1_0", 1024), ("g_mix_0", 1024), ("g_xa_0", 1024), ("g_mem_0", 1024), ("g_ffn2_0", 1024),
               ("g_ffn1_1", 1024), ("g_mix_1", 1024), ("g_xa_1", 1024), ("g_mem_1", 1024), ("g_ffn2_1", 1024),
               ("g_final", 1024), ("lam", 256), ("subln", 16), ("sgnorm", 512), ("sgb", 512), ("maskb", 16)]:
    CST[_n] = (_off, _w)
    _off += _w
CST_COLS = _off


def build_program(phases=tuple(range(1, 11)), debug=False):
    nc = bass.Bass("TRN2", target_bir_lowering=False)
    es = ExitStack()
    with es:
        pg = Prog(nc, es)
        kin = "ExternalInput"
        kscr = "ExternalOutput" if debug else "Internal"

        def din(name, shape, dt=F32):
            return nc.dram_tensor(name, list(shape), dt, kind=kin).ap()

        xe = din("xe", [S_E, D])
        mem = din("mem", [256, D])
        cst = din("cst", [128, CST_COLS])
        W = {}
        for l in range(2):
            for nm, shp in [("ffn1_wg", [D, DFF]), ("ffn1_wu", [D, DFF]), ("ffn1_wd", [DFF, D]),
                            ("xa_wq", [D, D]), ("xa_wkv", [D, 2 * D]), ("xa_wo", [D, D]),
                            ("ffn2_wg", [D, DFF]), ("ffn2_wu", [D, DFF]), ("ffn2_wd", [DFF, D])]:
                W["%s%d" % (nm, l)] = din("%s%d" % (nm, l), shp)
        W["ab_w_in"] = din("ab_w_in", [D, 2560])
        W["ab_w_out"] = din("ab_w_out", [D, D])
        W["sgwT"] = din("sgwT", [128, 512])
        W["c_w_in"] = din("c_w_in", [D, 3072])
        W["c_w_out"] = din("c_w_out", [D, D])
        W["relb"] = din("relb", [128, 16 * 640])

        out = nc.dram_tensor("out", [4096, D], F32, kind="ExternalOutput").ap()
        xs = nc.dram_tensor("xs", [S_E, D], F32, kind=kscr).ap()
        qT0 = nc.dram_tensor("qT0", [4, 128, 4608], BF16, kind=kscr).ap()
        kT0 = nc.dram_tensor("kT0", [4, 128, S_E], BF16, kind=kscr).ap()
        v0 = nc.dram_tensor("v0", [S_E, 512], BF16, kind=kscr).ap()
        obT = nc.dram_tensor("obT", [4, 128, 4608], BF16, kind=kscr).ap()
        qT1 = nc.dram_tensor("qT1", [8, 128, 4608], BF16, kind=kscr).ap()
        kT1 = nc.dram_tensor("kT1", [8, 128, 4608], BF16, kind=kscr).ap()
        v1 = nc.dram_tensor("v1", [4608, 2 * D], BF16, kind=kscr).ap()

        DBG = {}
        if debug:
            DBG["hT"] = nc.dram_tensor("dbg_hT", [128, 4096], BF16, kind="ExternalOutput").ap()
            DBG["hid"] = nc.dram_tensor("dbg_hid", [128, NFF * 512], BF16, kind="ExternalOutput").ap()
            DBG["wg"] = nc.dram_tensor("dbg_wg", [128, 8 * DFF], BF16, kind="ExternalOutput").ap()
            DBG["xsb"] = nc.dram_tensor("dbg_xsb", [128, 1024], BF16, kind="ExternalOutput").ap()
            DBG["rs"] = nc.dram_tensor("dbg_rs", [128, 64], F32, kind="ExternalOutput").ap()
            DBG["cat"] = nc.dram_tensor("dbg_cat", [2, 128, 2048], BF16, kind="ExternalOutput").ap()
            DBG["oa"] = nc.dram_tensor("dbg_oa", [2, 128, 512], F32, kind="ExternalOutput").ap()
            DBG["P"] = nc.dram_tensor("dbg_P", [4, 128, 1024], BF16, kind="ExternalOutput").ap()
            DBG["rec"] = nc.dram_tensor("dbg_rec", [128, 1024], F32, kind="ExternalOutput").ap()
            DBG["t1"] = nc.dram_tensor("dbg_t1", [128, 512], F32, kind="ExternalOutput").ap()
            DBG["nl"] = nc.dram_tensor("dbg_nl", [128, 16], F32, kind="ExternalOutput").ap()
            DBG["ssum"] = nc.dram_tensor("dbg_ssum", [128, 32], F32, kind="ExternalOutput").ap()
            DBG["ee"] = nc.dram_tensor("dbg_ee", [128, 32], F32, kind="ExternalOutput").ap()
            DBG["lamt"] = nc.dram_tensor("dbg_lamt", [128, 256], F32, kind="ExternalOutput").ap()
        arena = es.enter_context(nc.sbuf_tensor("arena", [128, ARENA_COLS], F32))
        psum = es.enter_context(nc.psum_tensor("psum", [128, 4096], F32))
        al = Alloc(arena, ARENA_COLS)
        PB = [Buf("ps%d" % i) for i in range(8)]

        def bank(i, n=512, off=0):
            return psum[:, i * 512 + off:i * 512 + off + n]

        def bank16(i):
            return psum[:, i * 512:(i + 1) * 512].bitcast(BF16)

        XS = [Buf("xs_blk%d" % i) for i in range(16)]

        ch_x = [pg.chan("ch_x%d" % i) for i in range(2)]
        ch_r = [pg.chan("ch_r%d" % i) for i in range(2)]
        ch_st = [pg.chan("ch_st%d" % i) for i in range(2)]
        ch_w = [pg.chan("ch_w%d" % i) for i in range(4)]
        ch_c = pg.chan("ch_c")
        ch_m = [pg.chan("ch_m%d" % i) for i in range(6)]
        ch_dbg = pg.chan("ch_dbg")
        ch_sc = [pg.chan("ch_sc%d" % i) for i in range(2)]
        ch_wb = [pg.chan("ch_wb%d" % i) for i in range(12)]

        class NormT:
            def __init__(self, gname, src_fn, src_buf_fn, width=512, nslots=2):
                self.width = width
                self.nslots = nslots
                self.src_fn = src_fn
                self.src_buf_fn = src_buf_fn
                self.gam = al.f32(1024)
                self.b_gam = Buf("gam")
                o, w = CST[gname]
                pg.dma("sp", ch_c, self.gam, cst[:, o:o + w], writes=[self.b_gam])
                self.xin = [al.f32(1024) for _ in range(nslots)]
                self.b_xin = [Buf("xin%d" % i) for i in range(nslots)]
                self.xsb = [al.b16(1024) for _ in range(4)]
                self.b_xsb = [Buf("xsb%d" % i) for i in range(4)]
                self.junk = al.f32(1024)
                self.b_junk = Buf("junk")
                self.ss = al.f32(64)
                self.rs = al.f32(64)
                self.b_ss = [Buf("ss%d" % i) for i in range(4)]
                self.b_rs = [Buf("rs%d" % i) for i in range(4)]
                self.hT = al.b16(8 * width)
                self.b_hT = Buf("hT")
                self.ident = al.b16(128)
                self.b_ident = Buf("ident")
                ident = self.ident
                pg.op("pool", lambda e: e.memset(ident, 0.0), writes=[self.b_ident])
                pg.op("pool", lambda e: e.affine_select(out=ident, in_=ident, pattern=[[-1, 128]],
                                                          compare_op=ALU.not_equal, fill=1.0, base=0,
                                                          channel_multiplier=1), reads=[self.b_ident], writes=[self.b_ident],
                      strict=True)
                self.cnt = 0

            def norm(self, blk, nsub=4):
                if self.nslots >= 4:
                    self.load(blk, nsub)
                    self.compute(blk, nsub)
                    return
                for t in range(nsub):
                    self.norm1(blk, t)

            def load(self, blk, nsub=4):
                sb = self.src_buf_fn(blk)
                for t in range(nsub):
                    pg.dma("sp", ch_x[t % 2], self.xin[t], self.src_fn(blk, t), reads=[sb] if sb else [], writes=[self.b_xin[t]])

            def compute(self, blk, nsub=4, defer_scale=False):
                ssall = self.ss[:, 0:64]
                rsall = self.rs[:, 0:64]
                junk = self.junk
                pg.op("dve", lambda e: e.memset(ssall, 0.0), writes=self.b_ss)
                for t in range(nsub):
                    xin, bx = self.xin[t], self.b_xin[t]
                    ss = self.ss[:, 16 * t:16 * t + 1]
                    pg.op("act", lambda e, xin=xin, ss=ss: e.activation(out=junk, in_=xin, func=AF.Square, accum_out=ss),
                          reads=[bx], writes=[self.b_junk, self.b_ss[t]])
                pg.op("act", lambda e: e.activation(out=rsall, in_=ssall, func=AF.Sqrt, scale=1.0 / 1024, bias=EPS),
                      reads=self.b_ss, writes=self.b_rs, strict=True)
                pg.op("dve", lambda e: e.reciprocal(out=rsall, in_=rsall), reads=self.b_rs, writes=self.b_rs)
                gam = self.gam
                scales = []
                for t in range(nsub):
                    def sc(t=t):
                        xin, bx = self.xin[t], self.b_xin[t]
                        rs = self.rs[:, 16 * t:16 * t + 1]
                        xsb = self.xsb[t]
                        pg.op("dve", lambda e, xsb=xsb, xin=xin, rs=rs: e.scalar_tensor_tensor(
                            out=xsb, in0=xin, scalar=rs, in1=gam, op0=ALU.mult, op1=ALU.mult),
                            reads=[bx, self.b_rs[t], self.b_gam], writes=[self.b_xsb[t]], strict=True)
                    scales.append(sc)
                if defer_scale:
                    return scales
                for sc in scales:
                    sc()

            def norm1(self, blk, t):
                if True:
                    k = self.cnt % 2
                    self.cnt += 1
                    xin, bx = self.xin[k], self.b_xin[k]
                    sb = self.src_buf_fn(blk)
                    pg.dma("sp", ch_x[k], xin, self.src_fn(blk, t), reads=[sb] if sb else [], writes=[bx])
                    ss = self.ss[:, 16 * t:16 * t + 1]
                    rs = self.rs[:, 16 * t:16 * t + 1]
                    junk = self.junk
                    pg.op("dve", lambda e, ss=ss: e.memset(ss, 0.0), writes=[self.b_ss[t]])
                    pg.op("act", lambda e, xin=xin, ss=ss, junk=junk: e.activation(
                        out=junk, in_=xin, func=AF.Square, accum_out=ss),
                        reads=[bx], writes=[self.b_junk, self.b_ss[t]])
                    pg.op("act", lambda e, ss=ss, rs=rs: e.activation(
                        out=rs, in_=ss, func=AF.Sqrt, scale=1.0 / 1024, bias=EPS),
                        reads=[self.b_ss[t]], writes=[self.b_rs[t]], strict=True)
                    pg.op("dve", lambda e, rs=rs: e.reciprocal(out=rs, in_=rs),
                          reads=[self.b_rs[t]], writes=[self.b_rs[t]])
                    xsb = self.xsb[t]
                    gam = self.gam
                    pg.op("dve", lambda e, xsb=xsb, xin=xin, rs=rs, gam=gam: e.scalar_tensor_tensor(
                        out=xsb, in0=xin, scalar=rs, in1=gam, op0=ALU.mult, op1=ALU.mult),
                        reads=[bx, self.b_rs[t], self.b_gam], writes=[self.b_xsb[t]], strict=True)

            def tr(self, blk, pbanks, nsub=4):
                hT = self.hT
                for t in range(nsub):
                    pb = pbanks[t % 2]
                    p16 = bank16(pb)
                    xsb = self.xsb[t]
                    ident = self.ident

                    def f(e, p16=p16, xsb=xsb, ident=ident):
                        for c in range(8):
                            ins = e.transpose(p16[:, c * 128:(c + 1) * 128], xsb[:, c * 128:(c + 1) * 128], ident)
                        return ins

                    pg.op("pe", f, reads=[self.b_xsb[t], self.b_ident], writes=[PB[pb]])
                    dst = hT.rearrange("p (c n) -> p c n", c=8)[:, :, t * 128:(t + 1) * 128]
                    src = p16.rearrange("p (c n) -> p c n", c=8)
                    if t % 2 == 0:
                        pg.op("act", lambda e, dst=dst, src=src: e.activation(out=dst, in_=src, func=AF.Copy),
                              reads=[PB[pb]], writes=[self.b_hT])
                    else:
                        pg.op("dve", lambda e, dst=dst, src=src: e.tensor_copy(dst, src),
                              reads=[PB[pb]], writes=[self.b_hT])

        def load_weight_cast(dst16, src, kchunks, ncols, bufs_cols, chans):
            dv = dst16.rearrange("p (k n) -> p k n", k=kchunks)
            sv = src.rearrange("(k p) n -> p k n", p=128)
            for i, (b, c0, c1) in enumerate(bufs_cols):
                pg.dma("pool", chans[i % len(chans)], dv[:, :, c0:c1], sv[:, :, c0:c1], writes=[b])

        def ffn_phase(layer, which, blocks, src, dst_is_out=False, final=False):
            al.reset()
            wg = W["ffn%d_wg%d" % (which, layer)]
            wu = W["ffn%d_wu%d" % (which, layer)]
            wd = W["ffn%d_wd%d" % (which, layer)]
            wg16 = al.b16(8 * DFF)
            wu16 = al.b16(8 * DFF)
            wd16 = al.b16(NFF * D)
            bounds = [0, 128, 704, 1408, 2112, DFF]
            NCB = len(bounds) - 1
            b_wg = [Buf("wg%d" % i) for i in range(NCB)]
            b_wu = [Buf("wu%d" % i) for i in range(NCB)]
            b_wd = [Buf("wd%d" % i) for i in range(2)]
            for i in range(NCB):
                load_weight_cast(wg16, wg, 8, DFF, [(b_wg[i], bounds[i], bounds[i + 1])], [ch_wb[i]])
                load_weight_cast(wu16, wu, 8, DFF, [(b_wu[i], bounds[i], bounds[i + 1])], [ch_wb[5 + i]])

            def wblocks(m):
                return [i for i in range(NCB) if bounds[i] < (m + 1) * 128 and bounds[i + 1] > m * 128]
            for i in range(2):
                load_weight_cast(wd16, wd, NFF, D, [(b_wd[i], i * 512, (i + 1) * 512)], [ch_wb[10 + i]])

            def src_fn(blk, t):
                r0 = blk * 512 + t * 128
                return src[r0:r0 + 128, :]

            def src_buf(blk):
                return XS[blk] if src is xs else None

            nt = NormT("g_ffn%d_%d" % (which, layer), src_fn, src_buf)
            hid = al.b16(NFF * 512)
            b_hid = Buf("hid")
            sg = [al.f32(512) for _ in range(2)]
            b_sg = [Buf("sg%d" % i) for i in range(2)]
            xres = [al.f32(1024) for _ in range(2)]
            b_xres = [Buf("xres%d" % i) for i in range(2)]
            if final:
                gfin = al.f32(1024)
                b_gfin = Buf("gfin")
                o, w = CST["g_final"]
                pg.dma("sp", ch_c, gfin, cst[:, o:o + w], writes=[b_gfin])
                fss = al.f32(32)
                frs = al.f32(32)
                b_fss = [Buf("fss%d" % i) for i in range(2)]
                b_frs = [Buf("frs%d" % i) for i in range(2)]
                fjunk = nt.junk
                b_fjunk = nt.b_junk
            rescnt = [0]

            def gate_up(blk, nxt=None):
                hT = nt.hT
                for m in range(NFF):
                    if nxt is not None and m in (2, 7, 12, 17):
                        nt.norm1(nxt, (2, 7, 12, 17).index(m))
                    s = m % 2
                    pg_, pu_ = 2 + s, 4 + s
                    wbs = wblocks(m)

                    def fg(e, m=m, pb=pg_, w16=wg16):
                        for k in range(8):
                            ins = e.matmul(bank(pb), lhsT=w16[:, k * DFF + m * 128:k * DFF + (m + 1) * 128],
                                           rhs=hT[:, k * 512:(k + 1) * 512], start=(k == 0), stop=(k == 7))
                        return ins

                    pg.op("pe", fg, reads=[nt.b_hT] + [b_wg[i] for i in wbs], writes=[PB[pg_]])

                    def fu(e, m=m, pb=pu_, w16=wu16):
                        for k in range(8):
                            ins = e.matmul(bank(pb), lhsT=w16[:, k * DFF + m * 128:k * DFF + (m + 1) * 128],
                                           rhs=hT[:, k * 512:(k + 1) * 512], start=(k == 0), stop=(k == 7))
                        return ins

                    pg.op("pe", fu, reads=[nt.b_hT] + [b_wu[i] for i in wbs], writes=[PB[pu_]])
                    sgt = sg[s]
                    pg.op("act", lambda e, sgt=sgt, pb=pg_: e.activation(out=sgt, in_=bank(pb), func=AF.Silu),
                          reads=[PB[pg_]], writes=[b_sg[s]])
                    hslice = hid[:, m * 512:(m + 1) * 512]
                    pg.op("dve", lambda e, hslice=hslice, sgt=sgt, pb=pu_: e.tensor_tensor(
                        out=hslice, in0=sgt, in1=bank(pb), op=ALU.mult),
                        reads=[b_sg[s], PB[pu_]], writes=[b_hid])

            def down(blk):
                for t in range(4):
                    k = rescnt[0] % 2
                    rescnt[0] += 1
                    xr, bxr = xres[k], b_xres[k]
                    sb = src_buf(blk)
                    pg.dma("sp", ch_r[k], xr, src_fn(blk, t), reads=[sb] if sb else [], writes=[bxr])
                    for nh in range(2):
                        pb = 6 + nh

                        def fd(e, t=t, nh=nh, pb=pb):
                            for kf in range(NFF):
                                ins = e.matmul(bank(pb), lhsT=hid[:, kf * 512 + t * 128:kf * 512 + (t + 1) * 128],
                                               rhs=wd16[:, kf * D + nh * 512:kf * D + (nh + 1) * 512],
                                               start=(kf == 0), stop=(kf == NFF - 1))
                            return ins

                        pg.op("pe", fd, reads=[b_hid, b_wd[nh]], writes=[PB[pb]])
                        xsl = xr[:, nh * 512:(nh + 1) * 512]
                        pg.op("dve", lambda e, xsl=xsl, pb=pb: e.scalar_tensor_tensor(
                            out=xsl, in0=bank(pb), scalar=0.5, in1=xsl, op0=ALU.mult, op1=ALU.add),
                            reads=[PB[pb], bxr], writes=[bxr])
                    r0 = blk * 512 + t * 128
                    if final:
                        ss = fss[:, 16 * k:16 * k + 1]
                        rs = frs[:, 16 * k:16 * k + 1]
                        pg.op("dve", lambda e, ss=ss: e.memset(ss, 0.0), writes=[b_fss[k]])
                        pg.op("act", lambda e, xr=xr, ss=ss: e.activation(out=fjunk, in_=xr, func=AF.Square,
                                                                            accum_out=ss),
                              reads=[bxr], writes=[b_fjunk, b_fss[k]])
                        pg.op("act", lambda e, ss=ss, rs=rs: e.activation(out=rs, in_=ss, func=AF.Sqrt,
                                                                            scale=1.0 / 1024, bias=EPS),
                              reads=[b_fss[k]], writes=[b_frs[k]], strict=True)
                        pg.op("dve", lambda e, rs=rs: e.reciprocal(out=rs, in_=rs), reads=[b_frs[k]],
                              writes=[b_frs[k]])
                        pg.op("dve", lambda e, xr=xr, rs=rs: e.scalar_tensor_tensor(
                            out=xr, in0=xr, scalar=rs, in1=gfin, op0=ALU.mult, op1=ALU.mult),
                            reads=[bxr, b_frs[k], b_gfin], writes=[bxr], strict=True)
                        ro = (blk - 8) * 512 + t * 128
                        pg.dma("pool", ch_st[k], out[ro:ro + 128, :], xr, reads=[bxr])
                    else:
                        pg.dma("pool", ch_st[k], xs[r0:r0 + 128, :], xr, reads=[bxr], writes=[XS[blk]])

            nt.norm(blocks[0])
            nt.tr(blocks[0], (0, 1))
            if debug and which == 1 and layer == 0:
                pg.dma("sp", ch_dbg, DBG["hT"], nt.hT, reads=[nt.b_hT])
                pg.dma("sp", ch_dbg, DBG["xsb"], nt.xsb[0], reads=[nt.b_xsb[0]])
                pg.dma("sp", ch_dbg, DBG["rs"], nt.rs, reads=nt.b_rs)
                pg.dma("sp", ch_dbg, DBG["wg"], wg16, reads=b_wg)
            for i, blk in enumerate(blocks):
                nxt = blocks[i + 1] if i + 1 < len(blocks) else None
                gate_up(blk, nxt)
                if debug and which == 1 and layer == 0 and i == 0:
                    pg.dma("sp", ch_dbg, DBG["hid"], hid, reads=[b_hid])
                if nxt is not None:
                    nt.tr(nxt, (0, 1))
                down(blk)
            pg.barrier()

        def mm_group(pb, n, lhs_fn, rhs_fn, nk, reads, off=0):
            def f(e):
                for k in range(nk):
                    ins = e.matmul(bank(pb, n, off), lhsT=lhs_fn(k), rhs=rhs_fn(k), start=(k == 0), stop=(k == nk - 1))
                return ins
            return pg.op("pe", f, reads=reads, writes=[PB[pb]])

        cp_cnt = [0]

        def evac_copy(dst, pb, n, dbuf, off=0, eng=None):
            if eng is None:
                eng = "act" if cp_cnt[0] % 2 == 0 else "dve"
                cp_cnt[0] += 1
            src = bank(pb, n, off)
            if eng == "act":
                pg.op("act", lambda e: e.activation(out=dst, in_=src, func=AF.Copy), reads=[PB[pb]], writes=[dbuf])
            else:
                pg.op("dve", lambda e: e.tensor_copy(dst, src), reads=[PB[pb]], writes=[dbuf])

        def xs_src(blk, t):
            r0 = blk * 512 + t * 128
            return xs[r0:r0 + 128, :]

        def cst_load(name, buf, ncol=None):
            o, w = CST[name]
            tl = al.f32(w if ncol is None else ncol)
            pg.dma("sp", ch_c, tl[:, 0:w], cst[:, o:o + w], writes=[buf])
            return tl

        def proj_ab_phase():
            al.reset()
            NW = 2560
            w16 = al.b16(8 * NW)
            b_w = [Buf("wab%d" % i) for i in range(5)]
            for ci in (1, 2, 0, 3, 4):
                load_weight_cast(w16, W["ab_w_in"], 8, NW, [(b_w[ci], ci * 512, (ci + 1) * 512)], [ch_w[ci % 4]])
            nt = NormT("g_mix_0", xs_src, lambda blk: XS[blk], nslots=4)
            b_c = Buf("cst2")
            sgn = cst_load("sgnorm", b_c)
            sgb = cst_load("sgb", b_c)
            sgw16 = al.b16(512)
            b_sgw = Buf("sgw")
            pg.dma("pool", ch_w[0], sgw16, W["sgwT"], writes=[b_sgw])
            for g in range(4):
                sl = sgw16[64:128, g * 128:g * 128 + 64]
                pg.op("pool", lambda e, sl=sl: e.memset(sl, 0.0), reads=[b_sgw], writes=[b_sgw])
            kst = al.b16(4 * 512); b_kst = Buf("kst")
            qst = al.b16(4 * 512); b_qst = Buf("qst")
            gu = al.b16(4 * 512); b_gu = Buf("gu")
            vst = al.b16(4 * 512); b_vst = Buf("vst")
            obst = al.b16(4 * 512); b_obst = Buf("obst")
            gv = [al.f32(512) for _ in range(4)]; b_gv = [Buf("gv%d" % i) for i in range(4)]
            vgn = [al.b16(512) for _ in range(4)]; b_vgn = [Buf("vgn%d" % i) for i in range(4)]
            tmpg = [al.f32(512) for _ in range(2)]; b_tmpg = [Buf("tmpg%d" % i) for i in range(2)]
            gss = al.f32(64); grs = al.f32(64)
            b_gss = [Buf("gss%d" % i) for i in range(4)]; b_grs = [Buf("grs%d" % i) for i in range(4)]
            gjunk = nt.junk; b_gjunk = nt.b_junk
            hT = nt.hT
            fm_rot = [0]

            def fm_chunk(colbase, f, wb):
                pb = 2 + (fm_rot[0] % 2)
                fm_rot[0] += 1
                mm_group(pb, 512, lambda k: w16[:, k * NW + colbase + f * 128:k * NW + colbase + (f + 1) * 128],
                         lambda k: hT[:, k * 512:(k + 1) * 512], 8, [nt.b_hT, wb])
                return pb

            tm_rot = [0]

            def tm_tile(colbase, t, wb):
                pb = 4 + (tm_rot[0] % 2)
                tm_rot[0] += 1
                mm_group(pb, 512, lambda k: hT[:, k * 512 + t * 128:k * 512 + (t + 1) * 128],
                         lambda k: w16[:, k * NW + colbase:k * NW + colbase + 512], 8, [nt.b_hT, wb])
                return pb

            blocks = list(range(16))
            nt.norm(blocks[0])
            nt.tr(blocks[0], (0, 1))
            for i, blk in enumerate(blocks):
                nxt = blocks[i + 1] if i + 1 < len(blocks) else None
                full = blk >= 7
                if nxt is not None:
                    nt.load(nxt)
                for f in range(4):
                    pb = fm_chunk(512, f, b_w[1])
                    evac_copy(kst[:, f * 512:(f + 1) * 512], pb, 512, b_kst)
                pg.dma("pool", ch_sc[0], kT0[:, :, blk * 512:(blk + 1) * 512].rearrange("h p n -> p h n"),
                       kst.rearrange("p (h n) -> p h n", h=4), reads=[b_kst])
                for t in range(4):
                    pb = tm_tile(1024, t, b_w[2])
                    evac_copy(vst[:, t * 512:(t + 1) * 512], pb, 512, b_vst)
                pg.dma("pool", ch_sc[1], v0[blk * 512:(blk + 1) * 512, :].rearrange("(t p) c -> p t c", p=128),
                       vst.rearrange("p (t c) -> p t c", t=4), reads=[b_vst])
                if full:
                    lb = blk - 7
                    pg.op("dve", lambda e: e.memset(gss, 0.0), writes=b_gss)
                    for t in range(4):
                        pb = tm_tile(2048, t, b_w[4])
                        gvt = gv[t]
                        pg.op("act", lambda e, gvt=gvt, pb=pb: e.activation(out=gvt, in_=bank(pb), func=AF.Gelu_apprx_tanh),
                              reads=[PB[pb]], writes=[b_gv[t]])
                        ss = gss[:, 16 * t:16 * t + 1]
                        pg.op("act", lambda e, gvt=gvt, ss=ss: e.activation(out=gjunk[:, 0:512], in_=gvt, func=AF.Square,
                                                                             accum_out=ss),
                              reads=[b_gv[t]], writes=[b_gjunk, b_gss[t]])
                if nxt is not None:
                    nt.compute(nxt)
                if full:
                    for f in range(4):
                        pb = fm_chunk(0, f, b_w[0])
                        evac_copy(qst[:, f * 512:(f + 1) * 512], pb, 512, b_qst)
                    pg.dma("pool", ch_sc[0], qT0[:, :, lb * 512:(lb + 1) * 512].rearrange("h p n -> p h n"),
                           qst.rearrange("p (h n) -> p h n", h=4), reads=[b_qst])
                    for f in range(4):
                        pb = fm_chunk(1536, f, b_w[3])
                        dst = gu[:, f * 512:(f + 1) * 512]
                        pg.op("act", lambda e, dst=dst, pb=pb: e.activation(out=dst, in_=bank(pb), func=AF.Gelu_apprx_tanh),
                              reads=[PB[pb]], writes=[b_gu])
                    pg.op("act", lambda e: e.activation(out=grs, in_=gss, func=AF.Sqrt, scale=1.0 / 512, bias=EPS),
                          reads=b_gss, writes=b_grs, strict=True)
                    pg.op("dve", lambda e: e.reciprocal(out=grs, in_=grs), reads=b_grs, writes=b_grs)
                    for t in range(4):
                        gvt = gv[t]
                        rs = grs[:, 16 * t:16 * t + 1]
                        vg = vgn[t]
                        pg.op("dve", lambda e, vg=vg, gvt=gvt, rs=rs: e.scalar_tensor_tensor(
                            out=vg, in0=gvt, scalar=rs, in1=sgn, op0=ALU.mult, op1=ALU.mult),
                            reads=[b_gv[t], b_grs[t], b_c], writes=[b_vgn[t]], strict=True)
                    for t in range(4):
                        s2 = t % 2
                        vg = vgn[t]
                        gb = (6, 7)[s2]

                        def fgate(e, vg=vg, gb=gb):
                            for g in range(4):
                                ins = e.matmul(bank(gb, 128, g * 128), lhsT=vg[:, g * 128:(g + 1) * 128],
                                               rhs=sgw16[:, g * 128:(g + 1) * 128], start=True, stop=True)
                            return ins

                        pg.op("pe", fgate, reads=[b_vgn[t], b_sgw], writes=[PB[gb]])
                        tg = tmpg[s2]
                        pg.op("dve", lambda e, tg=tg, gb=gb: e.tensor_tensor(out=tg, in0=bank(gb), in1=sgb, op=ALU.add),
                              reads=[PB[gb], b_c], writes=[b_tmpg[s2]])
                        o3 = obst.rearrange("p (g n) -> p g n", g=4)[:, :, t * 128:(t + 1) * 128]
                        g3 = gu.rearrange("p (g n) -> p g n", g=4)[:, :, t * 128:(t + 1) * 128]
                        t3 = tg.rearrange("p (g n) -> p g n", g=4)
                        pg.op("dve", lambda e, o3=o3, g3=g3, t3=t3: e.tensor_tensor(out=o3, in0=t3, in1=g3, op=ALU.mult),
                              reads=[b_tmpg[s2], b_gu], writes=[b_obst])
                    pg.dma("pool", ch_sc[1], obT[:, :, lb * 512:(lb + 1) * 512].rearrange("h p n -> p h n"),
                           obst.rearrange("p (h n) -> p h n", h=4), reads=[b_obst])
                if nxt is not None:
                    nt.tr(nxt, (0, 1))
            pg.barrier()

        def attn_ab_phase():
            al.reset()
            kT = al.b16(4 * S_E)
            V = al.b16(64 * 512)
            wo16 = al.b16(8 * D)
            b_kv = [Buf("kv%d" % g) for g in range(4)]
            b_wo = Buf("wo")
            kT3 = kT.rearrange("p (h n) -> p h n", h=4)
            V3 = V.rearrange("p (k c) -> p k c", k=64)
            for g in range(4):
                pg.dma("sp", ch_m[g % 4], kT3[:, :, g * 2048:(g + 1) * 2048],
                       kT0[:, :, g * 2048:(g + 1) * 2048].rearrange("h p n -> p h n"), writes=[b_kv[g]])
                pg.dma("sp", ch_m[g % 4], V3[:, g * 16:(g + 1) * 16, :],
                       v0[g * 2048:(g + 1) * 2048, :].rearrange("(k p) c -> p k c", p=128), writes=[b_kv[g]])
            load_weight_cast(wo16, W["ab_w_out"], 8, D, [(b_wo, 0, D)], [ch_w[0]])
            b_c = Buf("cst3")
            lamt = cst_load("lam", b_c)
            subln = cst_load("subln", b_c, 16)
            maskb = cst_load("maskb", b_c, 16)
            ones16 = al.b16(128); b_ones = Buf("ones")
            pg.op("pool", lambda e: e.memset(ones16, 1.0), writes=[b_ones])
            prod = al.f32(128); ssum = al.f32(32); ee = al.f32(32); nl = al.f32(16); sublnS = al.f32(16)
            b_l = Buf("lamc")
            pg.op("dve", lambda e: e.memset(ssum, 0.0), writes=[b_l])
            pg.op("dve", lambda e: e.scalar_tensor_tensor(out=prod[:, 0:64], in0=lamt[:, 0:64], scalar=1.0, in1=lamt[:, 64:128],
                                                            op0=ALU.mult, op1=ALU.mult, accum_out=ssum[:, 0:1]),
                  reads=[b_c], writes=[b_l], strict=True)
            pg.op("dve", lambda e: e.scalar_tensor_tensor(out=prod[:, 64:128], in0=lamt[:, 128:192], scalar=1.0, in1=lamt[:, 192:256],
                                                            op0=ALU.mult, op1=ALU.mult, accum_out=ssum[:, 16:17]),
                  reads=[b_c], writes=[b_l])
            b_l2 = Buf("lamc2")
            pg.op("act", lambda e: e.activation(out=ee[:, 0:1], in_=ssum[:, 0:1], func=AF.Exp), reads=[b_l], writes=[b_l2])
            pg.op("act", lambda e: e.activation(out=ee[:, 16:17], in_=ssum[:, 16:17], func=AF.Exp), reads=[b_l], writes=[b_l2])
            b_l3 = Buf("lamc3")
            pg.op("dve", lambda e: e.tensor_tensor(out=nl[:, 0:1], in0=ee[:, 16:17], in1=ee[:, 0:1], op=ALU.subtract),
                  reads=[b_l2], writes=[b_l3])
            pg.op("dve", lambda e: e.tensor_scalar(out=nl[:, 0:1], in0=nl[:, 0:1], scalar1=-0.2, scalar2=None, op0=ALU.add),
                  reads=[b_l3], writes=[b_l3], strict=True)
            pg.op("dve", lambda e: e.tensor_scalar(out=sublnS[:, 0:1], in0=subln[:, 0:1], scalar1=0.8, scalar2=None, op0=ALU.mult),
                  reads=[b_c], writes=[b_l3])
            qT = [al.b16(4 * 512) for _ in range(2)]; b_qT = [Buf("qT%d" % i) for i in range(2)]
            obt1 = al.b16(4 * 512); b_obt1 = Buf("obt")
            obt = [obt1, obt1]; b_obt = [b_obt1, b_obt1]
            NP = 5
            Pt = [al.b16(1024) for _ in range(NP)]; b_P = [Buf("P%d" % i) for i in range(NP)]
            cat = al.b16(4 * 512); b_cat = Buf("cat")
            rec = al.f32(1024); b_rec = Buf("rec")
            t1 = al.f32(512); t2 = al.f32(512); rstd = t1
            oa4 = [al.f32(512) for _ in range(4)]
            b_t = Buf("t12"); b_oa = [Buf("oa%d" % i) for i in range(4)]; b_rstd = b_t
            sq4 = [al.b16(512) for _ in range(4)]; b_sq = [Buf("sq%d" % i) for i in range(4)]
            lnS = rec; b_lnS = b_rec
            xres2 = [al.f32(1024) for _ in range(2)]; b_xres2 = [Buf("xres%d" % i) for i in range(2)]
            xres = xres2 + xres2; b_xres = b_xres2 + b_xres2
            acc = al.f32(1024); b_acc = Buf("acc")
            ones32 = al.f32(128); b_ones32 = Buf("ones32")
            pg.op("pool", lambda e: e.memset(ones32, 1.0), writes=[b_ones32])
            ucnt = [0]
            rcnt = [0]

            def sslot(k):
                return (4 + 2 * (k % 2), 5 + 2 * (k % 2))

            blocks = list(range(7, 16))
            DEPTH = 2

            def make_wout(blk):
                groups = []
                for t in range(4):
                    for nh in range(2):
                        def g(t=t, nh=nh, blk=blk):
                            kx = t % 2
                            xr, bxr = xres2[kx], b_xres2[kx]
                            if nh == 0:
                                pg.dma("sp", ch_r[kx], xr, xs_src(blk, t), reads=[XS[blk]], writes=[bxr])
                            pb = 2 + nh

                            def fo(e, t=t, nh=nh, pb=pb):
                                for c in range(8):
                                    src = cat if c < 4 else obt1
                                    cc = c % 4
                                    ins = e.matmul(bank(pb), lhsT=src[:, cc * 512 + t * 128:cc * 512 + (t + 1) * 128],
                                                   rhs=wo16[:, c * D + nh * 512:c * D + (nh + 1) * 512],
                                                   start=(c == 0), stop=(c == 7))
                                return ins

                            pg.op("pe", fo, reads=[b_cat, b_obt1, b_wo], writes=[PB[pb]])
                            xsl = xr[:, nh * 512:(nh + 1) * 512]
                            pg.op("dve", lambda e, xsl=xsl, pb=pb: e.tensor_tensor(out=xsl, in0=bank(pb), in1=xsl, op=ALU.add),
                                  reads=[PB[pb], bxr], writes=[bxr])
                            if nh == 1:
                                r0 = blk * 512 + t * 128
                                pg.dma("pool", ch_st[kx], xs[r0:r0 + 128, :], xr, reads=[bxr], writes=[XS[blk]])
                        groups.append(g)
                return groups

            pending = None
            for bi, blk in enumerate(blocks):
                lb = blk - 7
                qs = bi % 2
                q = qT[qs]
                pg.dma("sp", ch_x[qs], q.rearrange("p (h n) -> p h n", h=4),
                       qT0[:, :, lb * 512:(lb + 1) * 512].rearrange("h p n -> p h n"), writes=[b_qT[qs]])
                nkt = 4 * (blk + 1)
                units = [(h, kt) for h in range(4) for kt in range(nkt)]
                base = ucnt[0]
                wgroups = make_wout(pending) if pending is not None else []

                def stage_a(u):
                    h, kt = units[u]
                    k = base + u
                    b0, b1 = sslot(k)
                    kvb = b_kv[kt // 16]

                    def f(e, h=h, kt=kt, b0=b0, b1=b1, q=q):
                        e.matmul(bank(b0), lhsT=kT[0:64, h * S_E + kt * 128:h * S_E + (kt + 1) * 128],
                                 rhs=q[0:64, h * 512:(h + 1) * 512], start=True, stop=True)
                        return e.matmul(bank(b1), lhsT=kT[64:128, h * S_E + kt * 128:h * S_E + (kt + 1) * 128],
                                        rhs=q[64:128, h * 512:(h + 1) * 512], start=True, stop=True)

                    pg.op("pe", f, reads=[kvb, b_qT[qs]], writes=[PB[b0], PB[b1]])
                    ps2 = psum[:, b0 * 512:b0 * 512 + 1024]
                    P = Pt[k % NP]; bP = b_P[k % NP]
                    j = kt - 4 * blk
                    bias = maskb[:, 0:1] if kt < 32 else 0.0
                    if j <= 0:
                        pg.op("act", lambda e, P=P, ps2=ps2, bias=bias: e.activation(out=P, in_=ps2, func=AF.Exp,
                                                                                     scale=0.125, bias=bias),
                              reads=[PB[b0], PB[b1], b_c], writes=[bP])
                    else:
                        P3 = P.rearrange("p (m n) -> p m n", m=2)
                        s3 = ps2.rearrange("p (m n) -> p m n", m=2)
                        pg.op("act", lambda e, P3=P3, s3=s3, j=j, bias=bias: e.activation(
                            out=P3[:, :, 128 * j:512], in_=s3[:, :, 128 * j:512], func=AF.Exp, scale=0.125, bias=bias),
                            reads=[PB[b0], PB[b1], b_c], writes=[bP])
                        pg.op("pool", lambda e, P3=P3, j=j: e.memset(P3[:, :, 0:128 * j], 0.0), writes=[bP])
                    if j >= 0:
                        P3 = P.rearrange("p (m n) -> p m n", m=2)
                        pg.op("pool", lambda e, P3=P3, j=j: e.memset(P3[64:128, :, 128 * j:128 * j + 64], 0.0),
                              reads=[bP], writes=[bP])
                    if debug and bi == 0 and h == 3 and kt in (0, 1, 28, 31):
                        pg.dma("sp", ch_dbg, DBG["P"][(0, 1, 28, 31).index(kt)], P, reads=[bP])

                def stage_b_pv(u):
                    h, kt = units[u]
                    k = base + u
                    P = Pt[k % NP]; bP = b_P[k % NP]
                    kvb = b_kv[kt // 16]
                    first = (kt == 0)
                    last = (kt == nkt - 1)

                    def fpv(e, h=h, kt=kt, P=P, first=first, last=last):
                        vv = V[:, kt * 512 + h * 128:kt * 512 + (h + 1) * 128]
                        e.matmul(bank(0), lhsT=vv, rhs=P[:, 0:512], start=first, stop=last)
                        return e.matmul(bank(1), lhsT=vv, rhs=P[:, 512:1024], start=first, stop=last)

                    pg.op("pe", fpv, reads=[bP, kvb], writes=[PB[0], PB[1]])

                def stage_b_add(u):
                    h, kt = units[u]
                    k = base + u
                    P = Pt[k % NP]; bP = b_P[k % NP]
                    first = (kt == 0)
                    last = (kt == nkt - 1)
                    if first:
                        pg.op("dve", lambda e, P=P: e.tensor_copy(acc, P), reads=[bP], writes=[b_acc])
                    else:
                        pg.op("dve", lambda e, P=P: e.tensor_tensor(out=acc, in0=acc, in1=P, op=ALU.add),
                              reads=[bP, b_acc], writes=[b_acc])
                    if last:
                        def fsum(e):
                            e.matmul(bank(2), lhsT=ones32, rhs=acc[:, 0:512], start=True, stop=True)
                            return e.matmul(bank(3), lhsT=ones32, rhs=acc[:, 512:1024], start=True, stop=True)

                        pg.op("pe", fsum, reads=[b_acc, b_ones32], writes=[PB[2], PB[3]])
                        oa = oa4[h]
                        sq16 = sq4[h]
                        pg.op("act", lambda e: e.activation(out=lnS, in_=psum[:, 2 * 512:4 * 512], func=AF.Ln),
                              reads=[PB[2], PB[3]], writes=[b_lnS])
                        pg.op("act", lambda e: e.activation(out=rec, in_=lnS, func=AF.Exp, scale=-1.0),
                              reads=[b_lnS], writes=[b_rec])

                        def fin(h=h, oa=oa, sq16=sq16):
                            pg.op("dve", lambda e: e.tensor_tensor(out=t1, in0=bank(0), in1=rec[:, 0:512], op=ALU.mult),
                                  reads=[PB[0], b_rec], writes=[b_t])
                            pg.op("dve", lambda e: e.tensor_tensor(out=t2, in0=bank(1), in1=rec[:, 512:1024], op=ALU.mult),
                                  reads=[PB[1], b_rec], writes=[b_t])
                            pg.op("dve", lambda e, oa=oa: e.scalar_tensor_tensor(out=oa, in0=t2, scalar=nl[:, 0:1], in1=t1,
                                                                                  op0=ALU.mult, op1=ALU.add),
                                  reads=[b_t, b_l3], writes=[b_oa[h]])
                            pg.op("dve", lambda e, oa=oa, sq16=sq16: e.tensor_tensor(out=sq16, in0=oa, in1=oa, op=ALU.mult),
                                  reads=[b_oa[h]], writes=[b_sq[h]])
                        deferred.append([2, fin])

                nu = len(units)
                deferred = []
                pvq = []
                for step in range(nu + DEPTH):
                    if step < nu:
                        stage_a(step)
                    for d in deferred:
                        d[0] -= 1
                    while deferred and deferred[0][0] <= 0:
                        deferred.pop(0)[1]()
                    if step >= DEPTH:
                        pvq.append(step - DEPTH)
                        if not deferred:
                            while pvq:
                                stage_b_pv(pvq.pop(0))
                        stage_b_add(step - DEPTH)
                    if wgroups and step >= 6 and (step - 6) % 6 == 0:
                        wgroups.pop(0)()
                while deferred:
                    deferred.pop(0)[1]()
                while pvq:
                    stage_b_pv(pvq.pop(0))
                while wgroups:
                    wgroups.pop(0)()
                ucnt[0] += nu
                pg.dma("sp", ch_x[qs], obt1.rearrange("p (h n) -> p h n", h=4),
                       obT[:, :, lb * 512:(lb + 1) * 512].rearrange("h p n -> p h n"), writes=[b_obt1])
                for h in range(4):
                    ucnt[0] += 1
                    mb = sslot(ucnt[0])[0]
                    oa = oa4[h]
                    sq16 = sq4[h]
                    pg.op("pe", lambda e, mb=mb, sq16=sq16: e.matmul(bank(mb), lhsT=ones16, rhs=sq16, start=True, stop=True),
                          reads=[b_sq[h], b_ones], writes=[PB[mb]])
                    pg.op("act", lambda e, mb=mb: e.activation(out=rstd, in_=bank(mb), func=AF.Ln, scale=1.0 / 128, bias=EPS),
                          reads=[PB[mb]], writes=[b_rstd])
                    pg.op("act", lambda e: e.activation(out=rstd, in_=rstd, func=AF.Exp, scale=-0.5),
                          reads=[b_rstd], writes=[b_rstd])
                    ch = cat[:, h * 512:(h + 1) * 512]
                    pg.op("dve", lambda e, ch=ch, oa=oa: e.scalar_tensor_tensor(out=ch, in0=oa, scalar=sublnS[:, 0:1], in1=rstd,
                                                                                 op0=ALU.mult, op1=ALU.mult),
                          reads=[b_oa[h], b_rstd, b_l3], writes=[b_cat])
                pending = blk
            for g in make_wout(pending):
                g()
            pg.barrier()

        def xattn_phase(layer, blocks):
            al.reset()
            wq16 = al.b16(8 * D); wkv16 = al.b16(8 * 2 * D); wo16 = al.b16(8 * D)
            b_wq = Buf("wq"); b_wkv = [Buf("wkvk"), Buf("wkvv")]; b_wo = Buf("wo")
            load_weight_cast(wkv16, W["xa_wkv%d" % layer], 8, 2 * D, [(b_wkv[0], 0, D), (b_wkv[1], D, 2 * D)], [ch_w[0], ch_w[1]])
            load_weight_cast(wq16, W["xa_wq%d" % layer], 8, D, [(b_wq, 0, D)], [ch_w[2]])
            load_weight_cast(wo16, W["xa_wo%d" % layer], 8, D, [(b_wo, 0, D)], [ch_w[3]])
            ntm = NormT("g_mem_%d" % layer, lambda blk, t: mem[t * 128:(t + 1) * 128, :], lambda blk: None, width=256)
            nt = NormT("g_xa_%d" % layer, xs_src, lambda blk: XS[blk], nslots=4)
            ones16 = al.b16(128); b_ones = Buf("ones")
            pg.op("pool", lambda e: e.memset(ones16, 1.0), writes=[b_ones])
            kxT = al.b16(8 * 256); b_kx = Buf("kxT")
            Vx = al.b16(2 * D); b_vx = Buf("Vx")
            qT = al.b16(8 * 512); b_qT = Buf("qTx")
            oT = al.b16(8 * 512); b_oT = Buf("oTx")
            Pt = [al.b16(1024) for _ in range(2)]; b_P = [Buf("Px%d" % i) for i in range(2)]
            rec = al.f32(512); b_rec = Buf("recx")
            xres = [al.f32(1024) for _ in range(2)]; b_xres = [Buf("xres%d" % i) for i in range(2)]
            ntm.norm(0, nsub=2)
            ntm.tr(0, (0, 1), nsub=2)
            memT = ntm.hT
            prot = [0]

            def pbank():
                pb = (7, 6)[prot[0] % 2]
                prot[0] += 1
                return pb

            for f in range(8):
                pb = pbank()
                mm_group(pb, 256, lambda k, f=f: wkv16[:, k * 2 * D + f * 128:k * 2 * D + (f + 1) * 128],
                         lambda k: memT[:, k * 256:(k + 1) * 256], 8, [ntm.b_hT, b_wkv[0]])
                evac_copy(kxT[:, f * 256:(f + 1) * 256], pb, 256, b_kx)
            for mt in range(2):
                for nh in range(2):
                    pb = pbank()
                    mm_group(pb, 512, lambda k, mt=mt: memT[:, k * 256 + mt * 128:k * 256 + (mt + 1) * 128],
                             lambda k, nh=nh: wkv16[:, k * 2 * D + D + nh * 512:k * 2 * D + D + (nh + 1) * 512], 8,
                             [ntm.b_hT, b_wkv[1]])
                    evac_copy(Vx[:, mt * D + nh * 512:mt * D + (nh + 1) * 512], pb, 512, b_vx)
            hT = nt.hT
            scnt = [0]
            lnS = al.f32(512); b_lnS = Buf("lnS")
            xres4 = xres + [al.f32(1024) for _ in range(2)]
            b_xres4 = b_xres + [Buf("xres%d" % i) for i in range(2, 4)]
            nt.norm(blocks[0])
            nt.tr(blocks[0], (0, 1))
            for i, blk in enumerate(blocks):
                nxt = blocks[i + 1] if i + 1 < len(blocks) else None
                for t in range(4):
                    pg.dma("sp", ch_r[t % 2], xres4[t], xs_src(blk, t), reads=[XS[blk]], writes=[b_xres4[t]])
                if nxt is not None:
                    nt.load(nxt)
                for f in range(8):
                    pb = pbank()
                    mm_group(pb, 512, lambda k, f=f: wq16[:, k * D + f * 128:k * D + (f + 1) * 128],
                             lambda k: hT[:, k * 512:(k + 1) * 512], 8, [nt.b_hT, b_wq])
                    evac_copy(qT[:, f * 512:(f + 1) * 512], pb, 512, b_qT)

                def emit_qk(h):
                    sl = (scnt[0] + h) % 2
                    sb0 = (2, 0)[sl]

                    def fs(e, h=h, sb0=sb0):
                        for mt in range(2):
                            for c in range(2):
                                ins = e.matmul(bank(sb0 + mt), lhsT=kxT[:, (2 * h + c) * 256 + mt * 128:(2 * h + c) * 256 + (mt + 1) * 128],
                                               rhs=qT[:, (2 * h + c) * 512:(2 * h + c + 1) * 512], start=(c == 0), stop=(c == 1))
                        return ins

                    pg.op("pe", fs, reads=[b_kx, b_qT], writes=[PB[sb0], PB[sb0 + 1]])
                    ps2 = psum[:, sb0 * 512:sb0 * 512 + 1024]
                    P = Pt[sl]; bP = b_P[sl]
                    pg.op("act", lambda e, P=P, ps2=ps2: e.activation(out=P, in_=ps2, func=AF.Exp, scale=1.0 / 16),
                          reads=[PB[sb0], PB[sb0 + 1]], writes=[bP])

                emit_qk(0)
                for h in range(4):
                    sl = (scnt[0] + h) % 2
                    if h + 1 < 4:
                        emit_qk(h + 1)
                    P = Pt[sl]; bP = b_P[sl]

                    def fo(e, h=h, P=P):
                        for ei in range(2):
                            for mt in range(2):
                                e.matmul(bank(4 + ei), lhsT=Vx[:, mt * D + h * 256 + ei * 128:mt * D + h * 256 + (ei + 1) * 128],
                                         rhs=P[:, mt * 512:(mt + 1) * 512], start=(mt == 0), stop=(mt == 1))
                        for mt in range(2):
                            ins = e.matmul(bank(6), lhsT=ones16, rhs=P[:, mt * 512:(mt + 1) * 512], start=(mt == 0), stop=(mt == 1))
                        return ins

                    pg.op("pe", fo, reads=[bP, b_vx, b_ones], writes=[PB[4], PB[5], PB[6]])
                    pg.op("act", lambda e: e.activation(out=lnS, in_=bank(6), func=AF.Ln), reads=[PB[6]], writes=[b_lnS])
                    pg.op("act", lambda e: e.activation(out=rec, in_=lnS, func=AF.Exp, scale=-1.0), reads=[b_lnS], writes=[b_rec])
                    for ei in range(2):
                        dst = oT[:, (2 * h + ei) * 512:(2 * h + ei + 1) * 512]
                        pg.op("dve", lambda e, dst=dst, ei=ei: e.tensor_tensor(out=dst, in0=bank(4 + ei), in1=rec, op=ALU.mult),
                              reads=[PB[4 + ei], b_rec], writes=[b_oT])
                scnt[0] += 4
                scales = nt.compute(nxt, defer_scale=True) if nxt is not None else []
                for t in range(4):
                    xr, bxr = xres4[t], b_xres4[t]
                    for nh in range(2):
                        pb = pbank()
                        mm_group(pb, 512, lambda c, t=t: oT[:, c * 512 + t * 128:c * 512 + (t + 1) * 128],
                                 lambda c, nh=nh: wo16[:, c * D + nh * 512:c * D + (nh + 1) * 512], 8, [b_oT, b_wo])
                        xsl = xr[:, nh * 512:(nh + 1) * 512]
                        pg.op("dve", lambda e, xsl=xsl, pb=pb: e.tensor_tensor(out=xsl, in0=bank(pb), in1=xsl, op=ALU.add),
                              reads=[PB[pb], bxr], writes=[bxr])
                    r0 = blk * 512 + t * 128
                    pg.dma("pool", ch_st[t % 2], xs[r0:r0 + 128, :], xr, reads=[bxr], writes=[XS[blk]])
                    if scales:
                        scales[t]()
                if nxt is not None:
                    nt.tr(nxt, (0, 1))
            pg.barrier()

        def proj_c_phase():
            al.reset()
            NW = 3072
            w16 = al.b16(8 * NW)
            b_w = [Buf("wc%d" % i) for i in range(3)]
            for ci in (1, 2, 0):
                load_weight_cast(w16, W["c_w_in"], 8, NW, [(b_w[ci], ci * D, ci * D + 512)], [ch_w[ci]])
                load_weight_cast(w16, W["c_w_in"], 8, NW, [(b_w[ci], ci * D + 512, (ci + 1) * D)], [ch_w[ci]])
            nt = NormT("g_mix_1", xs_src, lambda blk: XS[blk], nslots=4)
            hT = nt.hT
            kst = al.b16(8 * 512); b_kst = Buf("kst")
            qst = al.b16(8 * 512); b_qst = Buf("qst")
            vst = al.b16(4 * 2 * D); b_vst = Buf("vst")
            pg.op("pool", lambda e: e.memset(vst, 1.0), writes=[b_vst])
            rot = [0]

            def pbk():
                pb = 2 + rot[0] % 4
                rot[0] += 1
                return pb

            blocks = list(range(7, 16))
            nt.norm(blocks[0])
            nt.tr(blocks[0], (0, 1))
            for i, blk in enumerate(blocks):
                lb = blk - 7
                nxt = blocks[i + 1] if i + 1 < len(blocks) else None
                if nxt is not None:
                    nt.load(nxt)
                for f in range(8):
                    if nxt is not None and f == 4:
                        nt.compute(nxt)
                    pb = pbk()
                    mm_group(pb, 512, lambda k, f=f: w16[:, k * NW + D + f * 128:k * NW + D + (f + 1) * 128],
                             lambda k: hT[:, k * 512:(k + 1) * 512], 8, [nt.b_hT, b_w[1]])
                    evac_copy(kst[:, f * 512:(f + 1) * 512], pb, 512, b_kst)
                pg.dma("pool", ch_sc[0], kT1[:, :, lb * 512:(lb + 1) * 512].rearrange("h p n -> p h n"),
                       kst.rearrange("p (h n) -> p h n", h=8), reads=[b_kst])
                for t in range(4):
                    for nh in range(2):
                        pb = pbk()
                        mm_group(pb, 512, lambda k, t=t: hT[:, k * 512 + t * 128:k * 512 + (t + 1) * 128],
                                 lambda k, nh=nh: w16[:, k * NW + 2 * D + nh * 512:k * NW + 2 * D + (nh + 1) * 512], 8,
                                 [nt.b_hT, b_w[2]])
                        vdst = vst[:, t * 2 * D + nh * D:t * 2 * D + (nh + 1) * D].rearrange("p (h c) -> p h c", h=8)[:, :, 0:64]
                        vsrc = bank(pb).rearrange("p (h c) -> p h c", h=8)
                        if (t + nh) % 2 == 0:
                            pg.op("act", lambda e, vdst=vdst, vsrc=vsrc: e.activation(out=vdst, in_=vsrc, func=AF.Copy),
                                  reads=[PB[pb]], writes=[b_vst])
                        else:
                            pg.op("dve", lambda e, vdst=vdst, vsrc=vsrc: e.tensor_copy(vdst, vsrc), reads=[PB[pb]], writes=[b_vst])
                pg.dma("pool", ch_sc[1], v1[lb * 512:(lb + 1) * 512, :].rearrange("(t p) c -> p t c", p=128),
                       vst.rearrange("p (t c) -> p t c", t=4), reads=[b_vst])
                if blk >= 8:
                    for f in range(8):
                        pb = pbk()
                        mm_group(pb, 512, lambda k, f=f: w16[:, k * NW + f * 128:k * NW + (f + 1) * 128],
                                 lambda k: hT[:, k * 512:(k + 1) * 512], 8, [nt.b_hT, b_w[0]])
                        evac_copy(qst[:, f * 512:(f + 1) * 512], pb, 512, b_qst)
                    pg.dma("pool", ch_sc[0], qT1[:, :, lb * 512:(lb + 1) * 512].rearrange("h p n -> p h n"),
                           qst.rearrange("p (h n) -> p h n", h=8), reads=[b_qst])
                if nxt is not None:
                    nt.tr(nxt, (0, 1))
            pg.barrier()

        def band_phase():
            al.reset()
            wo = al.b16(8 * D)
            b_wo = Buf("woc")
            load_weight_cast(wo, W["c_w_out"], 8, D, [(b_wo, 0, D)], [ch_w[0]])
            relb = al.f32(16 * 640); b_relb = Buf("relb")
            for g in range(4):
                pg.dma("sp", ch_c, relb[:, g * 2560:(g + 1) * 2560], W["relb"][:, g * 2560:(g + 1) * 2560], writes=[b_relb])
            relb3 = relb.rearrange("p (h m) -> p h m", h=16)
            pg.op("pool", lambda e: e.memset(relb3[0:64, :, 576:640], -30000.0), reads=[b_relb], writes=[b_relb])
            pg.op("pool", lambda e: e.memset(relb3[64:128, :, 0:64], -30000.0), reads=[b_relb], writes=[b_relb])
            b_c = Buf("cst8")
            maskb = cst_load("maskb", b_c, 16)
            qTb = al.b16(8 * 512); b_q = Buf("qb")
            kTb = al.b16(8 * 1024); b_k = Buf("kb")
            Vb = al.b16(8 * 2 * D); b_v = Buf("vb")
            oT = al.b16(8 * 512); b_oT = Buf("oTc")
            NS = 6
            DEPTH = 4
            sbt = [al.f32(1024) for _ in range(NS)]; b_sb = [Buf("sb%d" % i) for i in range(NS)]
            Pt = [al.b16(1024) for _ in range(NS)]; b_P = [Buf("Pc%d" % i) for i in range(NS)]
            rec = [al.f32(1024) for _ in range(2)]; b_rec = [Buf("recc%d" % i) for i in range(2)]
            xres2 = [al.f32(1024) for _ in range(2)]; b_xres2 = [Buf("xres%d" % i) for i in range(2)]
            xres = xres2 + xres2; b_xres = b_xres2 + b_xres2
            ucnt = [0]
            KT_ORDER = (3, 4, 2, 5, 1, 6, 0, 7)
            blocks = list(range(8, 16))

            def load_blk(bi):
                blk = blocks[bi]
                li = blk - 8
                pg.dma("sp", ch_m[0], qTb.rearrange("p (f n) -> p f n", f=8),
                       qT1[:, :, (li + 1) * 512:(li + 2) * 512].rearrange("f p n -> p f n"), writes=[b_q])
                pg.dma("sp", ch_m[1], kTb.rearrange("p (f n) -> p f n", f=8),
                       kT1[:, :, li * 512:li * 512 + 1024].rearrange("f p n -> p f n"), writes=[b_k])
                pg.dma("sp", ch_m[2], Vb.rearrange("p (k c) -> p k c", k=8),
                       v1[li * 512:li * 512 + 1024, :].rearrange("(k p) c -> p k c", p=128), writes=[b_v])

            def geom(kt):
                cq_lo = max(0, 2 * kt - 8)
                cq_hi = min(7, 2 * kt + 1)
                n0 = 64 * cq_lo
                n1 = 64 * (cq_hi + 1)
                return n0, n1, n1 - n0, n0 + 512 - 128 * kt

            load_blk(0)
            for bi, blk in enumerate(blocks):
                for t in range(2):
                    pg.dma("sp", ch_r[t % 2], xres[t], xs_src(blk, t), reads=[XS[blk]], writes=[b_xres[t]])
                units = [(f, oi, kt) for f in range(8) for oi, kt in enumerate(KT_ORDER)]
                base = ucnt[0]

                def stage_a(u):
                    f, oi, kt = units[u]
                    g = base + u
                    n0, n1, N, m0 = geom(kt)
                    sb0 = (0, 2)[g % 2]
                    sl = g % NS

                    def fqk(e, kt=kt, n0=n0, n1=n1, N=N, sb0=sb0, f=f):
                        e.matmul(bank(sb0, N), lhsT=kTb[0:64, f * 1024 + kt * 128:f * 1024 + (kt + 1) * 128],
                                 rhs=qTb[0:64, f * 512 + n0:f * 512 + n1], start=True, stop=True)
                        return e.matmul(bank(sb0 + 1, N), lhsT=kTb[64:128, f * 1024 + kt * 128:f * 1024 + (kt + 1) * 128],
                                        rhs=qTb[64:128, f * 512 + n0:f * 512 + n1], start=True, stop=True)

                    pg.op("pe", fqk, reads=[b_k, b_q], writes=[PB[sb0], PB[sb0 + 1]])
                    sb3 = sbt[sl].rearrange("p (m n) -> p m n", m=2)[:, :, 0:N]
                    ps3 = psum[:, sb0 * 512:sb0 * 512 + 1024].rearrange("p (m n) -> p m n", m=2)[:, :, 0:N]
                    r3 = relb[:, 2 * f * 640:(2 * f + 2) * 640].rearrange("p (m n) -> p m n", m=2)[:, :, m0:m0 + N]
                    pg.op("dve", lambda e, sb3=sb3, ps3=ps3, r3=r3: e.scalar_tensor_tensor(
                        out=sb3, in0=ps3, scalar=0.125, in1=r3, op0=ALU.mult, op1=ALU.add),
                        reads=[PB[sb0], PB[sb0 + 1], b_relb], writes=[b_sb[sl]])
                    P3 = Pt[sl].rearrange("p (m n) -> p m n", m=2)[:, :, 0:N]
                    bias = maskb[:, 0:1] if (blk == 8 and kt < 4) else 0.0
                    pg.op("act", lambda e, P3=P3, sb3=sb3, bias=bias: e.activation(out=P3, in_=sb3, func=AF.Exp, bias=bias),
                          reads=[b_sb[sl], b_c], writes=[b_P[sl]])

                def stage_b(u):
                    f, oi, kt = units[u]
                    g = base + u
                    n0, n1, N, m0 = geom(kt)
                    sl = g % NS
                    P = Pt[sl]
                    fs = f % 2
                    pb0 = 4 + 2 * fs
                    first = (oi == 0)
                    last = (oi == 7)

                    def fpv(e, kt=kt, f=f, P=P, N=N, n0=n0, first=first, last=last, pb0=pb0):
                        for par in range(2):
                            h = 2 * f + par
                            ins = e.matmul(bank(pb0 + par, N, n0), lhsT=Vb[:, kt * 2 * D + h * 128:kt * 2 * D + (h + 1) * 128],
                                           rhs=P[:, par * 512:par * 512 + N], start=first, stop=last)
                        return ins

                    pg.op("pe", fpv, reads=[b_P[sl], b_v], writes=[PB[pb0], PB[pb0 + 1]])
                    if last:
                        rc = rec[fs]
                        pg.op("act", lambda e, rc=rc, pb0=pb0: e.activation(out=rc[0:64, :], in_=psum[64:128, pb0 * 512:pb0 * 512 + 1024],
                                                                              func=AF.Ln),
                              reads=[PB[pb0], PB[pb0 + 1]], writes=[b_rec[fs]])
                        pg.op("act", lambda e, rc=rc: e.activation(out=rc[0:64, :], in_=rc[0:64, :], func=AF.Exp, scale=-1.0),
                              reads=[b_rec[fs]], writes=[b_rec[fs]])
                        def fin(f=f, rc=rc, pb0=pb0, fs=fs):
                            for par in range(2):
                                dst = oT[64 * par:64 * par + 64, f * 512:(f + 1) * 512]
                                pg.op("dve", lambda e, dst=dst, rc=rc, pb0=pb0, par=par: e.tensor_tensor(
                                    out=dst, in0=psum[0:64, (pb0 + par) * 512:(pb0 + par + 1) * 512],
                                    in1=rc[0:64, par * 512:(par + 1) * 512], op=ALU.mult),
                                    reads=[PB[pb0 + par], b_rec[fs]], writes=[b_oT])
                        deferred.append([3, fin])

                nu = len(units)
                deferred = []
                for step in range(nu + DEPTH):
                    if step < nu:
                        stage_a(step)
                    for d in deferred:
                        d[0] -= 1
                    while deferred and deferred[0][0] <= 0:
                        deferred.pop(0)[1]()
                    if step >= DEPTH:
                        stage_b(step - DEPTH)
                while deferred:
                    deferred.pop(0)[1]()
                ucnt[0] += nu
                if bi + 1 < len(blocks):
                    load_blk(bi + 1)
                for t in range(4):
                    xr, bxr = xres[t], b_xres[t]
                    if t >= 2:
                        pg.dma("sp", ch_r[t % 2], xr, xs_src(blk, t), reads=[XS[blk]], writes=[bxr])
                    for nh in range(2):
                        pb = nh

                        def fo(e, t=t, nh=nh, pb=pb):
                            for ff in range(8):
                                ins = e.matmul(bank(pb), lhsT=oT[:, ff * 512 + t * 128:ff * 512 + (t + 1) * 128],
                                               rhs=wo[:, ff * D + nh * 512:ff * D + (nh + 1) * 512],
                                               start=(ff == 0), stop=(ff == 7))
                            return ins

                        pg.op("pe", fo, reads=[b_oT, b_wo], writes=[PB[pb]])
                        xsl = xr[:, nh * 512:(nh + 1) * 512]
                        pg.op("dve", lambda e, xsl=xsl, pb=pb: e.tensor_tensor(out=xsl, in0=bank(pb), in1=xsl, op=ALU.add),
                              reads=[PB[pb], bxr], writes=[bxr])
                    r0 = blk * 512 + t * 128
                    pg.dma("pool", ch_st[t % 2], xs[r0:r0 + 128, :], xr, reads=[bxr], writes=[XS[blk]])
            pg.barrier()

        if 1 in phases:
            ffn_phase(0, 1, list(range(16)), xe)
        if 2 in phases:
            proj_ab_phase()
        if 3 in phases:
            attn_ab_phase()
        if 4 in phases:
            xattn_phase(0, list(range(7, 16)))
        if 5 in phases:
            ffn_phase(0, 2, list(range(7, 16)), xs)
        if 6 in phases:
            ffn_phase(1, 1, list(range(7, 16)), xs)
        if 7 in phases:
            proj_c_phase()
        if 8 in phases:
            band_phase()
        if 9 in phases:
            xattn_phase(1, list(range(8, 16)))
        if 10 in phases:
            ffn_phase(1, 2, list(range(8, 16)), xs, final=True)

        pg.final_wait("sp")
        pg.emit()
    return nc


def _rep(v, n=128):
    return np.ascontiguousarray(np.broadcast_to(np.asarray(v, np.float32).reshape(1, -1), (n, v.size)))


def make_in_maps(inputs):
    x = np.asarray(inputs["x"], np.float32)
    mem = np.asarray(inputs["mem"], np.float32)
    g = lambda k: np.asarray(inputs[k], np.float32)
    shared = {}
    for l in range(2):
        for nm in ["ffn1_wg", "ffn1_wu", "ffn1_wd", "xa_wq", "xa_wkv", "xa_wo", "ffn2_wg", "ffn2_wu", "ffn2_wd"]:
            shared["%s%d" % (nm, l)] = np.ascontiguousarray(g(nm)[l])
    shared["ab_w_in"] = np.ascontiguousarray(g("ab_w_in")[0])
    shared["ab_w_out"] = np.ascontiguousarray(g("ab_w_out")[0])
    sgw = g("ab_sg_w")[0]
    shared["sgwT"] = np.ascontiguousarray(sgw.transpose(2, 0, 1).reshape(128, 512))
    shared["c_w_in"] = np.ascontiguousarray(g("c_w_in")[0])
    shared["c_w_out"] = np.ascontiguousarray(g("c_w_out")[0])
    rb = g("c_rel_bias")[0]
    j = np.arange(128)[:, None]
    m = np.arange(640)[None, :]
    idx = np.clip(m - j, -256, 256) + 256
    shared["relb"] = np.ascontiguousarray(rb[:, idx].transpose(1, 0, 2).reshape(128, 16 * 640))

    def cst_for(maskval):
        c = np.zeros((128, CST_COLS), np.float32)

        def put(name, arr):
            o, w = CST[name]
            c[:, o:o + w] = arr

        for l in range(2):
            put("g_ffn1_%d" % l, _rep(g("ffn1_norm")[l]))
            put("g_mix_%d" % l, _rep(g("mix_norm")[l]))
            put("g_xa_%d" % l, _rep(g("xa_norm")[l]))
            put("g_mem_%d" % l, _rep(g("xa_mem_norm")[l]))
            put("g_ffn2_%d" % l, _rep(g("ffn2_norm")[l]))
        put("g_final", _rep(g("final_norm")))
        put("lam", _rep(g("ab_lam")[0].reshape(-1)))
        put("subln", np.repeat(g("ab_subln")[0].reshape(128, 1), 16, axis=1))
        put("sgnorm", _rep(g("ab_sg_norm")[0]))
        put("sgb", _rep(g("ab_sg_b")[0].reshape(-1)))
        put("maskb", np.full((128, 16), maskval, np.float32))
        return c

    cstA = cst_for(NEG)
    cstB = cst_for(0.0)
    in_maps = []
    for b in range(4):
        for h in range(2):
            if h == 0:
                xe = np.concatenate([np.zeros((4096, D), np.float32), x[b, :4096]], axis=0)
            else:
                xe = np.ascontiguousarray(x[b])
            d = dict(shared)
            d["xe"] = xe
            d["mem"] = np.ascontiguousarray(mem[b])
            d["cst"] = cstA if h == 0 else cstB
            in_maps.append(d)
    return in_maps


_NC_CACHE = {}


def kernel(**inputs):
    in_maps = make_in_maps(inputs)
    if "nc" not in _NC_CACHE:
        _NC_CACHE["nc"] = build_program()
    nc = _NC_CACHE["nc"]
    res = run_bass_kernel_spmd(nc, in_maps, core_ids=list(range(8)))
    outp = np.empty((4, 8192, D), np.float32)
    for b in range(4):
        for h in range(2):
            outp[b, h * 4096:(h + 1) * 4096] = res.results[b * 2 + h]["out"]
    return outp
```

```python
import numpy as np
from contextlib import ExitStack
import concourse.bass as bass
import concourse.mybir as mybir
from concourse.bass_utils import run_bass_kernel_spmd

F32 = mybir.dt.float32
BF16 = mybir.dt.bfloat16
AF = mybir.ActivationFunctionType
ALU = mybir.AluOpType
AX = mybir.AxisListType

D = 1024
DFF = 2816
NFF = DFF // 128
S_E = 8192
EPS = 1e-6
SEM_CAP = 50000
NEG = -80.0


class Buf:
    __slots__ = ("name", "w", "r")

    def __init__(self, name=""):
        self.name = name
        self.w = None
        self.r = []


class Chan:
    def __init__(self, sem):
        self.sem = sem
        self.issued = 0


class Op:
    __slots__ = ("eng", "fn", "deps", "needed", "count", "chan")

    def __init__(self, eng, fn, deps, chan=None):
        self.eng = eng
        self.fn = fn
        self.deps = deps
        self.needed = False
        self.count = 0
        self.chan = chan


ENGS = ("pe", "act", "dve", "pool", "sp")


class Prog:
    def __init__(self, nc, es):
        self.nc = nc
        self.es = es
        self.ops = {e: [] for e in ENGS}
        self.extra = {e: [] for e in ENGS}
        self.chans = []
        self.sems = {e: [] for e in ENGS}

    def chan(self, name):
        c = Chan(self.es.enter_context(self.nc.semaphore(name)))
        self.chans.append(c)
        return c

    def _deps(self, eng, reads, writes, strict=False):
        deps = []
        for b in reads:
            if b.w is not None:
                deps.append(b.w)
        for b in writes:
            if b.w is not None:
                deps.append(b.w)
            deps.extend(b.r)
        out = []
        for d in deps:
            if d[0] == "op":
                if d[1].eng == eng and not strict:
                    continue
                d[1].needed = True
                out.append(d)
            else:
                out.append(("dma", d[1], d[1].issued))
        out.extend(self.extra[eng])
        self.extra[eng] = []
        return out

    def op(self, eng, fn, reads=(), writes=(), strict=False):
        o = Op(eng, fn, self._deps(eng, reads, writes, strict))
        self.ops[eng].append(o)
        ref = ("op", o)
        for b in reads:
            b.r.append(ref)
        for b in writes:
            b.w = ref
            b.r = []
        return o

    def dma(self, eng, chan, out, in_, reads=(), writes=()):
        o = Op(eng, lambda e: e.dma_start(out=out, in_=in_), self._deps(eng, reads, writes), chan=chan)
        chan.issued += 16
        self.ops[eng].append(o)
        ref = ("dma", chan)
        for b in reads:
            b.r.append(ref)
        for b in writes:
            b.w = ref
            b.r = []
        return o

    def barrier(self):
        for e in ENGS:
            for e2 in ENGS:
                if e2 != e and self.ops[e2]:
                    last = self.ops[e2][-1]
                    if last.chan is None:
                        last.needed = True
                        self.extra[e].append(("op", last))
                    else:
                        for o in reversed(self.ops[e2]):
                            if o.chan is None:
                                o.needed = True
                                self.extra[e].append(("op", o))
                                break
            for c in self.chans:
                if c.issued:
                    self.extra[e].append(("dma", c, c.issued))

    def final_wait(self, eng="sp"):
        deps = [("dma", c, c.issued) for c in self.chans if c.issued]
        o = Op(eng, None, deps)
        self.ops[eng].append(o)

    def emit(self):
        nc = self.nc
        for e in ENGS:
            c = 0
            for o in self.ops[e]:
                if o.chan is None and o.needed:
                    c += 1
                    o.count = c
            nsem = (c + SEM_CAP - 1) // SEM_CAP
            for i in range(max(nsem, 1)):
                self.sems[e].append(self.es.enter_context(nc.semaphore("s_%s%d" % (e, i))))
        block = self.es.enter_context(nc.Block())

        def replay(e, engobj):
            known_op = {}
            known_dma = {}
            for o in self.ops[e]:
                for d in o.deps:
                    if d[0] == "op":
                        t = d[1]
                        key = ((t.count - 1) // SEM_CAP, (t.count - 1) % SEM_CAP + 1)
                        if known_op.get(t.eng, (-1, 0)) >= key:
                            continue
                        known_op[t.eng] = key
                        engobj.wait_ge(self.sems[t.eng][key[0]], key[1])
                    else:
                        ch, val = d[1], d[2]
                        if known_dma.get(id(ch), 0) >= val:
                            continue
                        known_dma[id(ch)] = val
                        engobj.wait_ge(ch.sem, val)
                if o.fn is None:
                    continue
                ins = o.fn(engobj)
                if o.chan is not None:
                    ins.then_inc(o.chan.sem, 16)
                elif o.needed:
                    ins.then_inc(self.sems[e][(o.count - 1) // SEM_CAP], 1)

        @block.tensor
        def _(x):
            replay("pe", x)

        @block.scalar
        def _(x):
            replay("act", x)

        @block.vector
        def _(x):
            replay("dve", x)

        @block.gpsimd
        def _(x):
            replay("pool", x)

        @block.sync
        def _(x):
            replay("sp", x)


class Alloc:
    def __init__(self, arena, ncol):
        self.arena = arena
        self.n = ncol
        self.off = 0

    def reset(self):
        self.off = 0

    def f32(self, n):
        ap = self.arena[:, self.off:self.off + n]
        self.off += n
        assert self.off <= self.n, ("sbuf overflow", self.off, self.n)
        return ap

    def b16(self, n):
        assert n % 2 == 0
        ap = self.arena[:, self.off:self.off + n // 2].bitcast(BF16)
        self.off += n // 2
        assert self.off <= self.n, ("sbuf overflow", self.off, self.n)
        return ap


ARENA_COLS = 52800

CST = {}
_off = 0
for _n, _w in [("g_ffn1_0", 1024), ("g_mix_0", 1024), ("g_xa_0", 1024), ("g_mem_0", 1024), ("g_ffn2_0", 1024),
               ("g_ffn1_1", 1024), ("g_mix_1", 1024), ("g_xa_1", 1024), ("g_mem_1", 1024), ("g_ffn2_1", 1024),
               ("g_final", 1024), ("lam", 256), ("subln", 16), ("sgnorm", 512), ("sgb", 512), ("maskb", 16)]:
    CST[_n] = (_off, _w)
    _off += _w
CST_COLS = _off


def build_program(phases=tuple(range(1, 11)), debug=False):
    nc = bass.Bass("TRN2", target_bir_lowering=False)
    es = ExitStack()
    with es:
        pg = Prog(nc, es)
        kin = "ExternalInput"
        kscr = "ExternalOutput" if debug else "Internal"

        def din(name, shape, dt=F32):
            return nc.dram_tensor(name, list(shape), dt, kind=kin).ap()

        xe = din("xe", [S_E, D])
        mem = din("mem", [256, D])
        cst = din("cst", [128, CST_COLS])
        W = {}
        for l in range(2):
            for nm, shp in [("ffn1_wg", [D, DFF]), ("ffn1_wu", [D, DFF]), ("ffn1_wd", [DFF, D]),
                            ("xa_wq", [D, D]), ("xa_wkv", [D, 2 * D]), ("xa_wo", [D, D]),
                            ("ffn2_wg", [D, DFF]), ("ffn2_wu", [D, DFF]), ("ffn2_wd", [DFF, D])]:
                W["%s%d" % (nm, l)] = din("%s%d" % (nm, l), shp)
        W["ab_w_in"] = din("ab_w_in", [D, 2560])
        W["ab_w_out"] = din("ab_w_out", [D, D])
        W["sgwT"] = din("sgwT", [128, 512])
        W["c_w_in"] = din("c_w_in", [D, 3072])
        W["c_w_out"] = din("c_w_out", [D, D])
        W["relb"] = din("relb", [128, 16 * 640])

        out = nc.dram_tensor("out", [4096, D], F32, kind="ExternalOutput").ap()
        xs = nc.dram_tensor("xs", [S_E, D], F32, kind=kscr).ap()
        qT0 = nc.dram_tensor("qT0", [4, 128, 4608], BF16, kind=kscr).ap()
        kT0 = nc.dram_tensor("kT0", [4, 128, S_E], BF16, kind=kscr).ap()
        v0 = nc.dram_tensor("v0", [S_E, 512], BF16, kind=kscr).ap()
        obT = nc.dram_tensor("obT", [4, 128, 4608], BF16, kind=kscr).ap()
        qT1 = nc.dram_tensor("qT1", [8, 128, 4608], BF16, kind=kscr).ap()
        kT1 = nc.dram_tensor("kT1", [8, 128, 4608], BF16, kind=kscr).ap()
        v1 = nc.dram_tensor("v1", [4608, 2 * D], BF16, kind=kscr).ap()

        DBG = {}
        if debug:
            DBG["hT"] = nc.dram_tensor("dbg_hT", [128, 4096], BF16, kind="ExternalOutput").ap()
            DBG["hid"] = nc.dram_tensor("dbg_hid", [128, NFF * 512], BF16, kind="ExternalOutput").ap()
            DBG["wg"] = nc.dram_tensor("dbg_wg", [128, 8 * DFF], BF16, kind="ExternalOutput").ap()
            DBG["xsb"] = nc.dram_tensor("dbg_xsb", [128, 1024], BF16, kind="ExternalOutput").ap()
            DBG["rs"] = nc.dram_tensor("dbg_rs", [128, 64], F32, kind="ExternalOutput").ap()
            DBG["cat"] = nc.dram_tensor("dbg_cat", [2, 128, 2048], BF16, kind="ExternalOutput").ap()
            DBG["oa"] = nc.dram_tensor("dbg_oa", [2, 128, 512], F32, kind="ExternalOutput").ap()
            DBG["P"] = nc.dram_tensor("dbg_P", [4, 128, 1024], BF16, kind="ExternalOutput").ap()
            DBG["rec"] = nc.dram_tensor("dbg_rec", [128, 1024], F32, kind="ExternalOutput").ap()
            DBG["t1"] = nc.dram_tensor("dbg_t1", [128, 512], F32, kind="ExternalOutput").ap()
            DBG["nl"] = nc.dram_tensor("dbg_nl", [128, 16], F32, kind="ExternalOutput").ap()
            DBG["ssum"] = nc.dram_tensor("dbg_ssum", [128, 32], F32, kind="ExternalOutput").ap()
            DBG["ee"] = nc.dram_tensor("dbg_ee", [128, 32], F32, kind="ExternalOutput").ap()
            DBG["lamt"] = nc.dram_tensor("dbg_lamt", [128, 256], F32, kind="ExternalOutput").ap()
        arena = es.enter_context(nc.sbuf_tensor("arena", [128, ARENA_COLS], F32))
        psum = es.enter_context(nc.psum_tensor("psum", [128, 4096], F32))
        al = Alloc(arena, ARENA_COLS)
        PB = [Buf("ps%d" % i) for i in range(8)]

        def bank(i, n=512, off=0):
            return psum[:, i * 512 + off:i * 512 + off + n]

        def bank16(i):
            return psum[:, i * 512:(i + 1) * 512].bitcast(BF16)

        XS = [Buf("xs_blk%d" % i) for i in range(16)]

        ch_x = [pg.chan("ch_x%d" % i) for i in range(2)]
        ch_r = [pg.chan("ch_r%d" % i) for i in range(2)]
        ch_st = [pg.chan("ch_st%d" % i) for i in range(2)]
        ch_w = [pg.chan("ch_w%d" % i) for i in range(4)]
        ch_c = pg.chan("ch_c")
        ch_m = [pg.chan("ch_m%d" % i) for i in range(6)]
        ch_dbg = pg.chan("ch_dbg")
        ch_sc = [pg.chan("ch_sc%d" % i) for i in range(2)]
        ch_wb = [pg.chan("ch_wb%d" % i) for i in range(12)]

        class NormT:
            def __init__(self, gname, src_fn, src_buf_fn, width=512, nslots=2):
                self.width = width
                self.nslots = nslots
                self.src_fn = src_fn
                self.src_buf_fn = src_buf_fn
                self.gam = al.f32(1024)
                self.b_gam = Buf("gam")
                o, w = CST[gname]
                pg.dma("sp", ch_c, self.gam, cst[:, o:o + w], writes=[self.b_gam])
                self.xin = [al.f32(1024) for _ in range(nslots)]
                self.b_xin = [Buf("xin%d" % i) for i in range(nslots)]
                self.xsb = [al.b16(1024) for _ in range(4)]
                self.b_xsb = [Buf("xsb%d" % i) for i in range(4)]
                self.junk = al.f32(1024)
                self.b_junk = Buf("junk")
                self.ss = al.f32(64)
                self.rs = al.f32(64)
                self.b_ss = [Buf("ss%d" % i) for i in range(4)]
                self.b_rs = [Buf("rs%d" % i) for i in range(4)]
                self.hT = al.b16(8 * width)
                self.b_hT = Buf("hT")
                self.ident = al.b16(128)
                self.b_ident = Buf("ident")
                ident = self.ident
                pg.op("pool", lambda e: e.memset(ident, 0.0), writes=[self.b_ident])
                pg.op("pool", lambda e: e.affine_select(out=ident, in_=ident, pattern=[[-1, 128]],
                                                          compare_op=ALU.not_equal, fill=1.0, base=0,
                                                          channel_multiplier=1), reads=[self.b_ident], writes=[self.b_ident],
                      strict=True)
                self.cnt = 0

            def norm(self, blk, nsub=4):
                if self.nslots >= 4:
                    self.load(blk, nsub)
                    self.compute(blk, nsub)
                    return
                for t in range(nsub):
                    self.norm1(blk, t)

            def load(self, blk, nsub=4):
                sb = self.src_buf_fn(blk)
                for t in range(nsub):
                    pg.dma("sp", ch_x[t % 2], self.xin[t], self.src_fn(blk, t), reads=[sb] if sb else [], writes=[self.b_xin[t]])

            def compute(self, blk, nsub=4, defer_scale=False):
                ssall = self.ss[:, 0:64]
                rsall = self.rs[:, 0:64]
                junk = self.junk
                pg.op("dve", lambda e: e.memset(ssall, 0.0), writes=self.b_ss)
                for t in range(nsub):
                    xin, bx = self.xin[t], self.b_xin[t]
                    ss = self.ss[:, 16 * t:16 * t + 1]
                    pg.op("act", lambda e, xin=xin, ss=ss: e.activation(out=junk, in_=xin, func=AF.Square, accum_out=ss),
                          reads=[bx], writes=[self.b_junk, self.b_ss[t]])
                pg.op("act", lambda e: e.activation(out=rsall, in_=ssall, func=AF.Sqrt, scale=1.0 / 1024, bias=EPS),
                      reads=self.b_ss, writes=self.b_rs, strict=True)
                pg.op("dve", lambda e: e.reciprocal(out=rsall, in_=rsall), reads=self.b_rs, writes=self.b_rs)
                gam = self.gam
                scales = []
                for t in range(nsub):
                    def sc(t=t):
                        xin, bx = self.xin[t], self.b_xin[t]
                        rs = self.rs[:, 16 * t:16 * t + 1]
                        xsb = self.xsb[t]
                        pg.op("dve", lambda e, xsb=xsb, xin=xin, rs=rs: e.scalar_tensor_tensor(
                            out=xsb, in0=xin, scalar=rs, in1=gam, op0=ALU.mult, op1=ALU.mult),
                            reads=[bx, self.b_rs[t], self.b_gam], writes=[self.b_xsb[t]], strict=True)
                    scales.append(sc)
                if defer_scale:
                    return scales
                for sc in scales:
                    sc()

            def norm1(self, blk, t):
                if True:
                    k = self.cnt % 2
                    self.cnt += 1
                    xin, bx = self.xin[k], self.b_xin[k]
                    sb = self.src_buf_fn(blk)
                    pg.dma("sp", ch_x[k], xin, self.src_fn(blk, t), reads=[sb] if sb else [], writes=[bx])
                    ss = self.ss[:, 16 * t:16 * t + 1]
                    rs = self.rs[:, 16 * t:16 * t + 1]
                    junk = self.junk
                    pg.op("dve", lambda e, ss=ss: e.memset(ss, 0.0), writes=[self.b_ss[t]])
                    pg.op("act", lambda e, xin=xin, ss=ss, junk=junk: e.activation(
                        out=junk, in_=xin, func=AF.Square, accum_out=ss),
                        reads=[bx], writes=[self.b_junk, self.b_ss[t]])
                    pg.op("act", lambda e, ss=ss, rs=rs: e.activation(
                        out=rs, in_=ss, func=AF.Sqrt, scale=1.0 / 1024, bias=EPS),
                        reads=[self.b_ss[t]], writes=[self.b_rs[t]], strict=True)
                    pg.op("dve", lambda e, rs=rs: e.reciprocal(out=rs, in_=rs),
                          reads=[self.b_rs[t]], writes=[self.b_rs[t]])
                    xsb = self.xsb[t]
                    gam = self.gam
                    pg.op("dve", lambda e, xsb=xsb, xin=xin, rs=rs, gam=gam: e.scalar_tensor_tensor(
                        out=xsb, in0=xin, scalar=rs, in1=gam, op0=ALU.mult, op1=ALU.mult),
                        reads=[bx, self.b_rs[t], self.b_gam], writes=[self.b_xsb[t]], strict=True)

            def tr(self, blk, pbanks, nsub=4):
                hT = self.hT
                for t in range(nsub):
                    pb = pbanks[t % 2]
                    p16 = bank16(pb)
                    xsb = self.xsb[t]
                    ident = self.ident

                    def f(e, p16=p16, xsb=xsb, ident=ident):
                        for c in range(8):
                            ins = e.transpose(p16[:, c * 128:(c + 1) * 128], xsb[:, c * 128:(c + 1) * 128], ident)
                        return ins

                    pg.op("pe", f, reads=[self.b_xsb[t], self.b_ident], writes=[PB[pb]])
                    dst = hT.rearrange("p (c n) -> p c n", c=8)[:, :, t * 128:(t + 1) * 128]
                    src = p16.rearrange("p (c n) -> p c n", c=8)
                    if t % 2 == 0:
                        pg.op("act", lambda e, dst=dst, src=src: e.activation(out=dst, in_=src, func=AF.Copy),
                              reads=[PB[pb]], writes=[self.b_hT])
                    else:
                        pg.op("dve", lambda e, dst=dst, src=src: e.tensor_copy(dst, src),
                              reads=[PB[pb]], writes=[self.b_hT])

        def load_weight_cast(dst16, src, kchunks, ncols, bufs_cols, chans):
            dv = dst16.rearrange("p (k n) -> p k n", k=kchunks)
            sv = src.rearrange("(k p) n -> p k n", p=128)
            for i, (b, c0, c1) in enumerate(bufs_cols):
                pg.dma("pool", chans[i % len(chans)], dv[:, :, c0:c1], sv[:, :, c0:c1], writes=[b])

        def ffn_phase(layer, which, blocks, src, dst_is_out=False, final=False):
            al.reset()
            wg = W["ffn%d_wg%d" % (which, layer)]
            wu = W["ffn%d_wu%d" % (which, layer)]
            wd = W["ffn%d_wd%d" % (which, layer)]
            wg16 = al.b16(8 * DFF)
            wu16 = al.b16(8 * DFF)
            wd16 = al.b16(NFF * D)
            bounds = [0, 128, 704, 1408, 2112, DFF]
            NCB = len(bounds) - 1
            b_wg = [Buf("wg%d" % i) for i in range(NCB)]
            b_wu = [Buf("wu%d" % i) for i in range(NCB)]
            b_wd = [Buf("wd%d" % i) for i in range(2)]
            for i in range(NCB):
                load_weight_cast(wg16, wg, 8, DFF, [(b_wg[i], bounds[i], bounds[i + 1])], [ch_wb[i]])
                load_weight_cast(wu16, wu, 8, DFF, [(b_wu[i], bounds[i], bounds[i + 1])], [ch_wb[5 + i]])

            def wblocks(m):
                return [i for i in range(NCB) if bounds[i] < (m + 1) * 128 and bounds[i + 1] > m * 128]
            for i in range(2):
                load_weight_cast(wd16, wd, NFF, D, [(b_wd[i], i * 512, (i + 1) * 512)], [ch_wb[10 + i]])

            def src_fn(blk, t):
                r0 = blk * 512 + t * 128
                return src[r0:r0 + 128, :]

            def src_buf(blk):
                return XS[blk] if src is xs else None

            nt = NormT("g_ffn%d_%d" % (which, layer), src_fn, src_buf)
            hid = al.b16(NFF * 512)
            b_hid = Buf("hid")
            sg = [al.f32(512) for _ in range(2)]
            b_sg = [Buf("sg%d" % i) for i in range(2)]
            xres = [al.f32(1024) for _ in range(2)]
            b_xres = [Buf("xres%d" % i) for i in range(2)]
            if final:
                gfin = al.f32(1024)
                b_gfin = Buf("gfin")
                o, w = CST["g_final"]
                pg.dma("sp", ch_c, gfin, cst[:, o:o + w], writes=[b_gfin])
                fss = al.f32(32)
                frs = al.f32(32)
                b_fss = [Buf("fss%d" % i) for i in range(2)]
                b_frs = [Buf("frs%d" % i) for i in range(2)]
                fjunk = nt.junk
                b_fjunk = nt.b_junk
            rescnt = [0]

            def gate_up(blk, nxt=None):
                hT = nt.hT
                for m in range(NFF):
                    if nxt is not None and m in (2, 7, 12, 17):
                        nt.norm1(nxt, (2, 7, 12, 17).index(m))
                    s = m % 2
                    pg_, pu_ = 2 + s, 4 + s
                    wbs = wblocks(m)

                    def fg(e, m=m, pb=pg_, w16=wg16):
                        for k in range(8):
                            ins = e.matmul(bank(pb), lhsT=w16[:, k * DFF + m * 128:k * DFF + (m + 1) * 128],
                                           rhs=hT[:, k * 512:(k + 1) * 512], start=(k == 0), stop=(k == 7))
                        return ins

                    pg.op("pe", fg, reads=[nt.b_hT] + [b_wg[i] for i in wbs], writes=[PB[pg_]])

                    def fu(e, m=m, pb=pu_, w16=wu16):
                        for k in range(8):
                            ins = e.matmul(bank(pb), lhsT=w16[:, k * DFF + m * 128:k * DFF + (m + 1) * 128],
                                           rhs=hT[:, k * 512:(k + 1) * 512], start=(k == 0), stop=(k == 7))
                        return ins

                    pg.op("pe", fu, reads=[nt.b_hT] + [b_wu[i] for i in wbs], writes=[PB[pu_]])
                    sgt = sg[s]
                    pg.op("act", lambda e, sgt=sgt, pb=pg_: e.activation(out=sgt, in_=bank(pb), func=AF.Silu),
                          reads=[PB[pg_]], writes=[b_sg[s]])
                    hslice = hid[:, m * 512:(m + 1) * 512]
                    pg.op("dve", lambda e, hslice=hslice, sgt=sgt, pb=pu_: e.tensor_tensor(
                        out=hslice, in0=sgt, in1=bank(pb), op=ALU.mult),
                        reads=[b_sg[s], PB[pu_]], writes=[b_hid])

            def down(blk):
                for t in range(4):
                    k = rescnt[0] % 2
                    rescnt[0] += 1
                    xr, bxr = xres[k], b_xres[k]
                    sb = src_buf(blk)
                    pg.dma("sp", ch_r[k], xr, src_fn(blk, t), reads=[sb] if sb else [], writes=[bxr])
                    for nh in range(2):
                        pb = 6 + nh

                        def fd(e, t=t, nh=nh, pb=pb):
                            for kf in range(NFF):
                                ins = e.matmul(bank(pb), lhsT=hid[:, kf * 512 + t * 128:kf * 512 + (t + 1) * 128],
                                               rhs=wd16[:, kf * D + nh * 512:kf * D + (nh + 1) * 512],
                                               start=(kf == 0), stop=(kf == NFF - 1))
                            return ins

                        pg.op("pe", fd, reads=[b_hid, b_wd[nh]], writes=[PB[pb]])
                        xsl = xr[:, nh * 512:(nh + 1) * 512]
                        pg.op("dve", lambda e, xsl=xsl, pb=pb: e.scalar_tensor_tensor(
                            out=xsl, in0=bank(pb), scalar=0.5, in1=xsl, op0=ALU.mult, op1=ALU.add),
                            reads=[PB[pb], bxr], writes=[bxr])
                    r0 = blk * 512 + t * 128
                    if final:
                        ss = fss[:, 16 * k:16 * k + 1]
                        rs = frs[:, 16 * k:16 * k + 1]
                        pg.op("dve", lambda e, ss=ss: e.memset(ss, 0.0), writes=[b_fss[k]])
                        pg.op("act", lambda e, xr=xr, ss=ss: e.activation(out=fjunk, in_=xr, func=AF.Square,
                                                                            accum_out=ss),
                              reads=[bxr], writes=[b_fjunk, b_fss[k]])
                        pg.op("act", lambda e, ss=ss, rs=rs: e.activation(out=rs, in_=ss, func=AF.Sqrt,
                                                                            scale=1.0 / 1024, bias=EPS),
                              reads=[b_fss[k]], writes=[b_frs[k]], strict=True)
                        pg.op("dve", lambda e, rs=rs: e.reciprocal(out=rs, in_=rs), reads=[b_frs[k]],
                              writes=[b_frs[k]])
                        pg.op("dve", lambda e, xr=xr, rs=rs: e.scalar_tensor_tensor(
                            out=xr, in0=xr, scalar=rs, in1=gfin, op0=ALU.mult, op1=ALU.mult),
                            reads=[bxr, b_frs[k], b_gfin], writes=[bxr], strict=True)
                        ro = (blk - 8) * 512 + t * 128
                        pg.dma("pool", ch_st[k], out[ro:ro + 128, :], xr, reads=[bxr])
                    else:
                        pg.dma("pool", ch_st[k], xs[r0:r0 + 128, :], xr, reads=[bxr], writes=[XS[blk]])

            nt.norm(blocks[0])
            nt.tr(blocks[0], (0, 1))
            if debug and which == 1 and layer == 0:
                pg.dma("sp", ch_dbg, DBG["hT"], nt.hT, reads=[nt.b_hT])
                pg.dma("sp", ch_dbg, DBG["xsb"], nt.xsb[0], reads=[nt.b_xsb[0]])
                pg.dma("sp", ch_dbg, DBG["rs"], nt.rs, reads=nt.b_rs)
                pg.dma("sp", ch_dbg, DBG["wg"], wg16, reads=b_wg)
            for i, blk in enumerate(blocks):
                nxt = blocks[i + 1] if i + 1 < len(blocks) else None
                gate_up(blk, nxt)
                if debug and which == 1 and layer == 0 and i == 0:
                    pg.dma("sp", ch_dbg, DBG["hid"], hid, reads=[b_hid])
                if nxt is not None:
                    nt.tr(nxt, (0, 1))
                down(blk)
            pg.barrier()

        def mm_group(pb, n, lhs_fn, rhs_fn, nk, reads, off=0):
            def f(e):
                for k in range(nk):
                    ins = e.matmul(bank(pb, n, off), lhsT=lhs_fn(k), rhs=rhs_fn(k), start=(k == 0), stop=(k == nk - 1))
                return ins
            return pg.op("pe", f, reads=reads, writes=[PB[pb]])

        cp_cnt = [0]

        def evac_copy(dst, pb, n, dbuf, off=0, eng=None):
            if eng is None:
                eng = "act" if cp_cnt[0] % 2 == 0 else "dve"
                cp_cnt[0] += 1
            src = bank(pb, n, off)
            if eng == "act":
                pg.op("act", lambda e: e.activation(out=dst, in_=src, func=AF.Copy), reads=[PB[pb]], writes=[dbuf])
            else:
                pg.op("dve", lambda e: e.tensor_copy(dst, src), reads=[PB[pb]], writes=[dbuf])

        def xs_src(blk, t):
            r0 = blk * 512 + t * 128
            return xs[r0:r0 + 128, :]

        def cst_load(name, buf, ncol=None):
            o, w = CST[name]
            tl = al.f32(w if ncol is None else ncol)
            pg.dma("sp", ch_c, tl[:, 0:w], cst[:, o:o + w], writes=[buf])
            return tl

        def proj_ab_phase():
            al.reset()
            NW = 2560
            w16 = al.b16(8 * NW)
            b_w = [Buf("wab%d" % i) for i in range(5)]
            for ci in (1, 2, 0, 3, 4):
                load_weight_cast(w16, W["ab_w_in"], 8, NW, [(b_w[ci], ci * 512, (ci + 1) * 512)], [ch_w[ci % 4]])
            nt = NormT("g_mix_0", xs_src, lambda blk: XS[blk], nslots=4)
            b_c = Buf("cst2")
            sgn = cst_load("sgnorm", b_c)
            sgb = cst_load("sgb", b_c)
            sgw16 = al.b16(512)
            b_sgw = Buf("sgw")
            pg.dma("pool", ch_w[0], sgw16, W["sgwT"], writes=[b_sgw])
            for g in range(4):
                sl = sgw16[64:128, g * 128:g * 128 + 64]
                pg.op("pool", lambda e, sl=sl: e.memset(sl, 0.0), reads=[b_sgw], writes=[b_sgw])
            kst = al.b16(4 * 512); b_kst = Buf("kst")
            qst = al.b16(4 * 512); b_qst = Buf("qst")
            gu = al.b16(4 * 512); b_gu = Buf("gu")
            vst = al.b16(4 * 512); b_vst = Buf("vst")
            obst = al.b16(4 * 512); b_obst = Buf("obst")
            gv = [al.f32(512) for _ in range(4)]; b_gv = [Buf("gv%d" % i) for i in range(4)]
            vgn = [al.b16(512) for _ in range(4)]; b_vgn = [Buf("vgn%d" % i) for i in range(4)]
            tmpg = [al.f32(512) for _ in range(2)]; b_tmpg = [Buf("tmpg%d" % i) for i in range(2)]
            gss = al.f32(64); grs = al.f32(64)
            b_gss = [Buf("gss%d" % i) for i in range(4)]; b_grs = [Buf("grs%d" % i) for i in range(4)]
            gjunk = nt.junk; b_gjunk = nt.b_junk
            hT = nt.hT
            fm_rot = [0]

            def fm_chunk(colbase, f, wb):
                pb = 2 + (fm_rot[0] % 2)
                fm_rot[0] += 1
                mm_group(pb, 512, lambda k: w16[:, k * NW + colbase + f * 128:k * NW + colbase + (f + 1) * 128],
                         lambda k: hT[:, k * 512:(k + 1) * 512], 8, [nt.b_hT, wb])
                return pb

            tm_rot = [0]

            def tm_tile(colbase, t, wb):
                pb = 4 + (tm_rot[0] % 2)
                tm_rot[0] += 1
                mm_group(pb, 512, lambda k: hT[:, k * 512 + t * 128:k * 512 + (t + 1) * 128],
                         lambda k: w16[:, k * NW + colbase:k * NW + colbase + 512], 8, [nt.b_hT, wb])
                return pb

            blocks = list(range(16))
            nt.norm(blocks[0])
            nt.tr(blocks[0], (0, 1))
            for i, blk in enumerate(blocks):
                nxt = blocks[i + 1] if i + 1 < len(blocks) else None
                full = blk >= 7
                if nxt is not None:
                    nt.load(nxt)
                for f in range(4):
                    pb = fm_chunk(512, f, b_w[1])
                    evac_copy(kst[:, f * 512:(f + 1) * 512], pb, 512, b_kst)
                pg.dma("pool", ch_sc[0], kT0[:, :, blk * 512:(blk + 1) * 512].rearrange("h p n -> p h n"),
                       kst.rearrange("p (h n) -> p h n", h=4), reads=[b_kst])
                for t in range(4):
                    pb = tm_tile(1024, t, b_w[2])
                    evac_copy(vst[:, t * 512:(t + 1) * 512], pb, 512, b_vst)
                pg.dma("pool", ch_sc[1], v0[blk * 512:(blk + 1) * 512, :].rearrange("(t p) c -> p t c", p=128),
                       vst.rearrange("p (t c) -> p t c", t=4), reads=[b_vst])
                if full:
                    lb = blk - 7
                    pg.op("dve", lambda e: e.memset(gss, 0.0), writes=b_gss)
                    for t in range(4):
                        pb = tm_tile(2048, t, b_w[4])
                        gvt = gv[t]
                        pg.op("act", lambda e, gvt=gvt, pb=pb: e.activation(out=gvt, in_=bank(pb), func=AF.Gelu_apprx_tanh),
                              reads=[PB[pb]], writes=[b_gv[t]])
                        ss = gss[:, 16 * t:16 * t + 1]
                        pg.op("act", lambda e, gvt=gvt, ss=ss: e.activation(out=gjunk[:, 0:512], in_=gvt, func=AF.Square,
                                                                             accum_out=ss),
                              reads=[b_gv[t]], writes=[b_gjunk, b_gss[t]])
                if nxt is not None:
                    nt.compute(nxt)
                if full:
                    for f in range(4):
                        pb = fm_chunk(0, f, b_w[0])
                        evac_copy(qst[:, f * 512:(f + 1) * 512], pb, 512, b_qst)
                    pg.dma("pool", ch_sc[0], qT0[:, :, lb * 512:(lb + 1) * 512].rearrange("h p n -> p h n"),
                           qst.rearrange("p (h n) -> p h n", h=4), reads=[b_qst])
                    for f in range(4):
                        pb = fm_chunk(1536, f, b_w[3])
                        dst = gu[:, f * 512:(f + 1) * 512]
                        pg.op("act", lambda e, dst=dst, pb=pb: e.activation(out=dst, in_=bank(pb), func=AF.Gelu_apprx_tanh),
                              reads=[PB[pb]], writes=[b_gu])
                    pg.op("act", lambda e: e.activation(out=grs, in_=gss, func=AF.Sqrt, scale=1.0 / 512, bias=EPS),
                          reads=b_gss, writes=b_grs, strict=True)
                    pg.op("dve", lambda e: e.reciprocal(out=grs, in_=grs), reads=b_grs, writes=b_grs)
                    for t in range(4):
                        gvt = gv[t]
                        rs = grs[:, 16 * t:16 * t + 1]
                        vg = vgn[t]
                        pg.op("dve", lambda e, vg=vg, gvt=gvt, rs=rs: e.scalar_tensor_tensor(
                            out=vg, in0=gvt, scalar=rs, in1=sgn, op0=ALU.mult, op1=ALU.mult),
                            reads=[b_gv[t], b_grs[t], b_c], writes=[b_vgn[t]], strict=True)
                    for t in range(4):
                        s2 = t % 2
                        vg = vgn[t]
                        gb = (6, 7)[s2]

                        def fgate(e, vg=vg, gb=gb):
                            for g in range(4):
                                ins = e.matmul(bank(gb, 128, g * 128), lhsT=vg[:, g * 128:(g + 1) * 128],
                                               rhs=sgw16[:, g * 128:(g + 1) * 128], start=True, stop=True)
                            return ins

                        pg.op("pe", fgate, reads=[b_vgn[t], b_sgw], writes=[PB[gb]])
                        tg = tmpg[s2]
                        pg.op("dve", lambda e, tg=tg, gb=gb: e.tensor_tensor(out=tg, in0=bank(gb), in1=sgb, op=ALU.add),
                              reads=[PB[gb], b_c], writes=[b_tmpg[s2]])
                        o3 = obst.rearrange("p (g n) -> p g n", g=4)[:, :, t * 128:(t + 1) * 128]
                        g3 = gu.rearrange("p (g n) -> p g n", g=4)[:, :, t * 128:(t + 1) * 128]
                        t3 = tg.rearrange("p (g n) -> p g n", g=4)
                        pg.op("dve", lambda e, o3=o3, g3=g3, t3=t3: e.tensor_tensor(out=o3, in0=t3, in1=g3, op=ALU.mult),
                              reads=[b_tmpg[s2], b_gu], writes=[b_obst])
                    pg.dma("pool", ch_sc[1], obT[:, :, lb * 512:(lb + 1) * 512].rearrange("h p n -> p h n"),
                           obst.rearrange("p (h n) -> p h n", h=4), reads=[b_obst])
                if nxt is not None:
                    nt.tr(nxt, (0, 1))
            pg.barrier()

        def attn_ab_phase():
            al.reset()
            kT = al.b16(4 * S_E)
            V = al.b16(64 * 512)
            wo16 = al.b16(8 * D)
            b_kv = [Buf("kv%d" % g) for g in range(4)]
            b_wo = Buf("wo")
            kT3 = kT.rearrange("p (h n) -> p h n", h=4)
            V3 = V.rearrange("p (k c) -> p k c", k=64)
            for g in range(4):
                pg.dma("sp", ch_m[g % 4], kT3[:, :, g * 2048:(g + 1) * 2048],
                       kT0[:, :, g * 2048:(g + 1) * 2048].rearrange("h p n -> p h n"), writes=[b_kv[g]])
                pg.dma("sp", ch_m[g % 4], V3[:, g * 16:(g + 1) * 16, :],
                       v0[g * 2048:(g + 1) * 2048, :].rearrange("(k p) c -> p k c", p=128), writes=[b_kv[g]])
            load_weight_cast(wo16, W["ab_w_out"], 8, D, [(b_wo, 0, D)], [ch_w[0]])
            b_c = Buf("cst3")
            lamt = cst_load("lam", b_c)
            subln = cst_load("subln", b_c, 16)
            maskb = cst_load("maskb", b_c, 16)
            ones16 = al.b16(128); b_ones = Buf("ones")
            pg.op("pool", lambda e: e.memset(ones16, 1.0), writes=[b_ones])
            prod = al.f32(128); ssum = al.f32(32); ee = al.f32(32); nl = al.f32(16); sublnS = al.f32(16)
            b_l = Buf("lamc")
            pg.op("dve", lambda e: e.memset(ssum, 0.0), writes=[b_l])
            pg.op("dve", lambda e: e.scalar_tensor_tensor(out=prod[:, 0:64], in0=lamt[:, 0:64], scalar=1.0, in1=lamt[:, 64:128],
                                                            op0=ALU.mult, op1=ALU.mult, accum_out=ssum[:, 0:1]),
                  reads=[b_c], writes=[b_l], strict=True)
            pg.op("dve", lambda e: e.scalar_tensor_tensor(out=prod[:, 64:128], in0=lamt[:, 128:192], scalar=1.0, in1=lamt[:, 192:256],
                                                            op0=ALU.mult, op1=ALU.mult, accum_out=ssum[:, 16:17]),
                  reads=[b_c], writes=[b_l])
            b_l2 = Buf("lamc2")
            pg.op("act", lambda e: e.activation(out=ee[:, 0:1], in_=ssum[:, 0:1], func=AF.Exp), reads=[b_l], writes=[b_l2])
            pg.op("act", lambda e: e.activation(out=ee[:, 16:17], in_=ssum[:, 16:17], func=AF.Exp), reads=[b_l], writes=[b_l2])
            b_l3 = Buf("lamc3")
            pg.op("dve", lambda e: e.tensor_tensor(out=nl[:, 0:1], in0=ee[:, 16:17], in1=ee[:, 0:1], op=ALU.subtract),
                  reads=[b_l2], writes=[b_l3])
            pg.op("dve", lambda e: e.tensor_scalar(out=nl[:, 0:1], in0=nl[:, 0:1], scalar1=-0.2, scalar2=None, op0=ALU.add),
                  reads=[b_l3], writes=[b_l3], strict=True)
            pg.op("dve", lambda e: e.tensor_scalar(out=sublnS[:, 0:1], in0=subln[:, 0:1], scalar1=0.8, scalar2=None, op0=ALU.mult),
                  reads=[b_c], writes=[b_l3])
            qT = [al.b16(4 * 512) for _ in range(2)]; b_qT = [Buf("qT%d" % i) for i in range(2)]
            obt1 = al.b16(4 * 512); b_obt1 = Buf("obt")
            obt = [obt1, obt1]; b_obt = [b_obt1, b_obt1]
            NP = 5
            Pt = [al.b16(1024) for _ in range(NP)]; b_P = [Buf("P%d" % i) for i in range(NP)]
            cat = al.b16(4 * 512); b_cat = Buf("cat")
            rec = al.f32(1024); b_rec = Buf("rec")
            t1 = al.f32(512); t2 = al.f32(512); rstd = t1
            oa4 = [al.f32(512) for _ in range(4)]
            b_t = Buf("t12"); b_oa = [Buf("oa%d" % i) for i in range(4)]; b_rstd = b_t
            sq4 = [al.b16(512) for _ in range(4)]; b_sq = [Buf("sq%d" % i) for i in range(4)]
            lnS = rec; b_lnS = b_rec
            xres2 = [al.f32(1024) for _ in range(2)]; b_xres2 = [Buf("xres%d" % i) for i in range(2)]
            xres = xres2 + xres2; b_xres = b_xres2 + b_xres2
            acc = al.f32(1024); b_acc = Buf("acc")
            ones32 = al.f32(128); b_ones32 = Buf("ones32")
            pg.op("pool", lambda e: e.memset(ones32, 1.0), writes=[b_ones32])
            ucnt = [0]
            rcnt = [0]

            def sslot(k):
                return (4 + 2 * (k % 2), 5 + 2 * (k % 2))

            blocks = list(range(7, 16))
            DEPTH = 2

            def make_wout(blk):
                groups = []
                for t in range(4):
                    for nh in range(2):
                        def g(t=t, nh=nh, blk=blk):
                            kx = t % 2
                            xr, bxr = xres2[kx], b_xres2[kx]
                            if nh == 0:
                                pg.dma("sp", ch_r[kx], xr, xs_src(blk, t), reads=[XS[blk]], writes=[bxr])
                            pb = 2 + nh

                            def fo(e, t=t, nh=nh, pb=pb):
                                for c in range(8):
                                    src = cat if c < 4 else obt1
                                    cc = c % 4
                                    ins = e.matmul(bank(pb), lhsT=src[:, cc * 512 + t * 128:cc * 512 + (t + 1) * 128],
                                                   rhs=wo16[:, c * D + nh * 512:c * D + (nh + 1) * 512],
                                                   start=(c == 0), stop=(c == 7))
                                return ins

                            pg.op("pe", fo, reads=[b_cat, b_obt1, b_wo], writes=[PB[pb]])
                            xsl = xr[:, nh * 512:(nh + 1) * 512]
                            pg.op("dve", lambda e, xsl=xsl, pb=pb: e.tensor_tensor(out=xsl, in0=bank(pb), in1=xsl, op=ALU.add),
                                  reads=[PB[pb], bxr], writes=[bxr])
                            if nh == 1:
                                r0 = blk * 512 + t * 128
                                pg.dma("pool", ch_st[kx], xs[r0:r0 + 128, :], xr, reads=[bxr], writes=[XS[blk]])
                        groups.append(g)
                return groups

            pending = None
            for bi, blk in enumerate(blocks):
                lb = blk - 7
                qs = bi % 2
                q = qT[qs]
                pg.dma("sp", ch_x[qs], q.rearrange("p (h n) -> p h n", h=4),
                       qT0[:, :, lb * 512:(lb + 1) * 512].rearrange("h p n -> p h n"), writes=[b_qT[qs]])
                nkt = 4 * (blk + 1)
                units = [(h, kt) for h in range(4) for kt in range(nkt)]
                base = ucnt[0]
                wgroups = make_wout(pending) if pending is not None else []

                def stage_a(u):
                    h, kt = units[u]
                    k = base + u
                    b0, b1 = sslot(k)
                    kvb = b_kv[kt // 16]

                    def f(e, h=h, kt=kt, b0=b0, b1=b1, q=q):
                        e.matmul(bank(b0), lhsT=kT[0:64, h * S_E + kt * 128:h * S_E + (kt + 1) * 128],
                                 rhs=q[0:64, h * 512:(h + 1) * 512], start=True, stop=True)
                        return e.matmul(bank(b1), lhsT=kT[64:128, h * S_E + kt * 128:h * S_E + (kt + 1) * 128],
                                        rhs=q[64:128, h * 512:(h + 1) * 512], start=True, stop=True)

                    pg.op("pe", f, reads=[kvb, b_qT[qs]], writes=[PB[b0], PB[b1]])
                    ps2 = psum[:, b0 * 512:b0 * 512 + 1024]
                    P = Pt[k % NP]; bP = b_P[k % NP]
                    j = kt - 4 * blk
                    bias = maskb[:, 0:1] if kt < 32 else 0.0
                    if j <= 0:
                        pg.op("act", lambda e, P=P, ps2=ps2, bias=bias: e.activation(out=P, in_=ps2, func=AF.Exp,
                                                                                     scale=0.125, bias=bias),
                              reads=[PB[b0], PB[b1], b_c], writes=[bP])
                    else:
                        P3 = P.rearrange("p (m n) -> p m n", m=2)
                        s3 = ps2.rearrange("p (m n) -> p m n", m=2)
                        pg.op("act", lambda e, P3=P3, s3=s3, j=j, bias=bias: e.activation(
                            out=P3[:, :, 128 * j:512], in_=s3[:, :, 128 * j:512], func=AF.Exp, scale=0.125, bias=bias),
                            reads=[PB[b0], PB[b1], b_c], writes=[bP])
                        pg.op("pool", lambda e, P3=P3, j=j: e.memset(P3[:, :, 0:128 * j], 0.0), writes=[bP])
                    if j >= 0:
                        P3 = P.rearrange("p (m n) -> p m n", m=2)
                        pg.op("pool", lambda e, P3=P3, j=j: e.memset(P3[64:128, :, 128 * j:128 * j + 64], 0.0),
                              reads=[bP], writes=[bP])
                    if debug and bi == 0 and h == 3 and kt in (0, 1, 28, 31):
                        pg.dma("sp", ch_dbg, DBG["P"][(0, 1, 28, 31).index(kt)], P, reads=[bP])

                def stage_b_pv(u):
                    h, kt = units[u]
                    k = base + u
                    P = Pt[k % NP]; bP = b_P[k % NP]
                    kvb = b_kv[kt // 16]
                    first = (kt == 0)
                    last = (kt == nkt - 1)

                    def fpv(e, h=h, kt=kt, P=P, first=first, last=last):
                        vv = V[:, kt * 512 + h * 128:kt * 512 + (h + 1) * 128]
                        e.matmul(bank(0), lhsT=vv, rhs=P[:, 0:512], start=first, stop=last)
                        return e.matmul(bank(1), lhsT=vv, rhs=P[:, 512:1024], start=first, stop=last)

                    pg.op("pe", fpv, reads=[bP, kvb], writes=[PB[0], PB[1]])

                def stage_b_add(u):
                    h, kt = units[u]
                    k = base + u
                    P = Pt[k % NP]; bP = b_P[k % NP]
                    first = (kt == 0)
                    last = (kt == nkt - 1)
                    if first:
                        pg.op("dve", lambda e, P=P: e.tensor_copy(acc, P), reads=[bP], writes=[b_acc])
                    else:
                        pg.op("dve", lambda e, P=P: e.tensor_tensor(out=acc, in0=acc, in1=P, op=ALU.add),
                              reads=[bP, b_acc], writes=[b_acc])
                    if last:
                        def fsum(e):
                            e.matmul(bank(2), lhsT=ones32, rhs=acc[:, 0:512], start=True, stop=True)
                            return e.matmul(bank(3), lhsT=ones32, rhs=acc[:, 512:1024], start=True, stop=True)

                        pg.op("pe", fsum, reads=[b_acc, b_ones32], writes=[PB[2], PB[3]])
                        oa = oa4[h]
                        sq16 = sq4[h]
                        pg.op("act", lambda e: e.activation(out=lnS, in_=psum[:, 2 * 512:4 * 512], func=AF.Ln),
                              reads=[PB[2], PB[3]], writes=[b_lnS])
                        pg.op("act", lambda e: e.activation(out=rec, in_=lnS, func=AF.Exp, scale=-1.0),
                              reads=[b_lnS], writes=[b_rec])

                        def fin(h=h, oa=oa, sq16=sq16):
                            pg.op("dve", lambda e: e.tensor_tensor(out=t1, in0=bank(0), in1=rec[:, 0:512], op=ALU.mult),
                                  reads=[PB[0], b_rec], writes=[b_t])
                            pg.op("dve", lambda e: e.tensor_tensor(out=t2, in0=bank(1), in1=rec[:, 512:1024], op=ALU.mult),
                                  reads=[PB[1], b_rec], writes=[b_t])
                            pg.op("dve", lambda e, oa=oa: e.scalar_tensor_tensor(out=oa, in0=t2, scalar=nl[:, 0:1], in1=t1,
                                                                                  op0=ALU.mult, op1=ALU.add),
                                  reads=[b_t, b_l3], writes=[b_oa[h]])
                            pg.op("dve", lambda e, oa=oa, sq16=sq16: e.tensor_tensor(out=sq16, in0=oa, in1=oa, op=ALU.mult),
                                  reads=[b_oa[h]], writes=[b_sq[h]])
                        deferred.append([2, fin])

                nu = len(units)
                deferred = []
                pvq = []
                for step in range(nu + DEPTH):
                    if step < nu:
                        stage_a(step)
                    for d in deferred:
                        d[0] -= 1
                    while deferred and deferred[0][0] <= 0:
                        deferred.pop(0)[1]()
                    if step >= DEPTH:
                        pvq.append(step - DEPTH)
                        if not deferred:
                            while pvq:
                                stage_b_pv(pvq.pop(0))
                        stage_b_add(step - DEPTH)
                    if wgroups and step >= 6 and (step - 6) % 6 == 0:
                        wgroups.pop(0)()
                while deferred:
                    deferred.pop(0)[1]()
                while pvq:
                    stage_b_pv(pvq.pop(0))
                while wgroups:
                    wgroups.pop(0)()
                ucnt[0] += nu
                pg.dma("sp", ch_x[qs], obt1.rearrange("p (h n) -> p h n", h=4),
                       obT[:, :, lb * 512:(lb + 1) * 512].rearrange("h p n -> p h n"), writes=[b_obt1])
                for h in range(4):
                    ucnt[0] += 1
                    mb = sslot(ucnt[0])[0]
                    oa = oa4[h]
                    sq16 = sq4[h]
                    pg.op("pe", lambda e, mb=mb, sq16=sq16: e.matmul(bank(mb), lhsT=ones16, rhs=sq16, start=True, stop=True),
                          reads=[b_sq[h], b_ones], writes=[PB[mb]])
                    pg.op("act", lambda e, mb=mb: e.activation(out=rstd, in_=bank(mb), func=AF.Ln, scale=1.0 / 128, bias=EPS),
                          reads=[PB[mb]], writes=[b_rstd])
                    pg.op("act", lambda e: e.activation(out=rstd, in_=rstd, func=AF.Exp, scale=-0.5),
                          reads=[b_rstd], writes=[b_rstd])
                    ch = cat[:, h * 512:(h + 1) * 512]
                    pg.op("dve", lambda e, ch=ch, oa=oa: e.scalar_tensor_tensor(out=ch, in0=oa, scalar=sublnS[:, 0:1], in1=rstd,
                                                                                 op0=ALU.mult, op1=ALU.mult),
                          reads=[b_oa[h], b_rstd, b_l3], writes=[b_cat])
                pending = blk
            for g in make_wout(pending):
                g()
            pg.barrier()

        def xattn_phase(layer, blocks):
            al.reset()
            wq16 = al.b16(8 * D); wkv16 = al.b16(8 * 2 * D); wo16 = al.b16(8 * D)
            b_wq = Buf("wq"); b_wkv = [Buf("wkvk"), Buf("wkvv")]; b_wo = Buf("wo")
            load_weight_cast(wkv16, W["xa_wkv%d" % layer], 8, 2 * D, [(b_wkv[0], 0, D)], [ch_w[0]])
            load_weight_cast(wq16, W["xa_wq%d" % layer], 8, D, [(b_wq, 0, D)], [ch_w[2]])
            load_weight_cast(wkv16, W["xa_wkv%d" % layer], 8, 2 * D, [(b_wkv[1], D, 2 * D)], [ch_w[1]])
            load_weight_cast(wo16, W["xa_wo%d" % layer], 8, D, [(b_wo, 0, D)], [ch_w[3]])
            ntm = NormT("g_mem_%d" % layer, lambda blk, t: mem[t * 128:(t + 1) * 128, :], lambda blk: None, width=256)
            nt = NormT("g_xa_%d" % layer, xs_src, lambda blk: XS[blk], nslots=4)
            ones16 = al.b16(128); b_ones = Buf("ones")
            pg.op("pool", lambda e: e.memset(ones16, 1.0), writes=[b_ones])
            kxT = al.b16(8 * 256); b_kx = Buf("kxT")
            Vx = al.b16(2 * D); b_vx = Buf("Vx")
            qT = al.b16(8 * 512); b_qT = Buf("qTx")
            oT = al.b16(8 * 512); b_oT = Buf("oTx")
            Pt = [al.b16(1024) for _ in range(2)]; b_P = [Buf("Px%d" % i) for i in range(2)]
            rec = al.f32(512); b_rec = Buf("recx")
            xres = [al.f32(1024) for _ in range(2)]; b_xres = [Buf("xres%d" % i) for i in range(2)]
            ntm.norm(0, nsub=2)
            ntm.tr(0, (0, 1), nsub=2)
            memT = ntm.hT
            prot = [0]

            def pbank():
                pb = (7, 6)[prot[0] % 2]
                prot[0] += 1
                return pb

            for f in range(8):
                pb = pbank()
                mm_group(pb, 256, lambda k, f=f: wkv16[:, k * 2 * D + f * 128:k * 2 * D + (f + 1) * 128],
                         lambda k: memT[:, k * 256:(k + 1) * 256], 8, [ntm.b_hT, b_wkv[0]])
                evac_copy(kxT[:, f * 256:(f + 1) * 256], pb, 256, b_kx)
            for mt in range(2):
                for nh in range(2):
                    pb = pbank()
                    mm_group(pb, 512, lambda k, mt=mt: memT[:, k * 256 + mt * 128:k * 256 + (mt + 1) * 128],
                             lambda k, nh=nh: wkv16[:, k * 2 * D + D + nh * 512:k * 2 * D + D + (nh + 1) * 512], 8,
                             [ntm.b_hT, b_wkv[1]])
                    evac_copy(Vx[:, mt * D + nh * 512:mt * D + (nh + 1) * 512], pb, 512, b_vx)
            hT = nt.hT
            scnt = [0]
            lnS = al.f32(512); b_lnS = Buf("lnS")
            xres4 = xres + [al.f32(1024) for _ in range(2)]
            b_xres4 = b_xres + [Buf("xres%d" % i) for i in range(2, 4)]
            nt.norm(blocks[0])
            nt.tr(blocks[0], (0, 1))
            for i, blk in enumerate(blocks):
                nxt = blocks[i + 1] if i + 1 < len(blocks) else None
                for t in range(4):
                    pg.dma("sp", ch_r[t % 2], xres4[t], xs_src(blk, t), reads=[XS[blk]], writes=[b_xres4[t]])
                if nxt is not None:
                    nt.load(nxt)
                for f in range(8):
                    pb = pbank()
                    mm_group(pb, 512, lambda k, f=f: wq16[:, k * D + f * 128:k * D + (f + 1) * 128],
                             lambda k: hT[:, k * 512:(k + 1) * 512], 8, [nt.b_hT, b_wq])
                    evac_copy(qT[:, f * 512:(f + 1) * 512], pb, 512, b_qT)

                def emit_qk(h):
                    sl = (scnt[0] + h) % 2
                    sb0 = (2, 0)[sl]

                    def fs(e, h=h, sb0=sb0):
                        for mt in range(2):
                            for c in range(2):
                                ins = e.matmul(bank(sb0 + mt), lhsT=kxT[:, (2 * h + c) * 256 + mt * 128:(2 * h + c) * 256 + (mt + 1) * 128],
                                               rhs=qT[:, (2 * h + c) * 512:(2 * h + c + 1) * 512], start=(c == 0), stop=(c == 1))
                        return ins

                    pg.op("pe", fs, reads=[b_kx, b_qT], writes=[PB[sb0], PB[sb0 + 1]])
                    ps2 = psum[:, sb0 * 512:sb0 * 512 + 1024]
                    P = Pt[sl]; bP = b_P[sl]
                    pg.op("act", lambda e, P=P, ps2=ps2: e.activation(out=P, in_=ps2, func=AF.Exp, scale=1.0 / 16),
                          reads=[PB[sb0], PB[sb0 + 1]], writes=[bP])

                emit_qk(0)
                for h in range(4):
                    sl = (scnt[0] + h) % 2
                    if h + 1 < 4:
                        emit_qk(h + 1)
                    P = Pt[sl]; bP = b_P[sl]

                    def fo(e, h=h, P=P):
                        for ei in range(2):
                            for mt in range(2):
                                e.matmul(bank(4 + ei), lhsT=Vx[:, mt * D + h * 256 + ei * 128:mt * D + h * 256 + (ei + 1) * 128],
                                         rhs=P[:, mt * 512:(mt + 1) * 512], start=(mt == 0), stop=(mt == 1))
                        for mt in range(2):
                            ins = e.matmul(bank(6), lhsT=ones16, rhs=P[:, mt * 512:(mt + 1) * 512], start=(mt == 0), stop=(mt == 1))
                        return ins

                    pg.op("pe", fo, reads=[bP, b_vx, b_ones], writes=[PB[4], PB[5], PB[6]])
                    pg.op("act", lambda e: e.activation(out=lnS, in_=bank(6), func=AF.Ln), reads=[PB[6]], writes=[b_lnS])
                    pg.op("act", lambda e: e.activation(out=rec, in_=lnS, func=AF.Exp, scale=-1.0), reads=[b_lnS], writes=[b_rec])
                    for ei in range(2):
                        dst = oT[:, (2 * h + ei) * 512:(2 * h + ei + 1) * 512]
                        pg.op("dve", lambda e, dst=dst, ei=ei: e.tensor_tensor(out=dst, in0=bank(4 + ei), in1=rec, op=ALU.mult),
                              reads=[PB[4 + ei], b_rec], writes=[b_oT])
                scnt[0] += 4
                scales = nt.compute(nxt, defer_scale=True) if nxt is not None else []
                for t in range(4):
                    xr, bxr = xres4[t], b_xres4[t]
                    for nh in range(2):
                        pb = pbank()
                        mm_group(pb, 512, lambda c, t=t: oT[:, c * 512 + t * 128:c * 512 + (t + 1) * 128],
                                 lambda c, nh=nh: wo16[:, c * D + nh * 512:c * D + (nh + 1) * 512], 8, [b_oT, b_wo])
                        xsl = xr[:, nh * 512:(nh + 1) * 512]
                        pg.op("dve", lambda e, xsl=xsl, pb=pb: e.tensor_tensor(out=xsl, in0=bank(pb), in1=xsl, op=ALU.add),
                              reads=[PB[pb], bxr], writes=[bxr])
                    r0 = blk * 512 + t * 128
                    pg.dma("pool", ch_st[t % 2], xs[r0:r0 + 128, :], xr, reads=[bxr], writes=[XS[blk]])
                    if scales:
                        scales[t]()
                if nxt is not None:
                    nt.tr(nxt, (0, 1))
            pg.barrier()

        def proj_c_phase():
            al.reset()
            NW = 3072
            w16 = al.b16(8 * NW)
            b_w = [Buf("wc%d" % i) for i in range(3)]
            for ci in (1, 2, 0):
                load_weight_cast(w16, W["c_w_in"], 8, NW, [(b_w[ci], ci * D, ci * D + 512)], [ch_w[ci]])
                load_weight_cast(w16, W["c_w_in"], 8, NW, [(b_w[ci], ci * D + 512, (ci + 1) * D)], [ch_w[ci]])
            nt = NormT("g_mix_1", xs_src, lambda blk: XS[blk], nslots=4)
            hT = nt.hT
            kst = al.b16(8 * 512); b_kst = Buf("kst")
            qst = al.b16(8 * 512); b_qst = Buf("qst")
            vst = al.b16(4 * 2 * D); b_vst = Buf("vst")
            pg.op("pool", lambda e: e.memset(vst, 1.0), writes=[b_vst])
            rot = [0]

            def pbk():
                pb = 2 + rot[0] % 4
                rot[0] += 1
                return pb

            blocks = list(range(7, 16))
            nt.norm(blocks[0])
            nt.tr(blocks[0], (0, 1))
            for i, blk in enumerate(blocks):
                lb = blk - 7
                nxt = blocks[i + 1] if i + 1 < len(blocks) else None
                if nxt is not None:
                    nt.load(nxt)
                for f in range(8):
                    if nxt is not None and f == 4:
                        nt.compute(nxt)
                    pb = pbk()
                    mm_group(pb, 512, lambda k, f=f: w16[:, k * NW + D + f * 128:k * NW + D + (f + 1) * 128],
                             lambda k: hT[:, k * 512:(k + 1) * 512], 8, [nt.b_hT, b_w[1]])
                    evac_copy(kst[:, f * 512:(f + 1) * 512], pb, 512, b_kst)
                pg.dma("pool", ch_sc[0], kT1[:, :, lb * 512:(lb + 1) * 512].rearrange("h p n -> p h n"),
                       kst.rearrange("p (h n) -> p h n", h=8), reads=[b_kst])
                for t in range(4):
                    for nh in range(2):
                        pb = pbk()
                        mm_group(pb, 512, lambda k, t=t: hT[:, k * 512 + t * 128:k * 512 + (t + 1) * 128],
                                 lambda k, nh=nh: w16[:, k * NW + 2 * D + nh * 512:k * NW + 2 * D + (nh + 1) * 512], 8,
                                 [nt.b_hT, b_w[2]])
                        vdst = vst[:, t * 2 * D + nh * D:t * 2 * D + (nh + 1) * D].rearrange("p (h c) -> p h c", h=8)[:, :, 0:64]
                        vsrc = bank(pb).rearrange("p (h c) -> p h c", h=8)
                        if (t + nh) % 2 == 0:
                            pg.op("act", lambda e, vdst=vdst, vsrc=vsrc: e.activation(out=vdst, in_=vsrc, func=AF.Copy),
                                  reads=[PB[pb]], writes=[b_vst])
                        else:
                            pg.op("dve", lambda e, vdst=vdst, vsrc=vsrc: e.tensor_copy(vdst, vsrc), reads=[PB[pb]], writes=[b_vst])
                pg.dma("pool", ch_sc[1], v1[lb * 512:(lb + 1) * 512, :].rearrange("(t p) c -> p t c", p=128),
                       vst.rearrange("p (t c) -> p t c", t=4), reads=[b_vst])
                if blk >= 8:
                    for f in range(8):
                        pb = pbk()
                        mm_group(pb, 512, lambda k, f=f: w16[:, k * NW + f * 128:k * NW + (f + 1) * 128],
                                 lambda k: hT[:, k * 512:(k + 1) * 512], 8, [nt.b_hT, b_w[0]])
                        evac_copy(qst[:, f * 512:(f + 1) * 512], pb, 512, b_qst)
                    pg.dma("pool", ch_sc[0], qT1[:, :, lb * 512:(lb + 1) * 512].rearrange("h p n -> p h n"),
                           qst.rearrange("p (h n) -> p h n", h=8), reads=[b_qst])
                if nxt is not None:
                    nt.tr(nxt, (0, 1))
            pg.barrier()

        def band_phase():
            al.reset()
            wo = al.b16(8 * D)
            b_wo = Buf("woc")
            load_weight_cast(wo, W["c_w_out"], 8, D, [(b_wo, 0, D)], [ch_w[0]])
            relb = al.f32(16 * 640); b_relb = Buf("relb")
            for g in range(4):
                pg.dma("sp", ch_c, relb[:, g * 2560:(g + 1) * 2560], W["relb"][:, g * 2560:(g + 1) * 2560], writes=[b_relb])
            relb3 = relb.rearrange("p (h m) -> p h m", h=16)
            pg.op("pool", lambda e: e.memset(relb3[0:64, :, 576:640], -30000.0), reads=[b_relb], writes=[b_relb])
            pg.op("pool", lambda e: e.memset(relb3[64:128, :, 0:64], -30000.0), reads=[b_relb], writes=[b_relb])
            b_c = Buf("cst8")
            maskb = cst_load("maskb", b_c, 16)
            qTb = al.b16(8 * 512); b_q = Buf("qb")
            kTb = al.b16(8 * 1024); b_k = Buf("kb")
            Vb = al.b16(8 * 2 * D); b_v = Buf("vb")
            oT = al.b16(8 * 512); b_oT = Buf("oTc")
            NS = 6
            DEPTH = 4
            sbt = [al.f32(1024) for _ in range(NS)]; b_sb = [Buf("sb%d" % i) for i in range(NS)]
            Pt = [al.b16(1024) for _ in range(NS)]; b_P = [Buf("Pc%d" % i) for i in range(NS)]
            rec = [al.f32(1024) for _ in range(2)]; b_rec = [Buf("recc%d" % i) for i in range(2)]
            xres2 = [al.f32(1024) for _ in range(2)]; b_xres2 = [Buf("xres%d" % i) for i in range(2)]
            xres = xres2 + xres2; b_xres = b_xres2 + b_xres2
            ucnt = [0]
            KT_ORDER = (3, 4, 2, 5, 1, 6, 0, 7)
            blocks = list(range(8, 16))

            def load_blk(bi):
                blk = blocks[bi]
                li = blk - 8
                pg.dma("sp", ch_m[0], qTb.rearrange("p (f n) -> p f n", f=8),
                       qT1[:, :, (li + 1) * 512:(li + 2) * 512].rearrange("f p n -> p f n"), writes=[b_q])
                pg.dma("sp", ch_m[1], kTb.rearrange("p (f n) -> p f n", f=8),
                       kT1[:, :, li * 512:li * 512 + 1024].rearrange("f p n -> p f n"), writes=[b_k])
                pg.dma("sp", ch_m[2], Vb.rearrange("p (k c) -> p k c", k=8),
                       v1[li * 512:li * 512 + 1024, :].rearrange("(k p) c -> p k c", p=128), writes=[b_v])

            def geom(kt):
                cq_lo = max(0, 2 * kt - 8)
                cq_hi = min(7, 2 * kt + 1)
                n0 = 64 * cq_lo
                n1 = 64 * (cq_hi + 1)
                return n0, n1, n1 - n0, n0 + 512 - 128 * kt

            load_blk(0)
            for bi, blk in enumerate(blocks):
                for t in range(2):
                    pg.dma("sp", ch_r[t % 2], xres[t], xs_src(blk, t), reads=[XS[blk]], writes=[b_xres[t]])
                units = [(f, oi, kt) for f in range(8) for oi, kt in enumerate(KT_ORDER)]
                base = ucnt[0]

                def stage_a(u):
                    f, oi, kt = units[u]
                    g = base + u
                    n0, n1, N, m0 = geom(kt)
                    sb0 = (0, 2)[g % 2]
                    sl = g % NS

                    def fqk(e, kt=kt, n0=n0, n1=n1, N=N, sb0=sb0, f=f):
                        e.matmul(bank(sb0, N), lhsT=kTb[0:64, f * 1024 + kt * 128:f * 1024 + (kt + 1) * 128],
                                 rhs=qTb[0:64, f * 512 + n0:f * 512 + n1], start=True, stop=True)
                        return e.matmul(bank(sb0 + 1, N), lhsT=kTb[64:128, f * 1024 + kt * 128:f * 1024 + (kt + 1) * 128],
                                        rhs=qTb[64:128, f * 512 + n0:f * 512 + n1], start=True, stop=True)

                    pg.op("pe", fqk, reads=[b_k, b_q], writes=[PB[sb0], PB[sb0 + 1]])
                    sb3 = sbt[sl].rearrange("p (m n) -> p m n", m=2)[:, :, 0:N]
                    ps3 = psum[:, sb0 * 512:sb0 * 512 + 1024].rearrange("p (m n) -> p m n", m=2)[:, :, 0:N]
                    r3 = relb[:, 2 * f * 640:(2 * f + 2) * 640].rearrange("p (m n) -> p m n", m=2)[:, :, m0:m0 + N]
                    pg.op("dve", lambda e, sb3=sb3, ps3=ps3, r3=r3: e.scalar_tensor_tensor(
                        out=sb3, in0=ps3, scalar=0.125, in1=r3, op0=ALU.mult, op1=ALU.add),
                        reads=[PB[sb0], PB[sb0 + 1], b_relb], writes=[b_sb[sl]])
                    P3 = Pt[sl].rearrange("p (m n) -> p m n", m=2)[:, :, 0:N]
                    bias = maskb[:, 0:1] if (blk == 8 and kt < 4) else 0.0
                    pg.op("act", lambda e, P3=P3, sb3=sb3, bias=bias: e.activation(out=P3, in_=sb3, func=AF.Exp, bias=bias),
                          reads=[b_sb[sl], b_c], writes=[b_P[sl]])

                def stage_b(u):
                    f, oi, kt = units[u]
                    g = base + u
                    n0, n1, N, m0 = geom(kt)
                    sl = g % NS
                    P = Pt[sl]
                    fs = f % 2
                    pb0 = 4 + 2 * fs
                    first = (oi == 0)
                    last = (oi == 7)

                    def fpv(e, kt=kt, f=f, P=P, N=N, n0=n0, first=first, last=last, pb0=pb0):
                        for par in range(2):
                            h = 2 * f + par
                            ins = e.matmul(bank(pb0 + par, N, n0), lhsT=Vb[:, kt * 2 * D + h * 128:kt * 2 * D + (h + 1) * 128],
                                           rhs=P[:, par * 512:par * 512 + N], start=first, stop=last)
                        return ins

                    pg.op("pe", fpv, reads=[b_P[sl], b_v], writes=[PB[pb0], PB[pb0 + 1]])
                    if last:
                        rc = rec[fs]
                        pg.op("act", lambda e, rc=rc, pb0=pb0: e.activation(out=rc[0:64, :], in_=psum[64:128, pb0 * 512:pb0 * 512 + 1024],
                                                                              func=AF.Ln),
                              reads=[PB[pb0], PB[pb0 + 1]], writes=[b_rec[fs]])
                        pg.op("act", lambda e, rc=rc: e.activation(out=rc[0:64, :], in_=rc[0:64, :], func=AF.Exp, scale=-1.0),
                              reads=[b_rec[fs]], writes=[b_rec[fs]])
                        def fin(f=f, rc=rc, pb0=pb0, fs=fs):
                            for par in range(2):
                                dst = oT[64 * par:64 * par + 64, f * 512:(f + 1) * 512]
                                pg.op("dve", lambda e, dst=dst, rc=rc, pb0=pb0, par=par: e.tensor_tensor(
                                    out=dst, in0=psum[0:64, (pb0 + par) * 512:(pb0 + par + 1) * 512],
                                    in1=rc[0:64, par * 512:(par + 1) * 512], op=ALU.mult),
                                    reads=[PB[pb0 + par], b_rec[fs]], writes=[b_oT])
                        deferred.append([3, fin])

                nu = len(units)
                deferred = []
                for step in range(nu + DEPTH):
                    if step < nu:
                        stage_a(step)
                    for d in deferred:
                        d[0] -= 1
                    while deferred and deferred[0][0] <= 0:
                        deferred.pop(0)[1]()
                    if step >= DEPTH:
                        stage_b(step - DEPTH)
                while deferred:
                    deferred.pop(0)[1]()
                ucnt[0] += nu
                if bi + 1 < len(blocks):
                    load_blk(bi + 1)
                for t in range(4):
                    xr, bxr = xres[t], b_xres[t]
                    if t >= 2:
                        pg.dma("sp", ch_r[t % 2], xr, xs_src(blk, t), reads=[XS[blk]], writes=[bxr])
                    for nh in range(2):
                        pb = nh

                        def fo(e, t=t, nh=nh, pb=pb):
                            for ff in range(8):
                                ins = e.matmul(bank(pb), lhsT=oT[:, ff * 512 + t * 128:ff * 512 + (t + 1) * 128],
                                               rhs=wo[:, ff * D + nh * 512:ff * D + (nh + 1) * 512],
                                               start=(ff == 0), stop=(ff == 7))
                            return ins

                        pg.op("pe", fo, reads=[b_oT, b_wo], writes=[PB[pb]])
                        xsl = xr[:, nh * 512:(nh + 1) * 512]
                        pg.op("dve", lambda e, xsl=xsl, pb=pb: e.tensor_tensor(out=xsl, in0=bank(pb), in1=xsl, op=ALU.add),
                              reads=[PB[pb], bxr], writes=[bxr])
                    r0 = blk * 512 + t * 128
                    pg.dma("pool", ch_st[t % 2], xs[r0:r0 + 128, :], xr, reads=[bxr], writes=[XS[blk]])
            pg.barrier()

        if 1 in phases:
            ffn_phase(0, 1, list(range(16)), xe)
        if 2 in phases:
            proj_ab_phase()
        if 3 in phases:
            attn_ab_phase()
        if 4 in phases:
            xattn_phase(0, list(range(7, 16)))
        if 5 in phases:
            ffn_phase(0, 2, list(range(7, 16)), xs)
        if 6 in phases:
            ffn_phase(1, 1, list(range(7, 16)), xs)
        if 7 in phases:
            proj_c_phase()
        if 8 in phases:
            band_phase()
        if 9 in phases:
            xattn_phase(1, list(range(8, 16)))
        if 10 in phases:
            ffn_phase(1, 2, list(range(8, 16)), xs, final=True)

        pg.final_wait("sp")
        pg.emit()
    return nc


def _rep(v, n=128):
    return np.ascontiguousarray(np.broadcast_to(np.asarray(v, np.float32).reshape(1, -1), (n, v.size)))


def make_in_maps(inputs):
    x = np.asarray(inputs["x"], np.float32)
    mem = np.asarray(inputs["mem"], np.float32)
    g = lambda k: np.asarray(inputs[k], np.float32)
    shared = {}
    for l in range(2):
        for nm in ["ffn1_wg", "ffn1_wu", "ffn1_wd", "xa_wq", "xa_wkv", "xa_wo", "ffn2_wg", "ffn2_wu", "ffn2_wd"]:
            shared["%s%d" % (nm, l)] = np.ascontiguousarray(g(nm)[l])
    shared["ab_w_in"] = np.ascontiguousarray(g("ab_w_in")[0])
    shared["ab_w_out"] = np.ascontiguousarray(g("ab_w_out")[0])
    sgw = g("ab_sg_w")[0]
    shared["sgwT"] = np.ascontiguousarray(sgw.transpose(2, 0, 1).reshape(128, 512))
    shared["c_w_in"] = np.ascontiguousarray(g("c_w_in")[0])
    shared["c_w_out"] = np.ascontiguousarray(g("c_w_out")[0])
    rb = g("c_rel_bias")[0]
    j = np.arange(128)[:, None]
    m = np.arange(640)[None, :]
    idx = np.clip(m - j, -256, 256) + 256
    shared["relb"] = np.ascontiguousarray(rb[:, idx].transpose(1, 0, 2).reshape(128, 16 * 640))

    def cst_for(maskval):
        c = np.zeros((128, CST_COLS), np.float32)

        def put(name, arr):
            o, w = CST[name]
            c[:, o:o + w] = arr

        for l in range(2):
            put("g_ffn1_%d" % l, _rep(g("ffn1_norm")[l]))
            put("g_mix_%d" % l, _rep(g("mix_norm")[l]))
            put("g_xa_%d" % l, _rep(g("xa_norm")[l]))
            put("g_mem_%d" % l, _rep(g("xa_mem_norm")[l]))
            put("g_ffn2_%d" % l, _rep(g("ffn2_norm")[l]))
        put("g_final", _rep(g("final_norm")))
        put("lam", _rep(g("ab_lam")[0].reshape(-1)))
        put("subln", np.repeat(g("ab_subln")[0].reshape(128, 1), 16, axis=1))
        put("sgnorm", _rep(g("ab_sg_norm")[0]))
        put("sgb", _rep(g("ab_sg_b")[0].reshape(-1)))
        put("maskb", np.full((128, 16), maskval, np.float32))
        return c

    cstA = cst_for(NEG)
    cstB = cst_for(0.0)
    in_maps = []
    for b in range(4):
        for h in range(2):
            if h == 0:
                xe = np.concatenate([np.zeros((4096, D), np.float32), x[b, :4096]], axis=0)
            else:
                xe = np.ascontiguousarray(x[b])
            d = dict(shared)
            d["xe"] = xe
            d["mem"] = np.ascontiguousarray(mem[b])
            d["cst"] = cstA if h == 0 else cstB
            in_maps.append(d)
    return in_maps


_NC_CACHE = {}


def kernel(**inputs):
    in_maps = make_in_maps(inputs)
    if "nc" not in _NC_CACHE:
        _NC_CACHE["nc"] = build_program()
    nc = _NC_CACHE["nc"]
    res = run_bass_kernel_spmd(nc, in_maps, core_ids=list(range(8)))
    outp = np.empty((4, 8192, D), np.float32)
    for b in range(4):
        for h in range(2):
            outp[b, h * 4096:(h + 1) * 4096] = res.results[b * 2 + h]["out"]
    return outp
```

```python
import numpy as np
from contextlib import ExitStack
import concourse.bass as bass
import concourse.mybir as mybir
from concourse.bass_utils import run_bass_kernel_spmd

F32 = mybir.dt.float32
BF16 = mybir.dt.bfloat16
AF = mybir.ActivationFunctionType
ALU = mybir.AluOpType
AX = mybir.AxisListType

D = 1024
DFF = 2816
NFF = DFF // 128
S_E = 8192
EPS = 1e-6
SEM_CAP = 50000
NEG = -80.0


class Buf:
    __slots__ = ("name", "w", "r")

    def __init__(self, name=""):
        self.name = name
        self.w = None
        self.r = []


class Chan:
    def __init__(self, sem):
        self.sem = sem
        self.issued = 0


class Op:
    __slots__ = ("eng", "fn", "deps", "needed", "count", "chan")

    def __init__(self, eng, fn, deps, chan=None):
        self.eng = eng
        self.fn = fn
        self.deps = deps
        self.needed = False
        self.count = 0
        self.chan = chan


ENGS = ("pe", "act", "dve", "pool", "sp")


class Prog:
    def __init__(self, nc, es):
        self.nc = nc
        self.es = es
        self.ops = {e: [] for e in ENGS}
        self.extra = {e: [] for e in ENGS}
        self.chans = []
        self.sems = {e: [] for e in ENGS}

    def chan(self, name):
        c = Chan(self.es.enter_context(self.nc.semaphore(name)))
        self.chans.append(c)
        return c

    def _deps(self, eng, reads, writes, strict=False):
        deps = []
        for b in reads:
            if b.w is not None:
                deps.append(b.w)
        for b in writes:
            if b.w is not None:
                deps.append(b.w)
            deps.extend(b.r)
        out = []
        for d in deps:
            if d[0] == "op":
                if d[1].eng == eng and not strict:
                    continue
                d[1].needed = True
                out.append(d)
            else:
                out.append(("dma", d[1], d[1].issued))
        out.extend(self.extra[eng])
        self.extra[eng] = []
        return out

    def op(self, eng, fn, reads=(), writes=(), strict=False):
        o = Op(eng, fn, self._deps(eng, reads, writes, strict))
        self.ops[eng].append(o)
        ref = ("op", o)
        for b in reads:
            b.r.append(ref)
        for b in writes:
            b.w = ref
            b.r = []
        return o

    def dma(self, eng, chan, out, in_, reads=(), writes=()):
        o = Op(eng, lambda e: e.dma_start(out=out, in_=in_), self._deps(eng, reads, writes), chan=chan)
        chan.issued += 16
        self.ops[eng].append(o)
        ref = ("dma", chan)
        for b in reads:
            b.r.append(ref)
        for b in writes:
            b.w = ref
            b.r = []
        return o

    def barrier(self):
        for e in ENGS:
            for e2 in ENGS:
                if e2 != e and self.ops[e2]:
                    last = self.ops[e2][-1]
                    if last.chan is None:
                        last.needed = True
                        self.extra[e].append(("op", last))
                    else:
                        for o in reversed(self.ops[e2]):
                            if o.chan is None:
                                o.needed = True
                                self.extra[e].append(("op", o))
                                break
            for c in self.chans:
                if c.issued:
                    self.extra[e].append(("dma", c, c.issued))

    def final_wait(self, eng="sp"):
        deps = [("dma", c, c.issued) for c in self.chans if c.issued]
        o = Op(eng, None, deps)
        self.ops[eng].append(o)

    def emit(self):
        nc = self.nc
        for e in ENGS:
            c = 0
            for o in self.ops[e]:
                if o.chan is None and o.needed:
                    c += 1
                    o.count = c
            nsem = (c + SEM_CAP - 1) // SEM_CAP
            for i in range(max(nsem, 1)):
                self.sems[e].append(self.es.enter_context(nc.semaphore("s_%s%d" % (e, i))))
        block = self.es.enter_context(nc.Block())

        def replay(e, engobj):
            known_op = {}
            known_dma = {}
            for o in self.ops[e]:
                for d in o.deps:
                    if d[0] == "op":
                        t = d[1]
                        key = ((t.count - 1) // SEM_CAP, (t.count - 1) % SEM_CAP + 1)
                        if known_op.get(t.eng, (-1, 0)) >= key:
                            continue
                        known_op[t.eng] = key
                        engobj.wait_ge(self.sems[t.eng][key[0]], key[1])
                    else:
                        ch, val = d[1], d[2]
                        if known_dma.get(id(ch), 0) >= val:
                            continue
                        known_dma[id(ch)] = val
                        engobj.wait_ge(ch.sem, val)
                if o.fn is None:
                    continue
                ins = o.fn(engobj)
                if o.chan is not None:
                    ins.then_inc(o.chan.sem, 16)
                elif o.needed:
                    ins.then_inc(self.sems[e][(o.count - 1) // SEM_CAP], 1)

        @block.tensor
        def _(x):
            replay("pe", x)

        @block.scalar
        def _(x):
            replay("act", x)

        @block.vector
        def _(x):
            replay("dve", x)

        @block.gpsimd
        def _(x):
            replay("pool", x)

        @block.sync
        def _(x):
            replay("sp", x)


class Alloc:
    def __init__(self, arena, ncol):
        self.arena = arena
        self.n = ncol
        self.off = 0

    def reset(self):
        self.off = 0

    def f32(self, n):
        ap = self.arena[:, self.off:self.off + n]
        self.off += n
        assert self.off <= self.n, ("sbuf overflow", self.off, self.n)
        return ap

    def b16(self, n):
        assert n % 2 == 0
        ap = self.arena[:, self.off:self.off + n // 2].bitcast(BF16)
        self.off += n // 2
        assert self.off <= self.n, ("sbuf overflow", self.off, self.n)
        return ap


ARENA_COLS = 52800

CST = {}
_off = 0
for _n, _w in [("g_ffn1_0", 1024), ("g_mix_0", 1024), ("g_xa_0", 1024), ("g_mem_0", 1024), ("g_ffn2_0", 1024),
               ("g_ffn1_1", 1024), ("g_mix_1", 1024), ("g_xa_1", 1024), ("g_mem_1", 1024), ("g_ffn2_1", 1024),
               ("g_final", 1024), ("lam", 256), ("subln", 16), ("sgnorm", 512), ("sgb", 512), ("maskb", 16)]:
    CST[_n] = (_off, _w)
    _off += _w
CST_COLS = _off


def build_program(phases=tuple(range(1, 11)), debug=False):
    nc = bass.Bass("TRN2", target_bir_lowering=False)
    es = ExitStack()
    with es:
        pg = Prog(nc, es)
        kin = "ExternalInput"
        kscr = "ExternalOutput" if debug else "Internal"

        def din(name, shape, dt=F32):
            return nc.dram_tensor(name, list(shape), dt, kind=kin).ap()

        xe = din("xe", [S_E, D])
        mem = din("mem", [256, D])
        cst = din("cst", [128, CST_COLS])
        W = {}
        for l in range(2):
            for nm, shp in [("ffn1_wg", [D, DFF]), ("ffn1_wu", [D, DFF]), ("ffn1_wd", [DFF, D]),
                            ("xa_wq", [D, D]), ("xa_wkv", [D, 2 * D]), ("xa_wo", [D, D]),
                            ("ffn2_wg", [D, DFF]), ("ffn2_wu", [D, DFF]), ("ffn2_wd", [DFF, D])]:
                W["%s%d" % (nm, l)] = din("%s%d" % (nm, l), shp)
        W["ab_w_in"] = din("ab_w_in", [D, 2560])
        W["ab_w_out"] = din("ab_w_out", [D, D])
        W["sgwT"] = din("sgwT", [128, 512])
        W["c_w_in"] = din("c_w_in", [D, 3072])
        W["c_w_out"] = din("c_w_out", [D, D])
        W["relb"] = din("relb", [128, 16 * 640])

        out = nc.dram_tensor("out", [4096, D], F32, kind="ExternalOutput").ap()
        xs = nc.dram_tensor("xs", [S_E, D], F32, kind=kscr).ap()
        qT0 = nc.dram_tensor("qT0", [4, 128, 4608], BF16, kind=kscr).ap()
        kT0 = nc.dram_tensor("kT0", [4, 128, S_E], BF16, kind=kscr).ap()
        v0 = nc.dram_tensor("v0", [S_E, 512], BF16, kind=kscr).ap()
        obT = nc.dram_tensor("obT", [4, 128, 4608], BF16, kind=kscr).ap()
        qT1 = nc.dram_tensor("qT1", [8, 128, 4608], BF16, kind=kscr).ap()
        kT1 = nc.dram_tensor("kT1", [8, 128, 4608], BF16, kind=kscr).ap()
        v1 = nc.dram_tensor("v1", [4608, 2 * D], BF16, kind=kscr).ap()

        DBG = {}
        if debug:
            DBG["hT"] = nc.dram_tensor("dbg_hT", [128, 4096], BF16, kind="ExternalOutput").ap()
            DBG["hid"] = nc.dram_tensor("dbg_hid", [128, NFF * 512], BF16, kind="ExternalOutput").ap()
            DBG["wg"] = nc.dram_tensor("dbg_wg", [128, 8 * DFF], BF16, kind="ExternalOutput").ap()
            DBG["xsb"] = nc.dram_tensor("dbg_xsb", [128, 1024], BF16, kind="ExternalOutput").ap()
            DBG["rs"] = nc.dram_tensor("dbg_rs", [128, 64], F32, kind="ExternalOutput").ap()
            DBG["cat"] = nc.dram_tensor("dbg_cat", [2, 128, 2048], BF16, kind="ExternalOutput").ap()
            DBG["oa"] = nc.dram_tensor("dbg_oa", [2, 128, 512], F32, kind="ExternalOutput").ap()
            DBG["P"] = nc.dram_tensor("dbg_P", [4, 128, 1024], BF16, kind="ExternalOutput").ap()
            DBG["rec"] = nc.dram_tensor("dbg_rec", [128, 1024], F32, kind="ExternalOutput").ap()
            DBG["t1"] = nc.dram_tensor("dbg_t1", [128, 512], F32, kind="ExternalOutput").ap()
            DBG["nl"] = nc.dram_tensor("dbg_nl", [128, 16], F32, kind="ExternalOutput").ap()
            DBG["ssum"] = nc.dram_tensor("dbg_ssum", [128, 32], F32, kind="ExternalOutput").ap()
            DBG["ee"] = nc.dram_tensor("dbg_ee", [128, 32], F32, kind="ExternalOutput").ap()
            DBG["lamt"] = nc.dram_tensor("dbg_lamt", [128, 256], F32, kind="ExternalOutput").ap()
        arena = es.enter_context(nc.sbuf_tensor("arena", [128, ARENA_COLS], F32))
        psum = es.enter_context(nc.psum_tensor("psum", [128, 4096], F32))
        al = Alloc(arena, ARENA_COLS)
        PB = [Buf("ps%d" % i) for i in range(8)]

        def bank(i, n=512, off=0):
            return psum[:, i * 512 + off:i * 512 + off + n]

        def bank16(i):
            return psum[:, i * 512:(i + 1) * 512].bitcast(BF16)

        XS = [Buf("xs_blk%d" % i) for i in range(16)]

        ch_x = [pg.chan("ch_x%d" % i) for i in range(2)]
        ch_r = [pg.chan("ch_r%d" % i) for i in range(2)]
        ch_st = [pg.chan("ch_st%d" % i) for i in range(2)]
        ch_w = [pg.chan("ch_w%d" % i) for i in range(4)]
        ch_c = pg.chan("ch_c")
        ch_m = [pg.chan("ch_m%d" % i) for i in range(6)]
        ch_dbg = pg.chan("ch_dbg")
        ch_sc = [pg.chan("ch_sc%d" % i) for i in range(2)]
        ch_wb = [pg.chan("ch_wb%d" % i) for i in range(12)]

        class NormT:
            def __init__(self, gname, src_fn, src_buf_fn, width=512, nslots=2):
                self.width = width
                self.nslots = nslots
                self.src_fn = src_fn
                self.src_buf_fn = src_buf_fn
                self.gam = al.f32(1024)
                self.b_gam = Buf("gam")
                o, w = CST[gname]
                pg.dma("sp", ch_c, self.gam, cst[:, o:o + w], writes=[self.b_gam])
                self.xin = [al.f32(1024) for _ in range(nslots)]
                self.b_xin = [Buf("xin%d" % i) for i in range(nslots)]
                self.xsb = [al.b16(1024) for _ in range(4)]
                self.b_xsb = [Buf("xsb%d" % i) for i in range(4)]
                self.junk = al.f32(1024)
                self.b_junk = Buf("junk")
                self.ss = al.f32(64)
                self.rs = al.f32(64)
                self.b_ss = [Buf("ss%d" % i) for i in range(4)]
                self.b_rs = [Buf("rs%d" % i) for i in range(4)]
                self.hT = al.b16(8 * width)
                self.b_hT = Buf("hT")
                self.ident = al.b16(128)
                self.b_ident = Buf("ident")
                ident = self.ident
                pg.op("pool", lambda e: e.memset(ident, 0.0), writes=[self.b_ident])
                pg.op("pool", lambda e: e.affine_select(out=ident, in_=ident, pattern=[[-1, 128]],
                                                          compare_op=ALU.not_equal, fill=1.0, base=0,
                                                          channel_multiplier=1), reads=[self.b_ident], writes=[self.b_ident],
                      strict=True)
                self.cnt = 0

            def norm(self, blk, nsub=4):
                if self.nslots >= 4:
                    self.load(blk, nsub)
                    self.compute(blk, nsub)
                    return
                for t in range(nsub):
                    self.norm1(blk, t)

            def load(self, blk, nsub=4):
                sb = self.src_buf_fn(blk)
                for t in range(nsub):
                    pg.dma("sp", ch_x[t % 2], self.xin[t], self.src_fn(blk, t), reads=[sb] if sb else [], writes=[self.b_xin[t]])

            def compute(self, blk, nsub=4, defer_scale=False):
                ssall = self.ss[:, 0:64]
                rsall = self.rs[:, 0:64]
                junk = self.junk
                pg.op("dve", lambda e: e.memset(ssall, 0.0), writes=self.b_ss)
                for t in range(nsub):
                    xin, bx = self.xin[t], self.b_xin[t]
                    ss = self.ss[:, 16 * t:16 * t + 1]
                    pg.op("act", lambda e, xin=xin, ss=ss: e.activation(out=junk, in_=xin, func=AF.Square, accum_out=ss),
                          reads=[bx], writes=[self.b_junk, self.b_ss[t]])
                pg.op("act", lambda e: e.activation(out=rsall, in_=ssall, func=AF.Sqrt, scale=1.0 / 1024, bias=EPS),
                      reads=self.b_ss, writes=self.b_rs, strict=True)
                pg.op("dve", lambda e: e.reciprocal(out=rsall, in_=rsall), reads=self.b_rs, writes=self.b_rs)
                gam = self.gam
                scales = []
                for t in range(nsub):
                    def sc(t=t):
                        xin, bx = self.xin[t], self.b_xin[t]
                        rs = self.rs[:, 16 * t:16 * t + 1]
                        xsb = self.xsb[t]
                        pg.op("dve", lambda e, xsb=xsb, xin=xin, rs=rs: e.scalar_tensor_tensor(
                            out=xsb, in0=xin, scalar=rs, in1=gam, op0=ALU.mult, op1=ALU.mult),
                            reads=[bx, self.b_rs[t], self.b_gam], writes=[self.b_xsb[t]], strict=True)
                    scales.append(sc)
                if defer_scale:
                    return scales
                for sc in scales:
                    sc()

            def norm1(self, blk, t):
                if True:
                    k = self.cnt % 2
                    self.cnt += 1
                    xin, bx = self.xin[k], self.b_xin[k]
                    sb = self.src_buf_fn(blk)
                    pg.dma("sp", ch_x[k], xin, self.src_fn(blk, t), reads=[sb] if sb else [], writes=[bx])
                    ss = self.ss[:, 16 * t:16 * t + 1]
                    rs = self.rs[:, 16 * t:16 * t + 1]
                    junk = self.junk
                    pg.op("dve", lambda e, ss=ss: e.memset(ss, 0.0), writes=[self.b_ss[t]])
                    pg.op("act", lambda e, xin=xin, ss=ss, junk=junk: e.activation(
                        out=junk, in_=xin, func=AF.Square, accum_out=ss),
                        reads=[bx], writes=[self.b_junk, self.b_ss[t]])
                    pg.op("act", lambda e, ss=ss, rs=rs: e.activation(
                        out=rs, in_=ss, func=AF.Sqrt, scale=1.0 / 1024, bias=EPS),
                        reads=[self.b_ss[t]], writes=[self.b_rs[t]], strict=True)
                    pg.op("dve", lambda e, rs=rs: e.reciprocal(out=rs, in_=rs),
                          reads=[self.b_rs[t]], writes=[self.b_rs[t]])
                    xsb = self.xsb[t]
                    gam = self.gam
                    pg.op("dve", lambda e, xsb=xsb, xin=xin, rs=rs, gam=gam: e.scalar_tensor_tensor(
                        out=xsb, in0=xin, scalar=rs, in1=gam, op0=ALU.mult, op1=ALU.mult),
                        reads=[bx, self.b_rs[t], self.b_gam], writes=[self.b_xsb[t]], strict=True)

            def tr(self, blk, pbanks, nsub=4):
                hT = self.hT
                for t in range(nsub):
                    pb = pbanks[t % 2]
                    p16 = bank16(pb)
                    xsb = self.xsb[t]
                    ident = self.ident

                    def f(e, p16=p16, xsb=xsb, ident=ident):
                        for c in range(8):
                            ins = e.transpose(p16[:, c * 128:(c + 1) * 128], xsb[:, c * 128:(c + 1) * 128], ident)
                        return ins

                    pg.op("pe", f, reads=[self.b_xsb[t], self.b_ident], writes=[PB[pb]])
                    dst = hT.rearrange("p (c n) -> p c n", c=8)[:, :, t * 128:(t + 1) * 128]
                    src = p16.rearrange("p (c n) -> p c n", c=8)
                    if t % 2 == 0:
                        pg.op("act", lambda e, dst=dst, src=src: e.activation(out=dst, in_=src, func=AF.Copy),
                              reads=[PB[pb]], writes=[self.b_hT])
                    else:
                        pg.op("dve", lambda e, dst=dst, src=src: e.tensor_copy(dst, src),
                              reads=[PB[pb]], writes=[self.b_hT])

        def load_weight_cast(dst16, src, kchunks, ncols, bufs_cols, chans):
            dv = dst16.rearrange("p (k n) -> p k n", k=kchunks)
            sv = src.rearrange("(k p) n -> p k n", p=128)
            for i, (b, c0, c1) in enumerate(bufs_cols):
                pg.dma("pool", chans[i % len(chans)], dv[:, :, c0:c1], sv[:, :, c0:c1], writes=[b])

        def ffn_phase(layer, which, blocks, src, dst_is_out=False, final=False):
            al.reset()
            wg = W["ffn%d_wg%d" % (which, layer)]
            wu = W["ffn%d_wu%d" % (which, layer)]
            wd = W["ffn%d_wd%d" % (which, layer)]
            wg16 = al.b16(8 * DFF)
            wu16 = al.b16(8 * DFF)
            wd16 = al.b16(NFF * D)
            bounds = [0, 128, 704, 1408, 2112, DFF]
            NCB = len(bounds) - 1
            b_wg = [Buf("wg%d" % i) for i in range(NCB)]
            b_wu = [Buf("wu%d" % i) for i in range(NCB)]
            b_wd = [Buf("wd%d" % i) for i in range(2)]
            for i in range(NCB):
                load_weight_cast(wg16, wg, 8, DFF, [(b_wg[i], bounds[i], bounds[i + 1])], [ch_wb[i]])
                load_weight_cast(wu16, wu, 8, DFF, [(b_wu[i], bounds[i], bounds[i + 1])], [ch_wb[5 + i]])

            def wblocks(m):
                return [i for i in range(NCB) if bounds[i] < (m + 1) * 128 and bounds[i + 1] > m * 128]
            for i in range(2):
                load_weight_cast(wd16, wd, NFF, D, [(b_wd[i], i * 512, (i + 1) * 512)], [ch_wb[10 + i]])

            def src_fn(blk, t):
                r0 = blk * 512 + t * 128
                return src[r0:r0 + 128, :]

            def src_buf(blk):
                return XS[blk] if src is xs else None

            nt = NormT("g_ffn%d_%d" % (which, layer), src_fn, src_buf)
            hid = al.b16(NFF * 512)
            b_hid = Buf("hid")
            sg = [al.f32(512) for _ in range(2)]
            b_sg = [Buf("sg%d" % i) for i in range(2)]
            xres = [al.f32(1024) for _ in range(2)]
            b_xres = [Buf("xres%d" % i) for i in range(2)]
            if final:
                gfin = al.f32(1024)
                b_gfin = Buf("gfin")
                o, w = CST["g_final"]
                pg.dma("sp", ch_c, gfin, cst[:, o:o + w], writes=[b_gfin])
                fss = al.f32(32)
                frs = al.f32(32)
                b_fss = [Buf("fss%d" % i) for i in range(2)]
                b_frs = [Buf("frs%d" % i) for i in range(2)]
                fjunk = nt.junk
                b_fjunk = nt.b_junk
            rescnt = [0]

            def gate_up(blk, nxt=None):
                hT = nt.hT
                for m in range(NFF):
                    if nxt is not None and m in (2, 7, 12, 17):
                        nt.norm1(nxt, (2, 7, 12, 17).index(m))
                    s = m % 2
                    pg_, pu_ = 2 + s, 4 + s
                    wbs = wblocks(m)

                    def fg(e, m=m, pb=pg_, w16=wg16):
                        for k in range(8):
                            ins = e.matmul(bank(pb), lhsT=w16[:, k * DFF + m * 128:k * DFF + (m + 1) * 128],
                                           rhs=hT[:, k * 512:(k + 1) * 512], start=(k == 0), stop=(k == 7))
                        return ins

                    pg.op("pe", fg, reads=[nt.b_hT] + [b_wg[i] for i in wbs], writes=[PB[pg_]])

                    def fu(e, m=m, pb=pu_, w16=wu16):
                        for k in range(8):
                            ins = e.matmul(bank(pb), lhsT=w16[:, k * DFF + m * 128:k * DFF + (m + 1) * 128],
                                           rhs=hT[:, k * 512:(k + 1) * 512], start=(k == 0), stop=(k == 7))
                        return ins

                    pg.op("pe", fu, reads=[nt.b_hT] + [b_wu[i] for i in wbs], writes=[PB[pu_]])
                    sgt = sg[s]
                    pg.op("act", lambda e, sgt=sgt, pb=pg_: e.activation(out=sgt, in_=bank(pb), func=AF.Silu),
                          reads=[PB[pg_]], writes=[b_sg[s]])
                    hslice = hid[:, m * 512:(m + 1) * 512]
                    pg.op("dve", lambda e, hslice=hslice, sgt=sgt, pb=pu_: e.tensor_tensor(
                        out=hslice, in0=sgt, in1=bank(pb), op=ALU.mult),
                        reads=[b_sg[s], PB[pu_]], writes=[b_hid])

            def down(blk):
                for t in range(4):
                    k = rescnt[0] % 2
                    rescnt[0] += 1
                    xr, bxr = xres[k], b_xres[k]
                    sb = src_buf(blk)
                    pg.dma("sp", ch_r[k], xr, src_fn(blk, t), reads=[sb] if sb else [], writes=[bxr])
                    for nh in range(2):
                        pb = 6 + nh

                        def fd(e, t=t, nh=nh, pb=pb):
                            for kf in range(NFF):
                                ins = e.matmul(bank(pb), lhsT=hid[:, kf * 512 + t * 128:kf * 512 + (t + 1) * 128],
                                               rhs=wd16[:, kf * D + nh * 512:kf * D + (nh + 1) * 512],
                                               start=(kf == 0), stop=(kf == NFF - 1))
                            return ins

                        pg.op("pe", fd, reads=[b_hid, b_wd[nh]], writes=[PB[pb]])
                        xsl = xr[:, nh * 512:(nh + 1) * 512]
                        pg.op("dve", lambda e, xsl=xsl, pb=pb: e.scalar_tensor_tensor(
                            out=xsl, in0=bank(pb), scalar=0.5, in1=xsl, op0=ALU.mult, op1=ALU.add),
                            reads=[PB[pb], bxr], writes=[bxr])
                    r0 = blk * 512 + t * 128
                    if final:
                        ss = fss[:, 16 * k:16 * k + 1]
                        rs = frs[:, 16 * k:16 * k + 1]
                        pg.op("dve", lambda e, ss=ss: e.memset(ss, 0.0), writes=[b_fss[k]])
                        pg.op("act", lambda e, xr=xr, ss=ss: e.activation(out=fjunk, in_=xr, func=AF.Square,
                                                                            accum_out=ss),
                              reads=[bxr], writes=[b_fjunk, b_fss[k]])
                        pg.op("act", lambda e, ss=ss, rs=rs: e.activation(out=rs, in_=ss, func=AF.Sqrt,
                                                                            scale=1.0 / 1024, bias=EPS),
                              reads=[b_fss[k]], writes=[b_frs[k]], strict=True)
                        pg.op("dve", lambda e, rs=rs: e.reciprocal(out=rs, in_=rs), reads=[b_frs[k]],
                              writes=[b_frs[k]])
                        pg.op("dve", lambda e, xr=xr, rs=rs: e.scalar_tensor_tensor(
                            out=xr, in0=xr, scalar=rs, in1=gfin, op0=ALU.mult, op1=ALU.mult),
                            reads=[bxr, b_frs[k], b_gfin], writes=[bxr], strict=True)
                        ro = (blk - 8) * 512 + t * 128
                        pg.dma("pool", ch_st[k], out[ro:ro + 128, :], xr, reads=[bxr])
                    else:
                        pg.dma("pool", ch_st[k], xs[r0:r0 + 128, :], xr, reads=[bxr], writes=[XS[blk]])

            nt.norm(blocks[0])
            nt.tr(blocks[0], (0, 1))
            if debug and which == 1 and layer == 0:
                pg.dma("sp", ch_dbg, DBG["hT"], nt.hT, reads=[nt.b_hT])
                pg.dma("sp", ch_dbg, DBG["xsb"], nt.xsb[0], reads=[nt.b_xsb[0]])
                pg.dma("sp", ch_dbg, DBG["rs"], nt.rs, reads=nt.b_rs)
                pg.dma("sp", ch_dbg, DBG["wg"], wg16, reads=b_wg)
            for i, blk in enumerate(blocks):
                nxt = blocks[i + 1] if i + 1 < len(blocks) else None
                gate_up(blk, nxt)
                if debug and which == 1 and layer == 0 and i == 0:
                    pg.dma("sp", ch_dbg, DBG["hid"], hid, reads=[b_hid])
                if nxt is not None:
                    nt.tr(nxt, (0, 1))
                down(blk)
            pg.barrier()

        def mm_group(pb, n, lhs_fn, rhs_fn, nk, reads, off=0):
            def f(e):
                for k in range(nk):
                    ins = e.matmul(bank(pb, n, off), lhsT=lhs_fn(k), rhs=rhs_fn(k), start=(k == 0), stop=(k == nk - 1))
                return ins
            return pg.op("pe", f, reads=reads, writes=[PB[pb]])

        cp_cnt = [0]

        def evac_copy(dst, pb, n, dbuf, off=0, eng=None):
            if eng is None:
                eng = "act" if cp_cnt[0] % 2 == 0 else "dve"
                cp_cnt[0] += 1
            src = bank(pb, n, off)
            if eng == "act":
                pg.op("act", lambda e: e.activation(out=dst, in_=src, func=AF.Copy), reads=[PB[pb]], writes=[dbuf])
            else:
                pg.op("dve", lambda e: e.tensor_copy(dst, src), reads=[PB[pb]], writes=[dbuf])

        def xs_src(blk, t):
            r0 = blk * 512 + t * 128
            return xs[r0:r0 + 128, :]

        def cst_load(name, buf, ncol=None):
            o, w = CST[name]
            tl = al.f32(w if ncol is None else ncol)
            pg.dma("sp", ch_c, tl[:, 0:w], cst[:, o:o + w], writes=[buf])
            return tl

        def proj_ab_phase():
            al.reset()
            NW = 2560
            w16 = al.b16(8 * NW)
            b_w = [Buf("wab%d" % i) for i in range(5)]
            for ci in (1, 2, 0, 3, 4):
                load_weight_cast(w16, W["ab_w_in"], 8, NW, [(b_w[ci], ci * 512, (ci + 1) * 512)], [ch_w[ci % 4]])
            nt = NormT("g_mix_0", xs_src, lambda blk: XS[blk], nslots=4)
            b_c = Buf("cst2")
            sgn = cst_load("sgnorm", b_c)
            sgb = cst_load("sgb", b_c)
            sgw16 = al.b16(512)
            b_sgw = Buf("sgw")
            pg.dma("pool", ch_w[0], sgw16, W["sgwT"], writes=[b_sgw])
            for g in range(4):
                sl = sgw16[64:128, g * 128:g * 128 + 64]
                pg.op("pool", lambda e, sl=sl: e.memset(sl, 0.0), reads=[b_sgw], writes=[b_sgw])
            kst = al.b16(4 * 512); b_kst = Buf("kst")
            qst = al.b16(4 * 512); b_qst = Buf("qst")
            gu = al.b16(4 * 512); b_gu = Buf("gu")
            vst = al.b16(4 * 512); b_vst = Buf("vst")
            obst = al.b16(4 * 512); b_obst = Buf("obst")
            gv = [al.f32(512) for _ in range(4)]; b_gv = [Buf("gv%d" % i) for i in range(4)]
            vgn = [al.b16(512) for _ in range(4)]; b_vgn = [Buf("vgn%d" % i) for i in range(4)]
            tmpg = [al.f32(512) for _ in range(2)]; b_tmpg = [Buf("tmpg%d" % i) for i in range(2)]
            gss = al.f32(64); grs = al.f32(64)
            b_gss = [Buf("gss%d" % i) for i in range(4)]; b_grs = [Buf("grs%d" % i) for i in range(4)]
            gjunk = nt.junk; b_gjunk = nt.b_junk
            hT = nt.hT
            fm_rot = [0]

            def fm_chunk(colbase, f, wb):
                pb = 2 + (fm_rot[0] % 2)
                fm_rot[0] += 1
                mm_group(pb, 512, lambda k: w16[:, k * NW + colbase + f * 128:k * NW + colbase + (f + 1) * 128],
                         lambda k: hT[:, k * 512:(k + 1) * 512], 8, [nt.b_hT, wb])
                return pb

            tm_rot = [0]

            def tm_tile(colbase, t, wb):
                pb = 4 + (tm_rot[0] % 2)
                tm_rot[0] += 1
                mm_group(pb, 512, lambda k: hT[:, k * 512 + t * 128:k * 512 + (t + 1) * 128],
                         lambda k: w16[:, k * NW + colbase:k * NW + colbase + 512], 8, [nt.b_hT, wb])
                return pb

            blocks = list(range(16))
            nt.norm(blocks[0])
            nt.tr(blocks[0], (0, 1))
            for i, blk in enumerate(blocks):
                nxt = blocks[i + 1] if i + 1 < len(blocks) else None
                full = blk >= 7
                if nxt is not None:
                    nt.load(nxt)
                for f in range(4):
                    pb = fm_chunk(512, f, b_w[1])
                    evac_copy(kst[:, f * 512:(f + 1) * 512], pb, 512, b_kst)
                pg.dma("pool", ch_sc[0], kT0[:, :, blk * 512:(blk + 1) * 512].rearrange("h p n -> p h n"),
                       kst.rearrange("p (h n) -> p h n", h=4), reads=[b_kst])
                for t in range(4):
                    pb = tm_tile(1024, t, b_w[2])
                    evac_copy(vst[:, t * 512:(t + 1) * 512], pb, 512, b_vst)
                pg.dma("pool", ch_sc[1], v0[blk * 512:(blk + 1) * 512, :].rearrange("(t p) c -> p t c", p=128),
                       vst.rearrange("p (t c) -> p t c", t=4), reads=[b_vst])
                if full:
                    lb = blk - 7
                    pg.op("dve", lambda e: e.memset(gss, 0.0), writes=b_gss)
                    for t in range(4):
                        pb = tm_tile(2048, t, b_w[4])
                        gvt = gv[t]
                        pg.op("act", lambda e, gvt=gvt, pb=pb: e.activation(out=gvt, in_=bank(pb), func=AF.Gelu_apprx_tanh),
                              reads=[PB[pb]], writes=[b_gv[t]])
                        ss = gss[:, 16 * t:16 * t + 1]
                        pg.op("act", lambda e, gvt=gvt, ss=ss: e.activation(out=gjunk[:, 0:512], in_=gvt, func=AF.Square,
                                                                             accum_out=ss),
                              reads=[b_gv[t]], writes=[b_gjunk, b_gss[t]])
                if nxt is not None:
                    nt.compute(nxt)
                if full:
                    for f in range(4):
                        pb = fm_chunk(0, f, b_w[0])
                        evac_copy(qst[:, f * 512:(f + 1) * 512], pb, 512, b_qst)
                    pg.dma("pool", ch_sc[0], qT0[:, :, lb * 512:(lb + 1) * 512].rearrange("h p n -> p h n"),
                           qst.rearrange("p (h n) -> p h n", h=4), reads=[b_qst])
                    for f in range(4):
                        pb = fm_chunk(1536, f, b_w[3])
                        dst = gu[:, f * 512:(f + 1) * 512]
                        pg.op("act", lambda e, dst=dst, pb=pb: e.activation(out=dst, in_=bank(pb), func=AF.Gelu_apprx_tanh),
                              reads=[PB[pb]], writes=[b_gu])
                    pg.op("act", lambda e: e.activation(out=grs, in_=gss, func=AF.Sqrt, scale=1.0 / 512, bias=EPS),
                          reads=b_gss, writes=b_grs, strict=True)
                    pg.op("dve", lambda e: e.reciprocal(out=grs, in_=grs), reads=b_grs, writes=b_grs)
                    for t in range(4):
                        gvt = gv[t]
                        rs = grs[:, 16 * t:16 * t + 1]
                        vg = vgn[t]
                        pg.op("dve", lambda e, vg=vg, gvt=gvt, rs=rs: e.scalar_tensor_tensor(
                            out=vg, in0=gvt, scalar=rs, in1=sgn, op0=ALU.mult, op1=ALU.mult),
                            reads=[b_gv[t], b_grs[t], b_c], writes=[b_vgn[t]], strict=True)
                    for t in range(4):
                        s2 = t % 2
                        vg = vgn[t]
                        gb = (6, 7)[s2]

                        def fgate(e, vg=vg, gb=gb):
                            for g in range(4):
                                ins = e.matmul(bank(gb, 128, g * 128), lhsT=vg[:, g * 128:(g + 1) * 128],
                                               rhs=sgw16[:, g * 128:(g + 1) * 128], start=True, stop=True)
                            return ins

                        pg.op("pe", fgate, reads=[b_vgn[t], b_sgw], writes=[PB[gb]])
                        tg = tmpg[s2]
                        pg.op("dve", lambda e, tg=tg, gb=gb: e.tensor_tensor(out=tg, in0=bank(gb), in1=sgb, op=ALU.add),
                              reads=[PB[gb], b_c], writes=[b_tmpg[s2]])
                        o3 = obst.rearrange("p (g n) -> p g n", g=4)[:, :, t * 128:(t + 1) * 128]
                        g3 = gu.rearrange("p (g n) -> p g n", g=4)[:, :, t * 128:(t + 1) * 128]
                        t3 = tg.rearrange("p (g n) -> p g n", g=4)
                        pg.op("dve", lambda e, o3=o3, g3=g3, t3=t3: e.tensor_tensor(out=o3, in0=t3, in1=g3, op=ALU.mult),
                              reads=[b_tmpg[s2], b_gu], writes=[b_obst])
                    pg.dma("pool", ch_sc[1], obT[:, :, lb * 512:(lb + 1) * 512].rearrange("h p n -> p h n"),
                           obst.rearrange("p (h n) -> p h n", h=4), reads=[b_obst])
                if nxt is not None:
                    nt.tr(nxt, (0, 1))
            pg.barrier()

        def attn_ab_phase():
            al.reset()
            kT = al.b16(4 * S_E)
            V = al.b16(64 * 512)
            wo16 = al.b16(8 * D)
            b_kv = [Buf("kv%d" % g) for g in range(4)]
            b_wo = Buf("wo")
            kT3 = kT.rearrange("p (h n) -> p h n", h=4)
            V3 = V.rearrange("p (k c) -> p k c", k=64)
            for g in range(4):
                pg.dma("sp", ch_m[g % 4], kT3[:, :, g * 2048:(g + 1) * 2048],
                       kT0[:, :, g * 2048:(g + 1) * 2048].rearrange("h p n -> p h n"), writes=[b_kv[g]])
                pg.dma("sp", ch_m[g % 4], V3[:, g * 16:(g + 1) * 16, :],
                       v0[g * 2048:(g + 1) * 2048, :].rearrange("(k p) c -> p k c", p=128), writes=[b_kv[g]])
            load_weight_cast(wo16, W["ab_w_out"], 8, D, [(b_wo, 0, D)], [ch_w[0]])
            b_c = Buf("cst3")
            lamt = cst_load("lam", b_c)
            subln = cst_load("subln", b_c, 16)
            maskb = cst_load("maskb", b_c, 16)
            ones16 = al.b16(128); b_ones = Buf("ones")
            pg.op("pool", lambda e: e.memset(ones16, 1.0), writes=[b_ones])
            prod = al.f32(128); ssum = al.f32(32); ee = al.f32(32); nl = al.f32(16); sublnS = al.f32(16)
            b_l = Buf("lamc")
            pg.op("dve", lambda e: e.memset(ssum, 0.0), writes=[b_l])
            pg.op("dve", lambda e: e.scalar_tensor_tensor(out=prod[:, 0:64], in0=lamt[:, 0:64], scalar=1.0, in1=lamt[:, 64:128],
                                                            op0=ALU.mult, op1=ALU.mult, accum_out=ssum[:, 0:1]),
                  reads=[b_c], writes=[b_l], strict=True)
            pg.op("dve", lambda e: e.scalar_tensor_tensor(out=prod[:, 64:128], in0=lamt[:, 128:192], scalar=1.0, in1=lamt[:, 192:256],
                                                            op0=ALU.mult, op1=ALU.mult, accum_out=ssum[:, 16:17]),
                  reads=[b_c], writes=[b_l])
            b_l2 = Buf("lamc2")
            pg.op("act", lambda e: e.activation(out=ee[:, 0:1], in_=ssum[:, 0:1], func=AF.Exp), reads=[b_l], writes=[b_l2])
            pg.op("act", lambda e: e.activation(out=ee[:, 16:17], in_=ssum[:, 16:17], func=AF.Exp), reads=[b_l], writes=[b_l2])
            b_l3 = Buf("lamc3")
            pg.op("dve", lambda e: e.tensor_tensor(out=nl[:, 0:1], in0=ee[:, 16:17], in1=ee[:, 0:1], op=ALU.subtract),
                  reads=[b_l2], writes=[b_l3])
            pg.op("dve", lambda e: e.tensor_scalar(out=nl[:, 0:1], in0=nl[:, 0:1], scalar1=-0.2, scalar2=None, op0=ALU.add),
                  reads=[b_l3], writes=[b_l3], strict=True)
            pg.op("dve", lambda e: e.tensor_scalar(out=sublnS[:, 0:1], in0=subln[:, 0:1], scalar1=0.8, scalar2=None, op0=ALU.mult),
                  reads=[b_c], writes=[b_l3])
            qT = [al.b16(4 * 512) for _ in range(2)]; b_qT = [Buf("qT%d" % i) for i in range(2)]
            obt1 = al.b16(4 * 512); b_obt1 = Buf("obt")
            obt = [obt1, obt1]; b_obt = [b_obt1, b_obt1]
            NP = 5
            Pt = [al.b16(1024) for _ in range(NP)]; b_P = [Buf("P%d" % i) for i in range(NP)]
            cat = al.b16(4 * 512); b_cat = Buf("cat")
            rec = al.f32(1024); b_rec = Buf("rec")
            t1 = al.f32(512); t2 = al.f32(512); rstd = t1
            oa4 = [al.f32(512) for _ in range(4)]
            b_t = Buf("t12"); b_oa = [Buf("oa%d" % i) for i in range(4)]; b_rstd = b_t
            sq4 = [al.b16(512) for _ in range(4)]; b_sq = [Buf("sq%d" % i) for i in range(4)]
            lnS = rec; b_lnS = b_rec
            xres2 = [al.f32(1024) for _ in range(2)]; b_xres2 = [Buf("xres%d" % i) for i in range(2)]
            xres = xres2 + xres2; b_xres = b_xres2 + b_xres2
            acc = al.f32(1024); b_acc = Buf("acc")
            ones32 = al.f32(128); b_ones32 = Buf("ones32")
            pg.op("pool", lambda e: e.memset(ones32, 1.0), writes=[b_ones32])
            ucnt = [0]
            rcnt = [0]

            def sslot(k):
                return (4 + 2 * (k % 2), 5 + 2 * (k % 2))

            blocks = list(range(7, 16))
            DEPTH = 2

            def make_wout(blk):
                groups = []
                for t in range(4):
                    for nh in range(2):
                        def g(t=t, nh=nh, blk=blk):
                            kx = t % 2
                            xr, bxr = xres2[kx], b_xres2[kx]
                            if nh == 0:
                                pg.dma("sp", ch_r[kx], xr, xs_src(blk, t), reads=[XS[blk]], writes=[bxr])
                            pb = 2 + nh

                            def fo(e, t=t, nh=nh, pb=pb):
                                for c in range(8):
                                    src = cat if c < 4 else obt1
                                    cc = c % 4
                                    ins = e.matmul(bank(pb), lhsT=src[:, cc * 512 + t * 128:cc * 512 + (t + 1) * 128],
                                                   rhs=wo16[:, c * D + nh * 512:c * D + (nh + 1) * 512],
                                                   start=(c == 0), stop=(c == 7))
                                return ins

                            pg.op("pe", fo, reads=[b_cat, b_obt1, b_wo], writes=[PB[pb]])
                            xsl = xr[:, nh * 512:(nh + 1) * 512]
                            pg.op("dve", lambda e, xsl=xsl, pb=pb: e.tensor_tensor(out=xsl, in0=bank(pb), in1=xsl, op=ALU.add),
                                  reads=[PB[pb], bxr], writes=[bxr])
                            if nh == 1:
                                r0 = blk * 512 + t * 128
                                pg.dma("pool", ch_st[kx], xs[r0:r0 + 128, :], xr, reads=[bxr], writes=[XS[blk]])
                        groups.append(g)
                return groups

            pending = None
            for bi, blk in enumerate(blocks):
                lb = blk - 7
                qs = bi % 2
                q = qT[qs]
                pg.dma("sp", ch_x[qs], q.rearrange("p (h n) -> p h n", h=4),
                       qT0[:, :, lb * 512:(lb + 1) * 512].rearrange("h p n -> p h n"), writes=[b_qT[qs]])
                nkt = 4 * (blk + 1)
                units = [(h, kt) for h in range(4) for kt in range(nkt)]
                base = ucnt[0]
                wgroups = make_wout(pending) if pending is not None else []

                def stage_a(u):
                    h, kt = units[u]
                    k = base + u
                    b0, b1 = sslot(k)
                    kvb = b_kv[kt // 16]

                    def f(e, h=h, kt=kt, b0=b0, b1=b1, q=q):
                        e.matmul(bank(b0), lhsT=kT[0:64, h * S_E + kt * 128:h * S_E + (kt + 1) * 128],
                                 rhs=q[0:64, h * 512:(h + 1) * 512], start=True, stop=True)
                        return e.matmul(bank(b1), lhsT=kT[64:128, h * S_E + kt * 128:h * S_E + (kt + 1) * 128],
                                        rhs=q[64:128, h * 512:(h + 1) * 512], start=True, stop=True)

                    pg.op("pe", f, reads=[kvb, b_qT[qs]], writes=[PB[b0], PB[b1]])
                    ps2 = psum[:, b0 * 512:b0 * 512 + 1024]
                    P = Pt[k % NP]; bP = b_P[k % NP]
                    j = kt - 4 * blk
                    bias = maskb[:, 0:1] if kt < 32 else 0.0
                    if j <= 0:
                        pg.op("act", lambda e, P=P, ps2=ps2, bias=bias: e.activation(out=P, in_=ps2, func=AF.Exp,
                                                                                     scale=0.125, bias=bias),
                              reads=[PB[b0], PB[b1], b_c], writes=[bP])
                    else:
                        P3 = P.rearrange("p (m n) -> p m n", m=2)
                        s3 = ps2.rearrange("p (m n) -> p m n", m=2)
                        pg.op("act", lambda e, P3=P3, s3=s3, j=j, bias=bias: e.activation(
                            out=P3[:, :, 128 * j:512], in_=s3[:, :, 128 * j:512], func=AF.Exp, scale=0.125, bias=bias),
                            reads=[PB[b0], PB[b1], b_c], writes=[bP])
                        pg.op("pool", lambda e, P3=P3, j=j: e.memset(P3[:, :, 0:128 * j], 0.0), writes=[bP])
                    if j >= 0:
                        P3 = P.rearrange("p (m n) -> p m n", m=2)
                        pg.op("pool", lambda e, P3=P3, j=j: e.memset(P3[64:128, :, 128 * j:128 * j + 64], 0.0),
                              reads=[bP], writes=[bP])
                    if debug and bi == 0 and h == 3 and kt in (0, 1, 28, 31):
                        pg.dma("sp", ch_dbg, DBG["P"][(0, 1, 28, 31).index(kt)], P, reads=[bP])

                def stage_b_pv(u):
                    h, kt = units[u]
                    k = base + u
                    P = Pt[k % NP]; bP = b_P[k % NP]
                    kvb = b_kv[kt // 16]
                    first = (kt == 0)
                    last = (kt == nkt - 1)

                    def fpv(e, h=h, kt=kt, P=P, first=first, last=last):
                        vv = V[:, kt * 512 + h * 128:kt * 512 + (h + 1) * 128]
                        e.matmul(bank(0), lhsT=vv, rhs=P[:, 0:512], start=first, stop=last)
                        return e.matmul(bank(1), lhsT=vv, rhs=P[:, 512:1024], start=first, stop=last)

                    pg.op("pe", fpv, reads=[bP, kvb], writes=[PB[0], PB[1]])

                def stage_b_add(u):
                    h, kt = units[u]
                    k = base + u
                    P = Pt[k % NP]; bP = b_P[k % NP]
                    first = (kt == 0)
                    last = (kt == nkt - 1)
                    if first:
                        pg.op("dve", lambda e, P=P: e.tensor_copy(acc, P), reads=[bP], writes=[b_acc])
                    else:
                        pg.op("dve", lambda e, P=P: e.tensor_tensor(out=acc, in0=acc, in1=P, op=ALU.add),
                              reads=[bP, b_acc], writes=[b_acc])
                    if last:
                        def fsum(e):
                            e.matmul(bank(2), lhsT=ones32, rhs=acc[:, 0:512], start=True, stop=True)
                            return e.matmul(bank(3), lhsT=ones32, rhs=acc[:, 512:1024], start=True, stop=True)

                        pg.op("pe", fsum, reads=[b_acc, b_ones32], writes=[PB[2], PB[3]])
                        oa = oa4[h]
                        sq16 = sq4[h]
                        pg.op("act", lambda e: e.activation(out=lnS, in_=psum[:, 2 * 512:4 * 512], func=AF.Ln),
                              reads=[PB[2], PB[3]], writes=[b_lnS])
                        pg.op("act", lambda e: e.activation(out=rec, in_=lnS, func=AF.Exp, scale=-1.0),
                              reads=[b_lnS], writes=[b_rec])

                        def fin(h=h, oa=oa, sq16=sq16):
                            pg.op("dve", lambda e: e.tensor_tensor(out=t1, in0=bank(0), in1=rec[:, 0:512], op=ALU.mult),
                                  reads=[PB[0], b_rec], writes=[b_t])
                            pg.op("dve", lambda e: e.tensor_tensor(out=t2, in0=bank(1), in1=rec[:, 512:1024], op=ALU.mult),
                                  reads=[PB[1], b_rec], writes=[b_t])
                            pg.op("dve", lambda e, oa=oa: e.scalar_tensor_tensor(out=oa, in0=t2, scalar=nl[:, 0:1], in1=t1,
                                                                                  op0=ALU.mult, op1=ALU.add),
                                  reads=[b_t, b_l3], writes=[b_oa[h]])
                            pg.op("dve", lambda e, oa=oa, sq16=sq16: e.tensor_tensor(out=sq16, in0=oa, in1=oa, op=ALU.mult),
                                  reads=[b_oa[h]], writes=[b_sq[h]])
                        deferred.append([2, fin])

                nu = len(units)
                deferred = []
                pvq = []
                for step in range(nu + DEPTH):
                    if step < nu:
                        stage_a(step)
                    for d in deferred:
                        d[0] -= 1
                    while deferred and deferred[0][0] <= 0:
                        deferred.pop(0)[1]()
                    if step >= DEPTH:
                        pvq.append(step - DEPTH)
                        if not deferred:
                            while pvq:
                                stage_b_pv(pvq.pop(0))
                        stage_b_add(step - DEPTH)
                    if wgroups and step >= 6 and (step - 6) % 6 == 0:
                        wgroups.pop(0)()
                while deferred:
                    deferred.pop(0)[1]()
                while pvq:
                    stage_b_pv(pvq.pop(0))
                while wgroups:
                    wgroups.pop(0)()
                ucnt[0] += nu
                pg.dma("sp", ch_x[qs], obt1.rearrange("p (h n) -> p h n", h=4),
                       obT[:, :, lb * 512:(lb + 1) * 512].rearrange("h p n -> p h n"), writes=[b_obt1])
                for h in range(4):
                    ucnt[0] += 1
                    mb = sslot(ucnt[0])[0]
                    oa = oa4[h]
                    sq16 = sq4[h]
                    pg.op("pe", lambda e, mb=mb, sq16=sq16: e.matmul(bank(mb), lhsT=ones16, rhs=sq16, start=True, stop=True),
                          reads=[b_sq[h], b_ones], writes=[PB[mb]])
                    pg.op("act", lambda e, mb=mb: e.activation(out=rstd, in_=bank(mb), func=AF.Ln, scale=1.0 / 128, bias=EPS),
                          reads=[PB[mb]], writes=[b_rstd])
                    pg.op("act", lambda e: e.activation(out=rstd, in_=rstd, func=AF.Exp, scale=-0.5),
                          reads=[b_rstd], writes=[b_rstd])
                    ch = cat[:, h * 512:(h + 1) * 512]
                    pg.op("dve", lambda e, ch=ch, oa=oa: e.scalar_tensor_tensor(out=ch, in0=oa, scalar=sublnS[:, 0:1], in1=rstd,
                                                                                 op0=ALU.mult, op1=ALU.mult),
                          reads=[b_oa[h], b_rstd, b_l3], writes=[b_cat])
                pending = blk
            for g in make_wout(pending):
                g()
            pg.barrier()

        def xattn_phase(layer, blocks):
            al.reset()
            wq16 = al.b16(8 * D); wkv16 = al.b16(8 * 2 * D); wo16 = al.b16(8 * D)
            b_wq = Buf("wq"); b_wkv = [Buf("wkvk"), Buf("wkvv")]; b_wo = Buf("wo")
            load_weight_cast(wkv16, W["xa_wkv%d" % layer], 8, 2 * D, [(b_wkv[0], 0, D)], [ch_w[0]])
            load_weight_cast(wq16, W["xa_wq%d" % layer], 8, D, [(b_wq, 0, D)], [ch_w[2]])
            load_weight_cast(wkv16, W["xa_wkv%d" % layer], 8, 2 * D, [(b_wkv[1], D, 2 * D)], [ch_w[1]])
            load_weight_cast(wo16, W["xa_wo%d" % layer], 8, D, [(b_wo, 0, D)], [ch_w[3]])
            ntm = NormT("g_mem_%d" % layer, lambda blk, t: mem[t * 128:(t + 1) * 128, :], lambda blk: None, width=256)
            nt = NormT("g_xa_%d" % layer, xs_src, lambda blk: XS[blk], nslots=4)
            ones16 = al.b16(128); b_ones = Buf("ones")
            pg.op("pool", lambda e: e.memset(ones16, 1.0), writes=[b_ones])
            kxT = al.b16(8 * 256); b_kx = Buf("kxT")
            Vx = al.b16(2 * D); b_vx = Buf("Vx")
            qT = al.b16(8 * 512); b_qT = Buf("qTx")
            oT = al.b16(8 * 512); b_oT = Buf("oTx")
            Pt = [al.b16(1024) for _ in range(2)]; b_P = [Buf("Px%d" % i) for i in range(2)]
            rec = al.f32(512); b_rec = Buf("recx")
            xres = [al.f32(1024) for _ in range(2)]; b_xres = [Buf("xres%d" % i) for i in range(2)]
            ntm.norm(0, nsub=2)
            ntm.tr(0, (0, 1), nsub=2)
            memT = ntm.hT
            prot = [0]

            def pbank():
                pb = (7, 6)[prot[0] % 2]
                prot[0] += 1
                return pb

            for f in range(8):
                pb = pbank()
                mm_group(pb, 256, lambda k, f=f: wkv16[:, k * 2 * D + f * 128:k * 2 * D + (f + 1) * 128],
                         lambda k: memT[:, k * 256:(k + 1) * 256], 8, [ntm.b_hT, b_wkv[0]])
                evac_copy(kxT[:, f * 256:(f + 1) * 256], pb, 256, b_kx)
            for mt in range(2):
                for nh in range(2):
                    pb = pbank()
                    mm_group(pb, 512, lambda k, mt=mt: memT[:, k * 256 + mt * 128:k * 256 + (mt + 1) * 128],
                             lambda k, nh=nh: wkv16[:, k * 2 * D + D + nh * 512:k * 2 * D + D + (nh + 1) * 512], 8,
                             [ntm.b_hT, b_wkv[1]])
                    evac_copy(Vx[:, mt * D + nh * 512:mt * D + (nh + 1) * 512], pb, 512, b_vx)
            hT = nt.hT
            scnt = [0]
            lnS = al.f32(512); b_lnS = Buf("lnS")
            xres4 = xres + [al.f32(1024) for _ in range(2)]
            b_xres4 = b_xres + [Buf("xres%d" % i) for i in range(2, 4)]
            nt.norm(blocks[0])
            nt.tr(blocks[0], (0, 1))
            for i, blk in enumerate(blocks):
                nxt = blocks[i + 1] if i + 1 < len(blocks) else None
                for t in range(4):
                    pg.dma("sp", ch_r[t % 2], xres4[t], xs_src(blk, t), reads=[XS[blk]], writes=[b_xres4[t]])
                if nxt is not None:
                    nt.load(nxt)
                for f in range(8):
                    pb = pbank()
                    mm_group(pb, 512, lambda k, f=f: wq16[:, k * D + f * 128:k * D + (f + 1) * 128],
                             lambda k: hT[:, k * 512:(k + 1) * 512], 8, [nt.b_hT, b_wq])
                    evac_copy(qT[:, f * 512:(f + 1) * 512], pb, 512, b_qT)

                def emit_qk(h):
                    sl = (scnt[0] + h) % 2
                    sb0 = (2, 0)[sl]

                    def fs(e, h=h, sb0=sb0):
                        for mt in range(2):
                            for c in range(2):
                                ins = e.matmul(bank(sb0 + mt), lhsT=kxT[:, (2 * h + c) * 256 + mt * 128:(2 * h + c) * 256 + (mt + 1) * 128],
                                               rhs=qT[:, (2 * h + c) * 512:(2 * h + c + 1) * 512], start=(c == 0), stop=(c == 1))
                        return ins

                    pg.op("pe", fs, reads=[b_kx, b_qT], writes=[PB[sb0], PB[sb0 + 1]])
                    ps2 = psum[:, sb0 * 512:sb0 * 512 + 1024]
                    P = Pt[sl]; bP = b_P[sl]
                    pg.op("act", lambda e, P=P, ps2=ps2: e.activation(out=P, in_=ps2, func=AF.Exp, scale=1.0 / 16),
                          reads=[PB[sb0], PB[sb0 + 1]], writes=[bP])

                emit_qk(0)
                for h in range(4):
                    sl = (scnt[0] + h) % 2
                    if h + 1 < 4:
                        emit_qk(h + 1)
                    P = Pt[sl]; bP = b_P[sl]

                    def fo(e, h=h, P=P):
                        for ei in range(2):
                            for mt in range(2):
                                e.matmul(bank(4 + ei), lhsT=Vx[:, mt * D + h * 256 + ei * 128:mt * D + h * 256 + (ei + 1) * 128],
                                         rhs=P[:, mt * 512:(mt + 1) * 512], start=(mt == 0), stop=(mt == 1))
                        for mt in range(2):
                            ins = e.matmul(bank(6), lhsT=ones16, rhs=P[:, mt * 512:(mt + 1) * 512], start=(mt == 0), stop=(mt == 1))
                        return ins

                    pg.op("pe", fo, reads=[bP, b_vx, b_ones], writes=[PB[4], PB[5], PB[6]])
                    pg.op("act", lambda e: e.activation(out=lnS, in_=bank(6), func=AF.Ln), reads=[PB[6]], writes=[b_lnS])
                    pg.op("act", lambda e: e.activation(out=rec, in_=lnS, func=AF.Exp, scale=-1.0), reads=[b_lnS], writes=[b_rec])
                    for ei in range(2):
                        dst = oT[:, (2 * h + ei) * 512:(2 * h + ei + 1) * 512]
                        pg.op("dve", lambda e, dst=dst, ei=ei: e.tensor_tensor(out=dst, in0=bank(4 + ei), in1=rec, op=ALU.mult),
                              reads=[PB[4 + ei], b_rec], writes=[b_oT])
                scnt[0] += 4
                scales = nt.compute(nxt, defer_scale=True) if nxt is not None else []
                for t in range(4):
                    if t == 3 and nxt is not None:
                        nt.tr(nxt, (0, 1))
                    xr, bxr = xres4[t], b_xres4[t]
                    for nh in range(2):
                        pb = pbank()
                        mm_group(pb, 512, lambda c, t=t: oT[:, c * 512 + t * 128:c * 512 + (t + 1) * 128],
                                 lambda c, nh=nh: wo16[:, c * D + nh * 512:c * D + (nh + 1) * 512], 8, [b_oT, b_wo])
                        xsl = xr[:, nh * 512:(nh + 1) * 512]
                        pg.op("dve", lambda e, xsl=xsl, pb=pb: e.tensor_tensor(out=xsl, in0=bank(pb), in1=xsl, op=ALU.add),
                              reads=[PB[pb], bxr], writes=[bxr])
                    r0 = blk * 512 + t * 128
                    pg.dma("pool", ch_st[t % 2], xs[r0:r0 + 128, :], xr, reads=[bxr], writes=[XS[blk]])
                    if scales and t < 2:
                        scales[2 * t]()
                        scales[2 * t + 1]()
            pg.barrier()

        def proj_c_phase():
            al.reset()
            NW = 3072
            w16 = al.b16(8 * NW)
            b_w = [Buf("wc%d" % i) for i in range(3)]
            for ci in (1, 2, 0):
                load_weight_cast(w16, W["c_w_in"], 8, NW, [(b_w[ci], ci * D, ci * D + 512)], [ch_w[ci]])
                load_weight_cast(w16, W["c_w_in"], 8, NW, [(b_w[ci], ci * D + 512, (ci + 1) * D)], [ch_w[ci]])
            nt = NormT("g_mix_1", xs_src, lambda blk: XS[blk], nslots=4)
            hT = nt.hT
            kst = al.b16(8 * 512); b_kst = Buf("kst")
            qst = al.b16(8 * 512); b_qst = Buf("qst")
            vst = al.b16(4 * 2 * D); b_vst = Buf("vst")
            pg.op("pool", lambda e: e.memset(vst, 1.0), writes=[b_vst])
            rot = [0]

            def pbk():
                pb = 2 + rot[0] % 4
                rot[0] += 1
                return pb

            blocks = list(range(7, 16))
            nt.norm(blocks[0])
            nt.tr(blocks[0], (0, 1))
            for i, blk in enumerate(blocks):
                lb = blk - 7
                nxt = blocks[i + 1] if i + 1 < len(blocks) else None
                if nxt is not None:
                    nt.load(nxt)
                for f in range(8):
                    if nxt is not None and f == 4:
                        nt.compute(nxt)
                    pb = pbk()
                    mm_group(pb, 512, lambda k, f=f: w16[:, k * NW + D + f * 128:k * NW + D + (f + 1) * 128],
                             lambda k: hT[:, k * 512:(k + 1) * 512], 8, [nt.b_hT, b_w[1]])
                    evac_copy(kst[:, f * 512:(f + 1) * 512], pb, 512, b_kst)
                pg.dma("pool", ch_sc[0], kT1[:, :, lb * 512:(lb + 1) * 512].rearrange("h p n -> p h n"),
                       kst.rearrange("p (h n) -> p h n", h=8), reads=[b_kst])
                for t in range(4):
                    for nh in range(2):
                        pb = pbk()
                        mm_group(pb, 512, lambda k, t=t: hT[:, k * 512 + t * 128:k * 512 + (t + 1) * 128],
                                 lambda k, nh=nh: w16[:, k * NW + 2 * D + nh * 512:k * NW + 2 * D + (nh + 1) * 512], 8,
                                 [nt.b_hT, b_w[2]])
                        vdst = vst[:, t * 2 * D + nh * D:t * 2 * D + (nh + 1) * D].rearrange("p (h c) -> p h c", h=8)[:, :, 0:64]
                        vsrc = bank(pb).rearrange("p (h c) -> p h c", h=8)
                        if (t + nh) % 2 == 0:
                            pg.op("act", lambda e, vdst=vdst, vsrc=vsrc: e.activation(out=vdst, in_=vsrc, func=AF.Copy),
                                  reads=[PB[pb]], writes=[b_vst])
                        else:
                            pg.op("dve", lambda e, vdst=vdst, vsrc=vsrc: e.tensor_copy(vdst, vsrc), reads=[PB[pb]], writes=[b_vst])
                pg.dma("pool", ch_sc[1], v1[lb * 512:(lb + 1) * 512, :].rearrange("(t p) c -> p t c", p=128),
                       vst.rearrange("p (t c) -> p t c", t=4), reads=[b_vst])
                if blk >= 8:
                    for f in range(8):
                        pb = pbk()
                        mm_group(pb, 512, lambda k, f=f: w16[:, k * NW + f * 128:k * NW + (f + 1) * 128],
                                 lambda k: hT[:, k * 512:(k + 1) * 512], 8, [nt.b_hT, b_w[0]])
                        evac_copy(qst[:, f * 512:(f + 1) * 512], pb, 512, b_qst)
                    pg.dma("pool", ch_sc[0], qT1[:, :, lb * 512:(lb + 1) * 512].rearrange("h p n -> p h n"),
                           qst.rearrange("p (h n) -> p h n", h=8), reads=[b_qst])
                if nxt is not None:
                    nt.tr(nxt, (0, 1))
            pg.barrier()

        def band_phase():
            al.reset()
            wo = al.b16(8 * D)
            b_wo = Buf("woc")
            load_weight_cast(wo, W["c_w_out"], 8, D, [(b_wo, 0, D)], [ch_w[0]])
            relb = al.f32(16 * 640); b_relb = Buf("relb")
            for g in range(4):
                pg.dma("sp", ch_c, relb[:, g * 2560:(g + 1) * 2560], W["relb"][:, g * 2560:(g + 1) * 2560], writes=[b_relb])
            relb3 = relb.rearrange("p (h m) -> p h m", h=16)
            pg.op("pool", lambda e: e.memset(relb3[0:64, :, 576:640], -30000.0), reads=[b_relb], writes=[b_relb])
            pg.op("pool", lambda e: e.memset(relb3[64:128, :, 0:64], -30000.0), reads=[b_relb], writes=[b_relb])
            b_c = Buf("cst8")
            maskb = cst_load("maskb", b_c, 16)
            qTb = al.b16(8 * 512); b_q = Buf("qb")
            kTb = al.b16(8 * 1024); b_k = Buf("kb")
            Vb = al.b16(8 * 2 * D); b_v = Buf("vb")
            oT = al.b16(8 * 512); b_oT = Buf("oTc")
            NS = 6
            DEPTH = 4
            sbt = [al.f32(1024) for _ in range(NS)]; b_sb = [Buf("sb%d" % i) for i in range(NS)]
            Pt = [al.b16(1024) for _ in range(NS)]; b_P = [Buf("Pc%d" % i) for i in range(NS)]
            rec = [al.f32(1024) for _ in range(2)]; b_rec = [Buf("recc%d" % i) for i in range(2)]
            xres2 = [al.f32(1024) for _ in range(2)]; b_xres2 = [Buf("xres%d" % i) for i in range(2)]
            xres = xres2 + xres2; b_xres = b_xres2 + b_xres2
            ucnt = [0]
            KT_ORDER = (3, 4, 2, 5, 1, 6, 0, 7)
            blocks = list(range(8, 16))

            def load_blk(bi):
                blk = blocks[bi]
                li = blk - 8
                pg.dma("sp", ch_m[0], qTb.rearrange("p (f n) -> p f n", f=8),
                       qT1[:, :, (li + 1) * 512:(li + 2) * 512].rearrange("f p n -> p f n"), writes=[b_q])
                pg.dma("sp", ch_m[1], kTb.rearrange("p (f n) -> p f n", f=8),
                       kT1[:, :, li * 512:li * 512 + 1024].rearrange("f p n -> p f n"), writes=[b_k])
                pg.dma("sp", ch_m[2], Vb.rearrange("p (k c) -> p k c", k=8),
                       v1[li * 512:li * 512 + 1024, :].rearrange("(k p) c -> p k c", p=128), writes=[b_v])

            def geom(kt):
                cq_lo = max(0, 2 * kt - 8)
                cq_hi = min(7, 2 * kt + 1)
                n0 = 64 * cq_lo
                n1 = 64 * (cq_hi + 1)
                return n0, n1, n1 - n0, n0 + 512 - 128 * kt

            load_blk(0)
            for bi, blk in enumerate(blocks):
                for t in range(2):
                    pg.dma("sp", ch_r[t % 2], xres[t], xs_src(blk, t), reads=[XS[blk]], writes=[b_xres[t]])
                units = [(f, oi, kt) for f in range(8) for oi, kt in enumerate(KT_ORDER)]
                base = ucnt[0]

                def stage_a(u):
                    f, oi, kt = units[u]
                    g = base + u
                    n0, n1, N, m0 = geom(kt)
                    sb0 = (0, 2)[g % 2]
                    sl = g % NS

                    def fqk(e, kt=kt, n0=n0, n1=n1, N=N, sb0=sb0, f=f):
                        e.matmul(bank(sb0, N), lhsT=kTb[0:64, f * 1024 + kt * 128:f * 1024 + (kt + 1) * 128],
                                 rhs=qTb[0:64, f * 512 + n0:f * 512 + n1], start=True, stop=True)
                        return e.matmul(bank(sb0 + 1, N), lhsT=kTb[64:128, f * 1024 + kt * 128:f * 1024 + (kt + 1) * 128],
                                        rhs=qTb[64:128, f * 512 + n0:f * 512 + n1], start=True, stop=True)

                    pg.op("pe", fqk, reads=[b_k, b_q], writes=[PB[sb0], PB[sb0 + 1]])
                    sb3 = sbt[sl].rearrange("p (m n) -> p m n", m=2)[:, :, 0:N]
                    ps3 = psum[:, sb0 * 512:sb0 * 512 + 1024].rearrange("p (m n) -> p m n", m=2)[:, :, 0:N]
                    r3 = relb[:, 2 * f * 640:(2 * f + 2) * 640].rearrange("p (m n) -> p m n", m=2)[:, :, m0:m0 + N]
                    pg.op("dve", lambda e, sb3=sb3, ps3=ps3, r3=r3: e.scalar_tensor_tensor(
                        out=sb3, in0=ps3, scalar=0.125, in1=r3, op0=ALU.mult, op1=ALU.add),
                        reads=[PB[sb0], PB[sb0 + 1], b_relb], writes=[b_sb[sl]])
                    P3 = Pt[sl].rearrange("p (m n) -> p m n", m=2)[:, :, 0:N]
                    bias = maskb[:, 0:1] if (blk == 8 and kt < 4) else 0.0
                    pg.op("act", lambda e, P3=P3, sb3=sb3, bias=bias: e.activation(out=P3, in_=sb3, func=AF.Exp, bias=bias),
                          reads=[b_sb[sl], b_c], writes=[b_P[sl]])

                def stage_b(u):
                    f, oi, kt = units[u]
                    g = base + u
                    n0, n1, N, m0 = geom(kt)
                    sl = g % NS
                    P = Pt[sl]
                    fs = f % 2
                    pb0 = 4 + 2 * fs
                    first = (oi == 0)
                    last = (oi == 7)

                    def fpv(e, kt=kt, f=f, P=P, N=N, n0=n0, first=first, last=last, pb0=pb0):
                        for par in range(2):
                            h = 2 * f + par
                            ins = e.matmul(bank(pb0 + par, N, n0), lhsT=Vb[:, kt * 2 * D + h * 128:kt * 2 * D + (h + 1) * 128],
                                           rhs=P[:, par * 512:par * 512 + N], start=first, stop=last)
                        return ins

                    pg.op("pe", fpv, reads=[b_P[sl], b_v], writes=[PB[pb0], PB[pb0 + 1]])
                    if last:
                        rc = rec[fs]
                        pg.op("act", lambda e, rc=rc, pb0=pb0: e.activation(out=rc[0:64, :], in_=psum[64:128, pb0 * 512:pb0 * 512 + 1024],
                                                                              func=AF.Ln),
                              reads=[PB[pb0], PB[pb0 + 1]], writes=[b_rec[fs]])
                        pg.op("act", lambda e, rc=rc: e.activation(out=rc[0:64, :], in_=rc[0:64, :], func=AF.Exp, scale=-1.0),
                              reads=[b_rec[fs]], writes=[b_rec[fs]])
                        def fin(f=f, rc=rc, pb0=pb0, fs=fs):
                            for par in range(2):
                                dst = oT[64 * par:64 * par + 64, f * 512:(f + 1) * 512]
                                pg.op("dve", lambda e, dst=dst, rc=rc, pb0=pb0, par=par: e.tensor_tensor(
                                    out=dst, in0=psum[0:64, (pb0 + par) * 512:(pb0 + par + 1) * 512],
                                    in1=rc[0:64, par * 512:(par + 1) * 512], op=ALU.mult),
                                    reads=[PB[pb0 + par], b_rec[fs]], writes=[b_oT])
                        deferred.append([3, fin])

                nu = len(units)
                deferred = []
                for step in range(nu + DEPTH):
                    if step < nu:
                        stage_a(step)
                    for d in deferred:
                        d[0] -= 1
                    while deferred and deferred[0][0] <= 0:
                        deferred.pop(0)[1]()
                    if step >= DEPTH:
                        stage_b(step - DEPTH)
                while deferred:
                    deferred.pop(0)[1]()
                ucnt[0] += nu
                if bi + 1 < len(blocks):
                    load_blk(bi + 1)
                for t in range(4):
                    xr, bxr = xres[t], b_xres[t]
                    if t >= 2:
                        pg.dma("sp", ch_r[t % 2], xr, xs_src(blk, t), reads=[XS[blk]], writes=[bxr])
                    for nh in range(2):
                        pb = nh

                        def fo(e, t=t, nh=nh, pb=pb):
                            for ff in range(8):
                                ins = e.matmul(bank(pb), lhsT=oT[:, ff * 512 + t * 128:ff * 512 + (t + 1) * 128],
                                               rhs=wo[:, ff * D + nh * 512:ff * D + (nh + 1) * 512],
                                               start=(ff == 0), stop=(ff == 7))
                            return ins

                        pg.op("pe", fo, reads=[b_oT, b_wo], writes=[PB[pb]])
                        xsl = xr[:, nh * 512:(nh + 1) * 512]
                        pg.op("dve", lambda e, xsl=xsl, pb=pb: e.tensor_tensor(out=xsl, in0=bank(pb), in1=xsl, op=ALU.add),
                              reads=[PB[pb], bxr], writes=[bxr])
                    r0 = blk * 512 + t * 128
                    pg.dma("pool", ch_st[t % 2], xs[r0:r0 + 128, :], xr, reads=[bxr], writes=[XS[blk]])
            pg.barrier()

        if 1 in phases:
            ffn_phase(0, 1, list(range(16)), xe)
        if 2 in phases:
            proj_ab_phase()
        if 3 in phases:
            attn_ab_phase()
        if 4 in phases:
            xattn_phase(0, list(range(7, 16)))
        if 5 in phases:
            ffn_phase(0, 2, list(range(7, 16)), xs)
        if 6 in phases:
            ffn_phase(1, 1, list(range(7, 16)), xs)
        if 7 in phases:
            proj_c_phase()
        if 8 in phases:
            band_phase()
        if 9 in phases:
            xattn_phase(1, list(range(8, 16)))
        if 10 in phases:
            ffn_phase(1, 2, list(range(8, 16)), xs, final=True)

        pg.final_wait("sp")
        pg.emit()
    return nc


def _rep(v, n=128):
    return np.ascontiguousarray(np.broadcast_to(np.asarray(v, np.float32).reshape(1, -1), (n, v.size)))


def make_in_maps(inputs):
    x = np.asarray(inputs["x"], np.float32)
    mem = np.asarray(inputs["mem"], np.float32)
    g = lambda k: np.asarray(inputs[k], np.float32)
    shared = {}
    for l in range(2):
        for nm in ["ffn1_wg", "ffn1_wu", "ffn1_wd", "xa_wq", "xa_wkv", "xa_wo", "ffn2_wg", "ffn2_wu", "ffn2_wd"]:
            shared["%s%d" % (nm, l)] = np.ascontiguousarray(g(nm)[l])
    shared["ab_w_in"] = np.ascontiguousarray(g("ab_w_in")[0])
    shared["ab_w_out"] = np.ascontiguousarray(g("ab_w_out")[0])
    sgw = g("ab_sg_w")[0]
    shared["sgwT"] = np.ascontiguousarray(sgw.transpose(2, 0, 1).reshape(128, 512))
    shared["c_w_in"] = np.ascontiguousarray(g("c_w_in")[0])
    shared["c_w_out"] = np.ascontiguousarray(g("c_w_out")[0])
    rb = g("c_rel_bias")[0]
    j = np.arange(128)[:, None]
    m = np.arange(640)[None, :]
    idx = np.clip(m - j, -256, 256) + 256
    shared["relb"] = np.ascontiguousarray(rb[:, idx].transpose(1, 0, 2).reshape(128, 16 * 640))

    def cst_for(maskval):
        c = np.zeros((128, CST_COLS), np.float32)

        def put(name, arr):
            o, w = CST[name]
            c[:, o:o + w] = arr

        for l in range(2):
            put("g_ffn1_%d" % l, _rep(g("ffn1_norm")[l]))
            put("g_mix_%d" % l, _rep(g("mix_norm")[l]))
            put("g_xa_%d" % l, _rep(g("xa_norm")[l]))
            put("g_mem_%d" % l, _rep(g("xa_mem_norm")[l]))
            put("g_ffn2_%d" % l, _rep(g("ffn2_norm")[l]))
        put("g_final", _rep(g("final_norm")))
        put("lam", _rep(g("ab_lam")[0].reshape(-1)))
        put("subln", np.repeat(g("ab_subln")[0].reshape(128, 1), 16, axis=1))
        put("sgnorm", _rep(g("ab_sg_norm")[0]))
        put("sgb", _rep(g("ab_sg_b")[0].reshape(-1)))
        put("maskb", np.full((128, 16), maskval, np.float32))
        return c

    cstA = cst_for(NEG)
    cstB = cst_for(0.0)
    in_maps = []
    for b in range(4):
        for h in range(2):
            if h == 0:
                xe = np.concatenate([np.zeros((4096, D), np.float32), x[b, :4096]], axis=0)
            else:
                xe = np.ascontiguousarray(x[b])
            d = dict(shared)
            d["xe"] = xe
            d["mem"] = np.ascontiguousarray(mem[b])
            d["cst"] = cstA if h == 0 else cstB
            in_maps.append(d)
    return in_maps


_NC_CACHE = {}


def kernel(**inputs):
    in_maps = make_in_maps(inputs)
    if "nc" not in _NC_CACHE:
        _NC_CACHE["nc"] = build_program()
    nc = _NC_CACHE["nc"]
    res = run_bass_kernel_spmd(nc, in_maps, core_ids=list(range(8)))
    outp = np.empty((4, 8192, D), np.float32)
    for b in range(4):
        for h in range(2):
            outp[b, h * 4096:(h + 1) * 4096] = res.results[b * 2 + h]["out"]
    return outp
```
